# Optimizing a Trainium2 kernel written in Bass

```python
import jax, jax.numpy as jnp
from jax import lax
import numpy as np

D_MODEL = 1024
BATCH = 4
SEQ = 4096
DEPTH = 1

GRID_W = 64
CTX_LEN = 256
HEAD_DIM = 64
N_HEADS = 16
N_KV_HEADS = 4
GROUP = N_HEADS // N_KV_HEADS
Q_W = N_HEADS * HEAD_DIM
KV_W = N_KV_HEADS * HEAD_DIM
WINDOW = 128
BLOCK = 128
ROPE_THETA = 10000.0
POOL_WINDOWS = (2, 4, 8, 16)
POOL_GROUPS = 4
POOL_W = D_MODEL // 2
POOL_GROUP_W = POOL_W // POOL_GROUPS
N_BRANCHES = 2
REST_START = Q_W + 2 * KV_W
IN_W = REST_START + POOL_W + N_BRANCHES * D_MODEL
D_FF = ((8 * D_MODEL // 3 + 255) // 256) * 256
N_MOD = 6
EPS = 1e-6

kernel_name = 'hybrid_pool_swa_flow_block'


def rms_norm(x, g):
    xf = x.astype(jnp.float32)
    y = xf * lax.rsqrt(jnp.mean(xf * xf, axis=-1, keepdims=True) + EPS)
    return (y * g.astype(jnp.float32)).astype(x.dtype)


def modulate(x, shift, scale):
    return x * (1 + scale) + shift


def heads(t, n):
    return t.reshape(*t.shape[:-1], n, HEAD_DIM)


def axial_rope_tables(seq):
    rows = seq // GRID_W
    row, col = jnp.meshgrid(jnp.arange(rows), jnp.arange(GRID_W), indexing='ij')
    row = row.reshape(-1).astype(jnp.float32)
    col = col.reshape(-1).astype(jnp.float32)
    half = HEAD_DIM // 2
    inv_freq = 1.0 / (ROPE_THETA ** (jnp.arange(0, half, 2, dtype=jnp.float32) / half))
    ang = jnp.stack([row[:, None] * inv_freq, col[:, None] * inv_freq], axis=1)
    return jnp.cos(ang), jnp.sin(ang)


def apply_axial_rope(x, cos, sin):
    B, S, H, _ = x.shape
    xa = x.astype(jnp.float32).reshape(B, S, H, 2, HEAD_DIM // 2)
    x1, x2 = jnp.split(xa, 2, axis=-1)
    c = cos[None, :, None]
    s = sin[None, :, None]
    out = jnp.concatenate([x1 * c - x2 * s, x2 * c + x1 * s], axis=-1)
    return out.reshape(B, S, H, HEAD_DIM).astype(x.dtype)


def band_mask(nb):
    i = jnp.arange(BLOCK)[:, None]
    j = jnp.arange(3 * BLOCK)[None, :]
    rel = j - BLOCK - i
    kpos = (jnp.arange(nb)[:, None, None] - 1) * BLOCK + j[None]
    return (jnp.abs(rel)[None] <= WINDOW) & (kpos >= 0) & (kpos < nb * BLOCK)


def window_attention(q, k, v, kc, vc, sink):
    B, S = q.shape[:2]
    nb = S // BLOCK
    scale = HEAD_DIM ** -0.5
    qb = q.reshape(B, nb, BLOCK, N_KV_HEADS, GROUP, HEAD_DIM)

    def neighbours(t):
        tp = jnp.pad(t, ((0, 0), (BLOCK, BLOCK), (0, 0), (0, 0)))
        tp = tp.reshape(B, nb + 2, BLOCK, N_KV_HEADS, HEAD_DIM)
        return jnp.concatenate([tp[:, :-2], tp[:, 1:-1], tp[:, 2:]], axis=2)

    kw, vw = neighbours(k), neighbours(v)
    s_win = jnp.einsum('bnqkgd,bnjkd->bkgnqj', qb, kw, preferred_element_type=jnp.float32) * scale
    s_win = jnp.where(band_mask(nb)[None, None, None], s_win, -jnp.inf)
    s_ctx = jnp.einsum('bnqkgd,bckd->bkgnqc', qb, kc, preferred_element_type=jnp.float32) * scale
    sink_l = sink.astype(jnp.float32).reshape(1, N_KV_HEADS, GROUP, 1, 1, 1)
    m = jnp.maximum(jnp.maximum(s_win.max(-1, keepdims=True), s_ctx.max(-1, keepdims=True)), sink_l)
    p_win = jnp.exp(s_win - m)
    p_ctx = jnp.exp(s_ctx - m)
    den = p_win.sum(-1, keepdims=True) + p_ctx.sum(-1, keepdims=True) + jnp.exp(sink_l - m)
    o = (jnp.einsum('bkgnqj,bnjkd->bkgnqd', p_win, vw.astype(jnp.float32))
         + jnp.einsum('bkgnqc,bckd->bkgnqd', p_ctx, vc.astype(jnp.float32))) / den
    o = o.transpose(0, 3, 4, 1, 2, 5).reshape(B, S, Q_W)
    return o.astype(q.dtype)


def context_attention(qc, kc, vc, sink):
    B, C = qc.shape[:2]
    qg = qc.reshape(B, C, N_KV_HEADS, GROUP, HEAD_DIM)
    s = jnp.einsum('bqkgd,bckd->bkgqc', qg, kc, preferred_element_type=jnp.float32) * HEAD_DIM ** -0.5
    sink_col = jnp.broadcast_to(sink.astype(jnp.float32).reshape(1, N_KV_HEADS, GROUP, 1, 1), s.shape[:-1] + (1,))
    p = jax.nn.softmax(jnp.concatenate([s, sink_col], axis=-1), axis=-1)[..., :C]
    o = jnp.einsum('bkgqc,bckd->bqkgd', p, vc.astype(jnp.float32))
    return o.reshape(B, C, Q_W).astype(qc.dtype)


def multiscale_pool(u, pool_w, pool_scale):
    B, S, _ = u.shape
    ug = u.astype(jnp.float32).reshape(B, S, POOL_GROUPS, POOL_GROUP_W)
    cs = jnp.pad(jnp.cumsum(ug, axis=1), ((0, 0), (1, 0), (0, 0), (0, 0)))
    t = jnp.arange(S)
    pooled = []
    for g, w in enumerate(POOL_WINDOWS):
        lo = jnp.clip(t - w // 2, 0, S)
        hi = jnp.clip(t + w // 2, 0, S)
        cs_g = cs[:, :, g, :]
        win_sum = cs_g[:, hi] - cs_g[:, lo]
        pooled.append(win_sum / (hi - lo).astype(jnp.float32)[None, :, None])
    diff = jnp.stack(pooled, axis=2) - ug
    mixed = jnp.einsum('bsgc,gcd->bsgd', diff, pool_w.astype(jnp.float32))
    return (mixed.reshape(B, S, POOL_W) * pool_scale).astype(u.dtype)


def merge_branches(attn, rest, gate_b, pool_w, pool_scale, w_attn_proj, w_pool_proj, w_out):
    pool_in, gate_logits = rest[..., :POOL_W], rest[..., POOL_W:]
    a = attn @ w_attn_proj
    p = multiscale_pool(pool_in, pool_w, pool_scale) @ w_pool_proj
    ga, gp = jnp.split(jax.nn.sigmoid(gate_logits + gate_b), N_BRANCHES, axis=-1)
    return (ga * a + gp * p) @ w_out


def swiglu_sublayer(x, shift, scale, gate, g, w_up, w_down):
    h = modulate(rms_norm(x, g), shift, scale)
    a, b = jnp.split(h @ w_up, 2, axis=-1)
    return x + gate * ((jax.nn.silu(a) * b) @ w_down)


def setup_inputs(seed: int = 0) -> dict:
    key = jax.random.key(seed)
    ks = jax.random.split(key, 20)

    def nrm(k, shape, s):
        return jax.random.normal(k, shape, jnp.float32) * s

    D, L = D_MODEL, DEPTH
    return {
        'x': nrm(ks[0], (BATCH, SEQ, D), 1.0),
        'c': nrm(ks[1], (BATCH, D), 1.0),
        'ctx': nrm(ks[2], (BATCH, CTX_LEN, D), 1.0),
        'c_ctx': nrm(ks[3], (D,), 1.0),
        'mod_w': nrm(ks[4], (L, D, N_MOD * D), 0.5 * D ** -0.5),
        'mod_b': nrm(ks[5], (L, N_MOD * D), 0.01),
        'norm1_g': 1.0 + nrm(ks[6], (L, D), 0.05),
        'norm2_g': 1.0 + nrm(ks[7], (L, D), 0.05),
        'w_in': nrm(ks[8], (L, D, IN_W), D ** -0.5),
        'gate_b': nrm(ks[9], (L, N_BRANCHES * D), 0.02),
        'q_norm_g': 1.0 + nrm(ks[10], (L, HEAD_DIM), 0.05),
        'k_norm_g': 1.0 + nrm(ks[11], (L, HEAD_DIM), 0.05),
        'sink': nrm(ks[12], (L, N_HEADS), 0.5),
        'pool_w': nrm(ks[13], (L, POOL_GROUPS, POOL_GROUP_W, POOL_GROUP_W), POOL_GROUP_W ** -0.5),
        'pool_scale': 1.0 + nrm(ks[14], (L, POOL_W), 0.1),
        'w_attn_proj': nrm(ks[15], (L, Q_W, D), Q_W ** -0.5),
        'w_pool_proj': nrm(ks[16], (L, POOL_W, D), POOL_W ** -0.5),
        'w_out': nrm(ks[17], (L, D, D), D ** -0.5),
        'w_up': nrm(ks[18], (L, D, 2 * D_FF), D ** -0.5),
        'w_down': nrm(ks[19], (L, D_FF, D), D_FF ** -0.5),
    }


def reference(x, c, ctx, c_ctx, mod_w, mod_b, norm1_g, norm2_g, w_in, gate_b, q_norm_g, k_norm_g,
              sink, pool_w, pool_scale, w_attn_proj, w_pool_proj, w_out, w_up, w_down):
    cos, sin = axial_rope_tables(x.shape[1])
    for l in range(DEPTH):
        mod_x = jax.nn.silu(c) @ mod_w[l] + mod_b[l]
        mod_c = jax.nn.silu(c_ctx) @ mod_w[l] + mod_b[l]
        sh1, sc1, g1, sh2, sc2, g2 = jnp.split(mod_x[:, None, :], N_MOD, axis=-1)
        csh1, csc1, cg1, csh2, csc2, cg2 = jnp.split(mod_c, N_MOD)

        hc = modulate(rms_norm(ctx, norm1_g[l]), csh1, csc1)
        kv_c = hc @ w_in[l][:, Q_W:REST_START]
        kc = rms_norm(heads(kv_c[..., :KV_W], N_KV_HEADS), k_norm_g[l])
        vc = heads(kv_c[..., KV_W:], N_KV_HEADS)

        h = modulate(rms_norm(x, norm1_g[l]), sh1, sc1)
        proj = h @ w_in[l]
        q = apply_axial_rope(rms_norm(heads(proj[..., :Q_W], N_HEADS), q_norm_g[l]), cos, sin)
        k = apply_axial_rope(rms_norm(heads(proj[..., Q_W:Q_W + KV_W], N_KV_HEADS), k_norm_g[l]), cos, sin)
        v = heads(proj[..., Q_W + KV_W:REST_START], N_KV_HEADS)
        attn = window_attention(q, k, v, kc, vc, sink[l])
        mix = merge_branches(attn, proj[..., REST_START:], gate_b[l], pool_w[l], pool_scale[l],
                             w_attn_proj[l], w_pool_proj[l], w_out[l])
        x_next = x + g1 * mix
        x_next = swiglu_sublayer(x_next, sh2, sc2, g2, norm2_g[l], w_up[l], w_down[l])

        if l + 1 < DEPTH:
            q_c = rms_norm(heads(hc @ w_in[l][:, :Q_W], N_HEADS), q_norm_g[l])
            attn_c = context_attention(q_c, kc, vc, sink[l])
            mix_c = merge_branches(attn_c, hc @ w_in[l][:, REST_START:], gate_b[l], pool_w[l], pool_scale[l],
                                   w_attn_proj[l], w_pool_proj[l], w_out[l])
            ctx = ctx + cg1 * mix_c
            ctx = swiglu_sublayer(ctx, csh2, csc2, cg2, norm2_g[l], w_up[l], w_down[l])
        x = x_next
    return x
```

```python
import numpy as np
import concourse.bass as bass
import concourse.mybir as mybir
from concourse.bass_utils import run_bass_kernel_spmd

F32 = mybir.dt.float32
BF16 = mybir.dt.bfloat16
ALU = mybir.AluOpType
AF = mybir.ActivationFunctionType

D = 1024
S = 4096
CTX = 256
TOWN = 2048
HALO = 128
TEXT = TOWN + 2 * HALO
NBLK = TEXT // 128
NH = 16
QW = 1024
KVW = 256
DFF = 2816
EPS = 1e-6
SB_BASE = 16512
SB_END = 229312
FPASS = [(0, 8), (8, 7), (15, 7)]


class Trk:
    ENG = ("pe", "act", "dve", "pool", "sp")

    def __init__(self, nc):
        self.nc = nc
        self.streams = {e: [] for e in self.ENG}
        self.cnt = {e: 0 for e in self.ENG}
        self.seen = {e: {} for e in self.ENG}
        self.state = {}
        self.dma_cnt = {}
        self.sem_handles = {}

    def sem(self, name):
        if name not in self.sem_handles:
            self.sem_handles[name] = self.nc.alloc_semaphore(name="s_" + name)
        return self.sem_handles[name]

    def _st(self, k):
        st = self.state.get(k)
        if st is None:
            st = {"w": None, "r": {}}
            self.state[k] = st
        return st

    def op(self, eng, fn, reads=(), writes=(), dma=None, inc=True):
        deps = {}

        def add(s, v):
            if s not in deps or deps[s] < v:
                deps[s] = v

        for k in reads:
            st = self._st(k)
            if st["w"] is not None:
                add(*st["w"])
            if k[0] == "ps":
                for s, v in st["r"].items():
                    add(s, v)
        for k in writes:
            st = self._st(k)
            if st["w"] is not None:
                add(*st["w"])
            for s, v in st["r"].items():
                add(s, v)
        waits = []
        for s, v in deps.items():
            if eng == "pe" and s == "pe":
                continue
            if self.seen[eng].get(s, 0) >= v:
                continue
            self.seen[eng][s] = v
            waits.append((s, v))
        if dma is not None:
            self.dma_cnt[dma] = self.dma_cnt.get(dma, 0) + 16
            ev = (dma, self.dma_cnt[dma])
            incr = (dma, 16)
        elif inc:
            self.cnt[eng] += 1
            ev = (eng, self.cnt[eng])
            incr = (eng, 1)
        else:
            ev = (eng, self.cnt[eng] + 1)
            incr = None
        self.streams[eng].append((waits, fn, incr, dma is not None))
        for k in reads:
            st = self._st(k)
            if st["r"].get(ev[0], 0) < ev[1]:
                st["r"][ev[0]] = ev[1]
        for k in writes:
            self.state[k] = {"w": ev, "r": {}}
        return ev

    def barrier(self, skip=()):
        evs = [(e, self.cnt[e]) for e in self.ENG if self.cnt[e] > 0]
        evs += list(self.dma_cnt.items())
        for e in self.ENG:
            if e in skip:
                continue
            waits = []
            for s, v in evs:
                if e == "pe" and s == "pe":
                    continue
                if self.seen[e].get(s, 0) >= v:
                    continue
                self.seen[e][s] = v
                waits.append((s, v))
            if waits:
                self.streams[e].append((waits, None, None, False))

    def final_wait(self, eng, sems):
        waits = [(s, self.dma_cnt[s]) for s in sems]
        self.streams[eng].append((waits, None, None, False))

    def replay(self, eng, e):
        for waits, fn, incr, isdma in self.streams[eng]:
            if fn is None:
                for s, v in waits:
                    e.wait_ge(self.sem(s), v)
                continue
            attach = None
            if waits and not isdma:
                attach = waits[0]
                rest = waits[1:]
            else:
                rest = waits
            for s, v in rest:
                e.wait_ge(self.sem(s), v)
            ins = fn(e)
            if attach is not None:
                ins._wait_ge(self.sem(attach[0]), attach[1])
            if incr is not None:
                ins.then_inc(self.sem(incr[0]), incr[1])


def build_program(stop_after=None, dumps=()):
    nc = bass.Bass("TRN2", target_bir_lowering=False)
    T = Trk(nc)
    cnt = [0]

    def at(off, shape, dtype, name):
        cnt[0] += 1
        esz = 4 if dtype == F32 else 2
        size = esz * int(np.prod(shape[1:]))
        assert off % 32 == 0 and off >= SB_BASE and off + size <= SB_END, (name, off, size)
        return nc.alloc_sbuf_tensor_at("%s_%d" % (name, cnt[0]), list(shape), dtype, offset=off)

    def din(name, shape):
        return nc.dram_tensor(name, list(shape), F32, kind="ExternalInput").ap()

    xext = din("xext", [TEXT, D])
    ctxd = din("ctx", [CTX, D])
    cTd = din("cT", [128, 16])
    modw = din("mod_w", [D, 6 * D])
    modbr = din("modb_rep", [128, 6 * D])
    n1g = din("n1g_rep", [128, D])
    n2g = din("n2g_rep", [128, D])
    w_in = din("w_in_p", [D, 4096])
    w_attn = din("w_attn_p", [QW, D])
    w_pool = din("w_pool", [512, D])
    w_out = din("w_out", [D, D])
    w_up = din("w_up", [D, 2 * DFF])
    w_down = din("w_down", [DFF, D])
    gatebc = din("gateb_col", [128, 16])
    gcols = din("g_cols", [128, 4])
    sinkrow = din("sink_row", [1, NH])
    sinkld = din("sinkl", [1, 256])
    sinkrepd = din("sink_rep", [128, NH])
    poolwd = din("pool_w", [4, 128, 128])
    pscol = din("pscale_col", [128, 4])
    ropecs = din("rope_cs", [128, 2, TEXT])
    masksd = din("masks", [128, 4, 128])
    pedge = din("pool_edge", [128, 66])
    cmat = din("cmat", [128, 3, 128])
    outd = nc.dram_tensor("out", [TOWN, D], F32, kind="ExternalOutput").ap()
    dump_aps = {}
    for nm, shp in dumps:
        dump_aps[nm] = nc.dram_tensor("dbg_" + nm, list(shp), F32, kind="ExternalOutput").ap()

    OC = SB_BASE
    OH = OC + 4608
    OX = OH + 36864
    OM = OX + 65536
    OP = OM + 32768
    OW = OP + 16384
    OMOD = SB_END - 16384
    assert OMOD - OW >= 34000, OMOD - OW

    c = OC
    ident = at(c, [128, 128], BF16, "ident"); c += 256
    bones = at(c, [128, 128], BF16, "bones"); c += 256
    permm = at(c, [128, 128], BF16, "permm"); c += 256
    masks = at(c, [128, 4, 128], BF16, "masks"); c += 1024
    gateb = at(c, [128, 16], F32, "gateb"); c += 64
    gcol = at(c, [128, 4], F32, "gcol"); c += 32
    pscl = at(c, [128, 4], F32, "pscl"); c += 32
    pedg = at(c, [128, 66], F32, "pedg"); c += 288
    cTs = at(c, [128, 16], F32, "cTs"); c += 64
    cTsil = at(c, [128, 16], F32, "cTsil"); c += 64
    stat = at(c, [128, 128], F32, "stat"); c += 512
    sinkl = at(c, [1, 256], BF16, "sinkl"); c += 512
    esrow = at(c, [1, NH], BF16, "esrow"); c += 32
    poolw = at(c, [128, 4, 128], BF16, "poolw"); c += 1024
    esrep = at(c, [128, NH], F32, "esrep"); c += 64
    assert c <= OH

    hT = at(OH, [128, 8, TEXT], BF16, "hT")
    h2T = at(OH, [128, 8, TOWN], BF16, "h2T")
    qT = at(OX, [128, 8, TOWN], BF16, "qT")
    kT = at(OX + 32768, [128, 2, TEXT], BF16, "kT")
    Vaug = at(OX + 41984, [128, NBLK, 2, 2, 128], BF16, "Vaug")
    kcT = at(OX + 60416, [128, 2, CTX], BF16, "kcT")
    Vcaug = at(OX + 61440, [128, 2, 2, 2, 128], BF16, "Vcaug")
    xacc = at(OX, [128, 16, D], F32, "xacc")
    g1rep = at(OMOD, [128, D], F32, "g1rep")
    A2rep = at(OMOD + 4096, [128, D], F32, "A2rep")
    B2rep = at(OMOD + 8192, [128, D], F32, "B2rep")
    g2rep = at(OMOD + 12288, [128, D], F32, "g2rep")

    psall = nc.alloc_psum_tensor("psall", [128, 4096], F32)
    ps = [psall[:, i * 512:(i + 1) * 512] for i in range(8)]

    def PS(i):
        return ("ps", i)

    def dma_in(eng, out_ap, in_ap, key, semname):
        T.op(eng, lambda e: e.dma_start(out=out_ap, in_=in_ap), writes=[key], dma=semname)

    def wview(dram, r0, nk, c0, ncol):
        return dram[r0:r0 + 128 * nk, c0:c0 + ncol].rearrange("(j p) n -> p j n", p=128)

    out_sems = []

    def dump(nm, ap, key):
        s = "dbg_" + nm
        if len(ap.shape) >= 3:
            for i in range(ap.shape[1]):
                T.op("pool", lambda e, i=i: e.dma_start(out=dump_aps[nm][:, i], in_=ap[:, i]), dma=s)
        else:
            T.op("pool", lambda e: e.dma_start(out=dump_aps[nm], in_=ap), dma=s)
        out_sems.append(s)

    def hkeys(t0, n):
        return [("hT", b) for b in range(t0 // 128, (t0 + n - 1) // 128 + 1)]

    def finish():
        T.final_wait("sp", out_sems)
        with nc.Block() as block:
            @block.tensor
            def _(e):
                T.replay("pe", e)

            @block.scalar
            def _(e):
                T.replay("act", e)

            @block.vector
            def _(e):
                T.replay("dve", e)

            @block.gpsimd
            def _(e):
                T.replay("pool", e)

            @block.sync
            def _(e):
                T.replay("sp", e)
        return nc

    cm3 = at(OW, [128, 3, 128], BF16, "cm3")
    sinkf = at(OW + 1024, [1, NH], F32, "sinkf")
    clhs = at(OW + 9216, [128, 16, 128], BF16, "clhs")
    mwr = [at(OW + 13312 + i * 8192, [128, 8, 512], BF16, "mwr%d" % i) for i in range(2)] + \
          [at(OM + i * 8192, [128, 8, 512], BF16, "mwr%d" % (2 + i)) for i in range(2)]
    mbr = [at(OW + 29696 + i * 2048, [128, 512], F32, "mbr%d" % i) for i in range(2)]
    hcT = at(OP, [128, 8, CTX], BF16, "hcT")
    ropet = at(OM, [128, 2, TEXT], F32, "ropet")
    A1rep = at(OM + 18432, [128, D], F32, "A1rep")
    B1rep = at(OM + 22528, [128, D], F32, "B1rep")
    sqscr = at(OM + 26624, [128, D], BF16, "sqscr")
    htok = [at(OX + 28672 + i * 2048, [128, D], BF16, "htok%d" % i) for i in range(4)]
    xblk = [at(OX + i * 4096, [128, D], F32, "xblk%d" % i) for i in range(3)] + [at(OX + 36864, [128, D], F32, "xblk3")]
    modtmp = [at(OX + 12288 + i * 2048, [128, 512], F32, "modtmp%d" % i) for i in range(2)]
    tmpA = [at(OX + 40960 + i * 4096, [128, D], F32, "tmpA%d" % i) for i in range(4)]
    cA1 = at(OX + 20480, [128, D], F32, "cA1")
    cB1 = at(OX + 24576, [128, D], F32, "cB1")

    T.op("dve", lambda e: e.memset(stat[:], 0.0), writes=[("statz",)])
    dma_in("pool", cm3[:], cmat, ("cm3",), "const0")
    dma_in("pool", masks[:], masksd, ("masks",), "const1")
    dma_in("pool", sinkl[:], sinkld, ("sinkl",), "const2")
    dma_in("pool", poolw[:], poolwd.rearrange("g c d -> c g d"), ("poolw",), "const3")
    for (t_, d_, k_) in [(gateb, gatebc, "gateb"), (gcol, gcols, "gcol"), (pscl, pscol, "pscl"), (pedg, pedge, "pedg"),
                         (cTs, cTd, "cTs"), (sinkf, sinkrow, "sinkf")]:
        dma_in("sp", t_[:], d_, (k_,), "c_" + k_)
    for i, (t_, k_) in enumerate([(ident, "ident"), (bones, "bones"), (permm, "permm")]):
        T.op("dve", lambda e, t_=t_, i=i: e.tensor_copy(out=t_[:], in_=cm3[:, i, :]), reads=[("cm3",)], writes=[(k_,)])
    T.op("act", lambda e: e.activation(out=esrow[:], in_=sinkf[:], func=AF.Exp), reads=[("sinkf",)], writes=[("esrow",)])
    dma_in("sp", esrep[:], sinkrepd, ("esrep",), "c_esrep")
    T.op("act", lambda e: e.activation(out=esrep[:], in_=esrep[:], func=AF.Exp), reads=[("esrep",)], writes=[("esrep",)])
    T.op("act", lambda e: e.activation(out=cTsil[:], in_=cTs[:], func=AF.Silu), reads=[("cTs",)], writes=[("cTsil",)])
    T.op("dve", lambda e: e.tensor_copy(out=clhs[:], in_=cTsil[:].unsqueeze(2).to_broadcast([128, 16, 128])),
         reads=[("cTsil",)], writes=[("clhs",)])

    def mod_dst(ch):
        kind = ch // 2
        return [(B1rep, 0), (A1rep, 1), (g1rep, 0), (B2rep, 0), (A2rep, 2), (g2rep, 0)][kind], (ch % 2) * 512

    def mod_chunk(ch, with_ctx, mwr_, mbr_, clhs_, clk, bankbase=6, preloaded=False):
        slot = ch % len(mwr_)
        wk = ("mwr", slot)
        bslot = ch % 2
        bk = ("mbr", bslot)
        if not preloaded:
            dma_in("pool", mwr_[slot][:], wview(modw, 0, 8, ch * 512, 512), wk, "mwr%d" % slot)
        dma_in("sp", mbr_[bslot][:], modbr[:, ch * 512:(ch + 1) * 512], bk, "mbr%d" % bslot)
        (dst, mode), c0 = mod_dst(ch)
        variants = []
        if with_ctx:
            variants.append((8, cB1 if ch < 2 else cA1, 1, 0 if ch < 2 else 3))
        variants.append((0, dst, 0, mode))
        for (cofs, dd, bi, md) in variants:
            bank = bankbase + bi
            for j in range(8):
                T.op("pe", lambda e, j=j, cofs=cofs, bank=bank: e.matmul(
                    ps[bank][:], lhsT=clhs_[:, cofs + j, :], rhs=mwr_[slot][:, j, :], start=(j == 0), stop=(j == 7)),
                    reads=[clk, wk], writes=[PS(bank)], inc=(j == 7))
            dk = ("rep", dd.name, c0)
            if md == 0:
                T.op("dve", lambda e, bank=bank, dd=dd: e.tensor_tensor(
                    out=dd[:, c0:c0 + 512], in0=ps[bank][:], in1=mbr_[bslot][:], op=ALU.add),
                    reads=[PS(bank), bk], writes=[dk])
            else:
                gsrc = dd if md != 3 else A1rep
                gk = dk if md != 3 else ("rep", A1rep.name, c0)
                tmpk = ("modtmp", bi)
                mt = modtmp[bi]
                T.op("dve", lambda e, bank=bank, mt=mt: e.tensor_tensor(
                    out=mt[:], in0=ps[bank][:], in1=mbr_[bslot][:], op=ALU.add),
                    reads=[PS(bank), bk], writes=[tmpk])
                T.op("dve", lambda e, dd=dd, mt=mt, gsrc=gsrc: e.scalar_tensor_tensor(
                    out=dd[:, c0:c0 + 512], in0=mt[:], scalar=1.0, in1=gsrc[:, c0:c0 + 512],
                    op0=ALU.add, op1=ALU.mult), reads=[tmpk, gk], writes=[dk])

    def repkeys(t):
        return [("rep", t.name, 0), ("rep", t.name, 512)]

    for hh_ in range(2):
        dma_in("sp", A1rep[:, hh_ * 512:(hh_ + 1) * 512], n1g[:, hh_ * 512:(hh_ + 1) * 512], ("rep", A1rep.name, hh_ * 512), "c_n1g%d" % hh_)
        dma_in("sp", A2rep[:, hh_ * 512:(hh_ + 1) * 512], n2g[:, hh_ * 512:(hh_ + 1) * 512], ("rep", A2rep.name, hh_ * 512), "c_n2g%d" % hh_)
    for ch in range(4):
        dma_in("pool", mwr[ch][:], wview(modw, 0, 8, ch * 512, 512), ("mwr", ch), "mwr%d" % ch)
    for ch in range(4):
        mod_chunk(ch, True, mwr, mbr, clhs, ("clhs",), preloaded=True)
    if stop_after == "P0":
        T.barrier(skip=("pe",))
        dump("A1", A1rep[:], None)
        return finish()

    wring = [at(OW + i * 8192, [128, 8, 512], BF16, "wring%d" % i) for i in range(2)]
    wv = at(OW + 32768, [128, 8, 256], BF16, "wv")
    T.op("pool", lambda e: e.dma_start(out=wring[0][:, :, 0:256], in_=wview(w_in, 0, 8, QW, 256)),
         writes=[("wring", 0), ("cm3",), ("sinkf",)], dma="wring0")
    T.op("pool", lambda e: e.dma_start(out=wv[:], in_=wview(w_in, 0, 8, QW + KVW, 256)),
         writes=[("wv",), ("mbr", 1)], dma="wv")
    T.op("pool", lambda e: e.dma_start(out=wring[1][:], in_=wview(w_in, 0, 8, 0, 512)),
         writes=[("wring", 1), ("clhs",), ("mwr", 0)], dma="wring1")

    def norm_block(src_ap, xb, xk, load, si, Arep, Brep, dstT, dst_c0, dkey, ht, hk, tpbank, sq):
        if load is not None:
            dma_in("sp", xb, src_ap, xk, load)
        T.op("act", lambda e: e.activation(out=sq[:], in_=xb, func=AF.Square, accum_out=stat[:, si:si + 1]),
             reads=[xk, ("statz",)], writes=[("sqscr",), ("stat", si)])
        T.op("act", lambda e: e.activation(out=stat[:, 64 + si:65 + si], in_=stat[:, si:si + 1], func=AF.Sqrt,
                                           scale=1.0 / D, bias=EPS), reads=[("stat", si)], writes=[("stat2", si)])
        return ht, hk

    def norm_finish(xin_ap, xin_keys, tmp_ap, tmpk, si, Arep, Brep, dstT, dst_c0, dkey, ht, hk, tpbank):
        norm_mod(xin_ap, xin_keys, tmp_ap, tmpk, si, Arep, Brep, ht, hk)
        norm_tp(dstT, dst_c0, dkey, ht, hk, tpbank)

    def norm_mod(xin_ap, xin_keys, tmp_ap, tmpk, si, Arep, Brep, ht, hk, split=False):
        T.op("dve", lambda e: e.reciprocal(out=stat[:, 64 + si:65 + si], in_=stat[:, 64 + si:65 + si]),
             reads=[("stat2", si)], writes=[("stat2", si)])
        T.op("dve", lambda e: e.scalar_tensor_tensor(out=tmp_ap, in0=xin_ap, scalar=stat[:, 64 + si:65 + si],
                                                     in1=Arep[:], op0=ALU.mult, op1=ALU.mult),
             reads=xin_keys + [("stat2", si)] + repkeys(Arep), writes=[tmpk])
        if split:
            T.op("pool", lambda e: e.tensor_tensor(out=ht[:, 0:512], in0=tmp_ap[:, 0:512], in1=Brep[:, 0:512], op=ALU.add),
                 reads=[tmpk] + repkeys(Brep), writes=[(hk, 0)])
            T.op("dve", lambda e: e.tensor_tensor(out=ht[:, 512:1024], in0=tmp_ap[:, 512:1024], in1=Brep[:, 512:1024], op=ALU.add),
                 reads=[tmpk] + repkeys(Brep), writes=[(hk, 1)])
        else:
            T.op("pool", lambda e: e.tensor_tensor(out=ht[:], in0=tmp_ap, in1=Brep[:], op=ALU.add),
                 reads=[tmpk] + repkeys(Brep), writes=[(hk, 0), (hk, 1)])

    def norm_tp(dstT, dst_c0, dkey, ht, hk, tpbank):
        tpv = ps[tpbank][:].bitcast(BF16)
        for j in range(8):
            T.op("pe", lambda e, j=j: e.transpose(tpv[:, j * 128:(j + 1) * 128], ht[:, j * 128:(j + 1) * 128], ident[:]),
                 reads=[(hk, 0), (hk, 1), ("ident",)], writes=[PS(tpbank)], inc=(j == 7))
        T.op("act", lambda e: e.activation(out=dstT[:, :, dst_c0:dst_c0 + 128],
                                           in_=tpv.rearrange("p (j t) -> p j t", j=8), func=AF.Identity),
             reads=[PS(tpbank)], writes=[dkey])

    a0 = []
    for which in range(2 + NBLK):
        if which < 2:
            a0.append((ctxd[which * 128:(which + 1) * 128, :], cA1, cB1, hcT, which * 128, ("hcT", which)))
        else:
            b = which - 2
            a0.append((xext[b * 128:(b + 1) * 128, :], A1rep, B1rep, hT, b * 128, ("hT", b)))

    def a0_s1(bi):
        src, Ar, Br, dstT, c0, dk = a0[bi]
        xs = bi % 4
        norm_block(src, xblk[xs][:], ("xblk", xs), "xblk%d" % xs, bi, Ar, Br, dstT, c0, dk, None, None, None, sqscr)

    def a0_s2(bi):
        src, Ar, Br, dstT, c0, dk = a0[bi]
        xs = bi % 4
        norm_mod(xblk[xs][:], [("xblk", xs)], tmpA[bi % 4][:], ("tmpA", bi % 4), bi, Ar, Br, htok[bi % 4], ("htok", bi % 4), split=True)

    def a0_s3(bi):
        src, Ar, Br, dstT, c0, dk = a0[bi]
        norm_tp(dstT, c0, dk, htok[bi % 4], ("htok", bi % 4), 4 + bi % 2)

    for it in range(len(a0) + 3):
        if it == 8:
            T.op("sp", lambda e: e.dma_start(out=ropet[:], in_=ropecs), writes=[("ropet",), ("mwr", 2), ("mwr", 3)], dma="c_ropet")
        if it < len(a0):
            a0_s1(it)
        if 0 <= it - 1 < len(a0):
            a0_s2(it - 1)
        if 0 <= it - 3 < len(a0):
            a0_s3(it - 3)
    T.barrier(skip=("pe",))
    if "hT" in dump_aps:
        dump("hT", hT[:], None)
    if "hcT" in dump_aps:
        dump("hcT", hcT[:], None)
    if "A1" in dump_aps:
        dump("A1", A1rep[:], None)
    if stop_after == "A0":
        return finish()

    scr = []
    for i in range(2):
        o = OW + 16384 + i * 8192
        scr.append(dict(sq=at(o, [128, 512], BF16, "sq%d" % i), rawb=at(o + 1024, [128, 512], BF16, "rawb%d" % i),
                        sroot=at(o + 2048, [128, 512], F32, "sroot%d" % i), t1=at(o + 4096, [128, 512], F32, "t1%d" % i),
                        t2=at(o + 6144, [128, 512], F32, "t2%d" % i)))
    cosT = ropet[:, 0, :]
    snT = ropet[:, 1, :]
    pctr = [0]
    cctr = [0]

    def qk_tile(wt, wtk, wcol0, srcT, skeyf, chunks, gci, rope, dst_fn, dkey_fn):
        pend = []

        def tail(st):
            pb, n, t0, sc, sck, ab, dst, dkeys = st
            T.op("pe", lambda e: e.matmul(ps[ab][:, :n], lhsT=bones[:], rhs=sc["sq"][:, :n], start=True, stop=True),
                 reads=[("bones",), (sck, "sq")], writes=[PS(ab)])
            if rope:
                T.op("pe", lambda e: e.matmul(ps[ab + 1][:, :n], lhsT=permm[:], rhs=sc["rawb"][:, :n], start=True, stop=True),
                     reads=[("permm",), (sck, "rawb")], writes=[PS(ab + 1)])
            T.op("act", lambda e: e.activation(out=sc["sroot"][:, :n], in_=ps[ab][:, :n], func=AF.Ln, scale=1.0 / 64, bias=EPS),
                 reads=[PS(ab)], writes=[(sck, "sroot")])
            T.op("act", lambda e: e.activation(out=sc["sroot"][:, :n], in_=sc["sroot"][:, :n], func=AF.Exp, scale=-0.5),
                 reads=[(sck, "sroot")], writes=[(sck, "sroot")])
            if rope:
                T.op("dve", lambda e: e.scalar_tensor_tensor(out=sc["t1"][:, :n], in0=ps[pb][:, :n], scalar=gcol[:, gci:gci + 1],
                                                             in1=cosT[:, t0:t0 + n], op0=ALU.mult, op1=ALU.mult),
                     reads=[PS(pb), ("gcol",), ("ropet",)], writes=[(sck, "t1")])
                T.op("dve", lambda e: e.scalar_tensor_tensor(out=sc["t2"][:, :n], in0=ps[ab + 1][:, :n], scalar=gcol[:, gci + 1:gci + 2],
                                                             in1=snT[:, t0:t0 + n], op0=ALU.mult, op1=ALU.mult),
                     reads=[PS(ab + 1), ("gcol",), ("ropet",)], writes=[(sck, "t2")])
                T.op("dve", lambda e: e.tensor_tensor(out=sc["t1"][:, :n], in0=sc["t1"][:, :n], in1=sc["t2"][:, :n], op=ALU.add),
                     reads=[(sck, "t1"), (sck, "t2")], writes=[(sck, "t1")])
                T.op("pool", lambda e: e.tensor_tensor(out=dst, in0=sc["t1"][:, :n], in1=sc["sroot"][:, :n], op=ALU.mult),
                     reads=[(sck, "t1"), (sck, "sroot")], writes=dkeys)
            else:
                T.op("dve", lambda e: e.scalar_tensor_tensor(out=dst, in0=ps[pb][:, :n], scalar=gcol[:, gci:gci + 1],
                                                             in1=sc["sroot"][:, :n], op0=ALU.mult, op1=ALU.mult),
                     reads=[PS(pb), ("gcol",), (sck, "sroot")], writes=dkeys)

        for (t0, n) in chunks:
            pb = pctr[0] % 3
            pctr[0] += 1
            si = cctr[0] % 2
            cctr[0] += 1
            sc = scr[si]
            sck = "scr%d" % si
            ab = 3 + 2 * si
            for j in range(8):
                T.op("pe", lambda e, j=j, pb=pb, t0=t0, n=n: e.matmul(
                    ps[pb][:, :n], lhsT=wt[:, j, wcol0:wcol0 + 128], rhs=srcT[:, j, t0:t0 + n], start=(j == 0), stop=(j == 7)),
                    reads=[wtk] + skeyf(t0, n), writes=[PS(pb)], inc=(j == 7))
            T.op("act", lambda e, pb=pb, n=n, sc=sc: e.activation(out=sc["sq"][:, :n], in_=ps[pb][:, :n], func=AF.Square),
                 reads=[PS(pb)], writes=[(sck, "sq")])
            if rope:
                T.op("act", lambda e, pb=pb, n=n, sc=sc: e.activation(out=sc["rawb"][:, :n], in_=ps[pb][:, :n], func=AF.Identity),
                     reads=[PS(pb)], writes=[(sck, "rawb")])
            if pend:
                tail(pend.pop())
            pend.append((pb, n, t0, sc, sck, ab, dst_fn(t0, n), dkey_fn(t0, n)))
        tail(pend.pop())

    XCH = [(0, 512), (512, 512), (1024, 512), (1536, 512), (2048, 256)]
    QCH = [(128 + i * 512, 512) for i in range(4)]

    T.op("dve", lambda e: e.memset(Vaug[:, :, :, 0, 64:128], 1.0), writes=[("Vones",)])
    T.op("dve", lambda e: e.memset(Vaug[:, :, :, 1, 0:64], 1.0), writes=[("Vones",)])
    T.op("dve", lambda e: e.memset(Vcaug[:, :, :, 0, 64:128], 1.0), writes=[("Vones",)])
    T.op("dve", lambda e: e.memset(Vcaug[:, :, :, 1, 0:64], 1.0), writes=[("Vones",)])
    for jj in range(2):
        qk_tile(wring[0], ("wring", 0), jj * 128, hcT, lambda t0, n: [("hcT", 0), ("hcT", 1)], [(0, 256)], 2, False,
                lambda t0, n, jj=jj: kcT[:, jj, t0:t0 + n], lambda t0, n, jj=jj: [("kcT", jj)])
    for jj in range(2):
        qk_tile(wring[0], ("wring", 0), jj * 128, hT, hkeys, XCH, 2, True,
                lambda t0, n, jj=jj: kT[:, jj, t0:t0 + n],
                lambda t0, n, jj=jj: [("kT", jj, b) for b in range(t0 // 128, (t0 + n) // 128)])

    vctr = [0]

    def v_pair(srcT, skey, b0, dstV, dkeyname):
        vb = 6 + vctr[0] % 2
        vctr[0] += 1
        for hh in range(2):
            b = b0 + hh
            for j in range(8):
                T.op("pe", lambda e, j=j, b=b, hh=hh: e.matmul(
                    ps[vb][:, hh * 256:(hh + 1) * 256], lhsT=srcT[:, j, b * 128:(b + 1) * 128], rhs=wv[:, j, :],
                    start=(j == 0), stop=(j == 7)), reads=[("wv",), skey(b)], writes=[PS(vb)], inc=(j == 7))
        pv = ps[vb][:].rearrange("p (b j h d) -> p b j h d", b=2, j=2, h=2)
        T.op("act", lambda e: e.activation(out=dstV[:, b0:b0 + 2, :, 0, 0:64], in_=pv[:, :, :, 0, :], func=AF.Copy),
             reads=[PS(vb), ("Vones",)], writes=[(dkeyname, b0, 0)])
        T.op("act", lambda e: e.activation(out=dstV[:, b0:b0 + 2, :, 1, 64:128], in_=pv[:, :, :, 1, :], func=AF.Copy),
             reads=[PS(vb), ("Vones",)], writes=[(dkeyname, b0, 1)])

    def v_blocks(srcT, skey, nblocks, dstV, dkeyname):
        for b0 in range(0, nblocks, 2):
            v_pair(srcT, skey, b0, dstV, dkeyname)

    v_blocks(hcT, lambda b: ("hcT", b), 2, Vcaug, "Vc")
    v_blocks(hT, lambda b: ("hT", b), NBLK, Vaug, "V")
    dma_in("pool", wring[0][:], wview(w_in, 0, 8, 512, 512), ("wring", 0), "wring0")
    for grp in range(2):
        slot = 1 - grp
        for tt in range(4):
            tile_i = grp * 4 + tt
            qk_tile(wring[slot], ("wring", slot), tt * 128, hT, hkeys, QCH, 0, True,
                    lambda t0, n, tile_i=tile_i: qT[:, tile_i, t0 - 128:t0 - 128 + n],
                    lambda t0, n, tile_i=tile_i: [("qT", tile_i, b) for b in range((t0 - 128) // 128, (t0 - 128 + n) // 128)])
    T.barrier(skip=("pe",))
    for nm, t_ in (("qT", qT), ("kT", kT), ("kcT", kcT), ("Vaug", Vaug), ("Vcaug", Vcaug)):
        if nm in dump_aps:
            dump(nm, t_[:], None)
    if stop_after == "A1":
        return finish()

    wpl = at(OW + 27648, [128, 8, 512], BF16, "wpl")
    dma_in("pool", wpl[:], wview(w_in, 0, 8, QW + 2 * KVW, 512), ("wpl",), "wpl")
    NPP = 10
    pT = [at(OM + i * 2048, [128, 2, 512], BF16, "pT%d" % i) for i in range(NPP)]
    Otok = [at(OM + 20480 + i * 1024, [128, 4, 2, 64], BF16, "Otok%d" % i) for i in range(2)]
    rdt = [at(OM + 22528 + i * 32, [128, 4], F32, "rdt%d" % i) for i in range(4)]
    bctr = dict(s=0, p=0, u=0, d=0)

    def b_scores(n, jj):
        kbs = [("w", n), ("w", n + 1), ("w", n + 2), ("c", 0), ("c", 1)]
        slots = []
        qk = [[("qT", 4 * jj + r, n) for r in range(4)] for hf in range(2)]

        def s_pair(ki, kind, kb):
            sp = bctr["s"] % 2
            bctr["s"] += 1
            pslot = bctr["p"] % NPP
            bctr["p"] += 1
            slots.append(pslot)
            mi = None
            if ki == 0:
                mi = 0 if n == 0 else 1
            elif ki == 2:
                mi = 3 if n == 15 else 2
            for hf in range(2):
                rows = slice(hf * 64, hf * 64 + 64)
                bank = 2 * sp + hf
                if kind == "w":
                    lhs = kT[rows, jj, kb * 128:(kb + 1) * 128]
                    lk = ("kT", jj, kb)
                else:
                    lhs = kcT[rows, jj, kb * 128:(kb + 1) * 128]
                    lk = ("kcT", jj)
                qrhs = qT[rows, 4 * jj:4 * jj + 4, n * 128:(n + 1) * 128]
                T.op("pe", lambda e, bank=bank, lhs=lhs, qrhs=qrhs: e.matmul(ps[bank][:], lhsT=lhs, rhs=qrhs, start=True, stop=True),
                     reads=[lk] + qk[hf], writes=[PS(bank)], inc=(hf == 1))
            T.op("act", lambda e: e.activation(out=pT[pslot][:].rearrange("p a b -> p (a b)"), in_=psall[:, sp * 1024:(sp + 1) * 1024],
                                               func=AF.Exp, scale=0.125),
                 reads=[PS(2 * sp), PS(2 * sp + 1)], writes=[("pT", pslot)])
            if mi is not None:
                pv8 = pT[pslot][:].rearrange("p a (r q) -> p (a r) q", r=4)
                T.op("dve", lambda e: e.tensor_tensor(out=pv8, in0=pv8, in1=masks[:, mi:mi + 1, :].to_broadcast([128, 8, 128]), op=ALU.mult),
                     reads=[("pT", pslot), ("masks",)], writes=[("pT", pslot)])

        fns = [(lambda ki=ki, kind=kind, kb=kb: s_pair(ki, kind, kb)) for ki, (kind, kb) in enumerate(kbs)]
        return (n, jj, kbs, slots), fns

    def b_pv(st):
        n, jj, kbs, slots = st
        osl = bctr["d"] % 2
        bctr["d"] += 1
        chunks = []
        qkeys = [("qT", 4 * jj + r, n) for r in range(4)]
        ot = Otok[osl]
        otk = ("Otok", osl)
        ob0 = 4 + 2 * osl
        for hf in range(2):
            g = 2 * jj + hf
            ob = 4 + 2 * osl + hf
            rd = rdt[2 * osl + hf]
            rk = ("rdt", 2 * osl + hf)
            c0 = 0 if hf == 0 else 63
            dcol = 64 if hf == 0 else 0
            o0 = 0 if hf == 0 else 1

            def head(r, ob=ob, hf=hf, c0=c0):
                for ki, (kind, kb) in enumerate(kbs):
                    if kind == "w":
                        rhs = Vaug[:, kb, jj, hf, c0:c0 + 65]
                        lk = [("V", (kb // 2) * 2, hf), ("Vones",)]
                    else:
                        rhs = Vcaug[:, kb, jj, hf, c0:c0 + 65]
                        lk = [("Vc", 0, hf), ("Vones",)]
                    sl = slots[ki]
                    T.op("pe", lambda e, rhs=rhs, sl=sl, ki=ki: e.matmul(ps[ob][:, r * 65:(r + 1) * 65], lhsT=pT[sl][:, hf, r * 128:(r + 1) * 128],
                                                                       rhs=rhs, start=(ki == 0), stop=(ki == 4)),
                         reads=lk + [("pT", sl)], writes=[PS(ob)], inc=(ki == 4))

            def epi(ob=ob, hf=hf, g=g, rd=rd, rk=rk, dcol=dcol, o0=o0):
                ov = ps[ob][:, 0:260].rearrange("p (r c) -> p r c", r=4)
                T.op("dve", lambda e: e.tensor_tensor(out=rd[:], in0=ov[:, :, dcol], in1=esrep[:, 4 * g:4 * g + 4], op=ALU.add),
                     reads=[PS(ob), ("esrep",)], writes=[rk])
                T.op("dve", lambda e: e.reciprocal(out=rd[:], in_=rd[:]), reads=[rk], writes=[rk])
                T.op("dve", lambda e: e.tensor_tensor(out=ot[:, :, hf, :], in0=ov[:, :, o0:o0 + 64],
                                                      in1=rd[:].unsqueeze(2).to_broadcast([128, 4, 64]), op=ALU.mult),
                     reads=[PS(ob), rk], writes=[(otk, hf)])

            def c_a(head=head):
                head(0); head(1)

            def c_b(head=head, epi=epi, hf=hf):
                head(2); head(3); epi()

            chunks += [c_a, c_b]

        def tail():
            tpv = ps[ob0][:].bitcast(BF16)
            o2 = ot[:].rearrange("p r a d -> p (r a d)")
            for r in range(4):
                T.op("pe", lambda e, r=r: e.transpose(tpv[:, r * 128:(r + 1) * 128], o2[:, r * 128:(r + 1) * 128], ident[:]),
                     reads=[(otk, 0), (otk, 1), ("ident",)], writes=[PS(ob0)], inc=(r == 3))
            T.op("dve", lambda e: e.tensor_copy(out=qT[:, 4 * jj:4 * jj + 4, n * 128:(n + 1) * 128],
                                                in_=tpv[:, 0:512].rearrange("p (r q) -> p r q", r=4)),
                 reads=[PS(ob0)], writes=qkeys)

        return chunks + [tail]

    mwr2 = [at(OW + 10240 + i * 8192, [128, 8, 512], BF16, "mwr2%d" % i) for i in range(2)]
    mbr2 = [at(OW + i * 2048, [128, 512], F32, "mbr2%d" % i) for i in range(2)]
    clhs2 = at(OW + 4096, [128, 16, 128], BF16, "clhs2")
    modtmp.clear()
    modtmp.append(at(OW + 8192, [128, 512], F32, "modtmpb"))
    T.op("dve", lambda e: e.tensor_copy(out=clhs2[:], in_=cTsil[:].unsqueeze(2).to_broadcast([128, 16, 128])),
         reads=[("cTsil",)], writes=[("clhs2",)])
    units = [(n, jj) for n in range(16) for jj in range(2)]
    noop = lambda: None
    prev_st = None
    prev_tail = noop
    for ui in range(len(units) + 1):
        if ui < len(units):
            st, sfn = b_scores(*units[ui])
        else:
            st, sfn = None, [noop] * 5
        pvc = b_pv(prev_st) if prev_st is not None else [noop] * 5
        sfn[0](); pvc[0](); sfn[1](); prev_tail(); pvc[1](); sfn[2](); pvc[2](); sfn[3](); pvc[3](); sfn[4]()
        prev_tail = pvc[4]
        prev_st = st
        if ui % 4 == 1 and ui // 4 < 8:
            mod_chunk(4 + ui // 4, False, mwr2, mbr2, clhs2, ("clhs2",), bankbase=7)
    prev_tail()
    T.barrier(skip=("pe",))
    if "attnT" in dump_aps:
        dump("attnT", qT[:], None)
    if stop_after == "B":
        return finish()

    pmT = at(OP, [128, 4, TOWN], BF16, "pmT")
    c1w = []
    for s in range(2):
        o = OX + 32768 + s * 7168
        c1w.append(dict(ga=at(o, [128, 8, 128], BF16, "wga%d" % s), gp=at(o + 2048, [128, 8, 128], BF16, "wgp%d" % s),
                        wa=at(o + 4096, [128, 8, 128], BF16, "wa%d" % s), wp=at(o + 6144, [128, 4, 128], BF16, "wp%d" % s)))

    def c1_load(j):
        s_ = j % 2
        cw = c1w[s_]
        dma_in("pool", cw["ga"][:], wview(w_in, 0, 8, 2048 + j * 128, 128), ("c1w", s_, 0), "c1w%d_0" % s_)
        dma_in("pool", cw["gp"][:], wview(w_in, 0, 8, 3072 + j * 128, 128), ("c1w", s_, 1), "c1w%d_1" % s_)
        dma_in("pool", cw["wa"][:], wview(w_attn, 0, 8, j * 128, 128), ("c1w", s_, 2), "c1w%d_2" % s_)
        dma_in("pool", cw["wp"][:], wview(w_pool, 0, 4, j * 128, 128), ("c1w", s_, 3), "c1w%d_3" % s_)

    c1_load(0)
    c1_load(1)
    ubuf = at(OW, [128, TEXT], F32, "ubuf")
    sa = at(OW + 9216, [128, TEXT], F32, "sa")
    sbb = at(OW + 18432, [128, TEXT], F32, "sbb")
    diff = [at(OM + i * 4096, [128, TOWN], BF16, "diff%d" % i) for i in range(2)]
    ubufs = [ubuf, at(OM + 8192, [128, TEXT], F32, "ubuf2")]

    def ap_proj(gi, ci, t0, n):
        ubuf = ubufs[gi % 2]
        pb = (gi * 5 + ci) % 4
        for j in range(8):
            T.op("pe", lambda e, j=j: e.matmul(ps[pb][:, :n], lhsT=wpl[:, j, gi * 128:(gi + 1) * 128], rhs=hT[:, j, t0:t0 + n],
                                               start=(j == 0), stop=(j == 7)),
                 reads=[("wpl",)] + hkeys(t0, n), writes=[PS(pb)], inc=(j == 7))
        T.op("act", lambda e: e.activation(out=ubuf[:, t0:t0 + n], in_=ps[pb][:, :n], func=AF.Copy),
             reads=[PS(pb)], writes=[("u", gi % 2, ci)])

    def ap_mm(gi, cc, df, dfk):
        pb = 4 + cc % 2
        T.op("pe", lambda e: e.matmul(ps[pb][:], lhsT=poolw[:, gi, :], rhs=df[:, cc * 512:(cc + 1) * 512], start=True, stop=True),
             reads=[("poolw",), dfk], writes=[PS(pb)])
        T.op("act", lambda e: e.activation(out=pmT[:, gi, cc * 512:(cc + 1) * 512], in_=ps[pb][:], func=AF.Copy, scale=pscl[:, gi:gi + 1]),
             reads=[PS(pb), ("pscl",)], writes=[("pmT", gi, cc)])

    def ap_projs(gi):
        for ci, (t0, n) in enumerate(XCH):
            ap_proj(gi, ci, t0, n)

    def ap_group(gi):
        w = 2 << gi
        ubuf = ubufs[gi % 2]
        uk = [("u", gi % 2, ci) for ci in range(5)]
        T.op("dve", lambda e: e.tensor_scalar(out=ubuf[:, 120:128], in0=ubuf[:, 120:128], scalar1=pedg[:, 0:1], scalar2=None, op0=ALU.mult),
             reads=[("u", gi % 2, 0), ("pedg",)], writes=[("u", gi % 2, 0)])
        T.op("dve", lambda e: e.tensor_scalar(out=ubuf[:, 2176:2184], in0=ubuf[:, 2176:2184], scalar1=pedg[:, 1:2], scalar2=None, op0=ALU.mult),
             reads=[("u", gi % 2, 4), ("pedg",)], writes=[("u", gi % 2, 4)])
        T.op("dve", lambda e: e.tensor_tensor(out=sa[:, 1:TEXT], in0=ubuf[:, 1:TEXT], in1=ubuf[:, 0:TEXT - 1], op=ALU.add),
             reads=uk, writes=[("sa",)])
        W_, wk_ = sa, ("sa",)
        if gi >= 1:
            T.op("dve", lambda e: e.tensor_tensor(out=sbb[:, 3:TEXT], in0=sa[:, 3:TEXT], in1=sa[:, 1:TEXT - 2], op=ALU.add),
                 reads=[("sa",)], writes=[("sbb",)])
            W_, wk_ = sbb, ("sbb",)
        if gi >= 2:
            T.op("dve", lambda e: e.tensor_tensor(out=sa[:, 7:TEXT], in0=sbb[:, 7:TEXT], in1=sbb[:, 3:TEXT - 4], op=ALU.add),
                 reads=[("sbb",)], writes=[("sa",)])
            W_, wk_ = sa, ("sa",)
        if gi >= 3:
            T.op("dve", lambda e: e.tensor_tensor(out=sbb[:, 15:TEXT], in0=sa[:, 15:TEXT], in1=sa[:, 7:TEXT - 8], op=ALU.add),
                 reads=[("sa",)], writes=[("sbb",)])
            W_, wk_ = sbb, ("sbb",)
        woff = 128 + (w // 2 - 1)
        T.op("dve", lambda e: e.tensor_tensor(out=W_[:, woff:woff + 8], in0=W_[:, woff:woff + 8],
                                              in1=pedg[:, 2 + gi * 8:10 + gi * 8], op=ALU.mult),
             reads=[wk_, ("pedg",)], writes=[wk_])
        T.op("dve", lambda e: e.tensor_tensor(out=W_[:, woff + 2040:woff + 2048], in0=W_[:, woff + 2040:woff + 2048],
                                              in1=pedg[:, 34 + gi * 8:42 + gi * 8], op=ALU.mult),
             reads=[wk_, ("pedg",)], writes=[wk_])
        df = diff[gi % 2]
        dfk = ("diff", gi % 2)
        T.op("dve", lambda e: e.scalar_tensor_tensor(out=df[:], in0=W_[:, woff:woff + TOWN], scalar=1.0 / w, in1=ubuf[:, 128:128 + TOWN],
                                                     op0=ALU.mult, op1=ALU.subtract),
             reads=[wk_] + uk, writes=[dfk])
        for cc in range(4):
            ap_mm(gi, cc, df, dfk)

    ap_projs(0)
    for gi in range(4):
        if gi + 1 < 4:
            ap_projs(gi + 1)
        ap_group(gi)
    T.barrier(skip=("pe",))
    if "pmT" in dump_aps:
        dump("pmT", pmT[:], None)
    if stop_after == "AP":
        return finish()

    mergedT = at(OM, [128, 8, TOWN], BF16, "mergedT")
    sig = [dict(ga=at(OW + 16384 + s * 4096, [128, 512], F32, "sga%d" % s), gp=at(OW + 18432 + s * 4096, [128, 512], F32, "sgp%d" % s))
           for s in range(2)]
    tt_ = [dict(t1=at(OW + 24576 + s * 4096, [128, 512], F32, "ct1%d" % s), t2=at(OW + 26624 + s * 4096, [128, 512], F32, "ct2%d" % s))
           for s in range(2)]
    wo = at(OW, [128, 8, D], BF16, "wo")
    dma_in("pool", wo[:], wview(w_out, 0, 8, 0, D), ("wo",), "wo")
    T.op("dve", lambda e: e.tensor_tensor(out=wo[:], in0=wo[:], in1=g1rep[:].unsqueeze(1).to_broadcast([128, 8, D]), op=ALU.mult),
         reads=[("wo",)] + repkeys(g1rep), writes=[("wo",)])
    def c1_unit(j, cc, un, s, cw):
        bs = 4 * (un % 2)
        sg = sig[un % 2]
        tq = tt_[un % 2]
        sk = "c1s%d" % (un % 2)
        t0 = 128 + cc * 512

        def grp(bo, wt, wi, srcf, skeys, nk):
            for k in range(nk):
                T.op("pe", lambda e, k=k: e.matmul(ps[bs + bo][:], lhsT=wt[:, k, :], rhs=srcf(k), start=(k == 0), stop=(k == nk - 1)),
                     reads=[("c1w", s, wi)] + skeys, writes=[PS(bs + bo)], inc=(k == nk - 1))

        grp(0, cw["ga"], 0, lambda k: hT[:, k, t0:t0 + 512], hkeys(t0, 512), 8)
        grp(1, cw["gp"], 1, lambda k: hT[:, k, t0:t0 + 512], hkeys(t0, 512), 8)
        grp(2, cw["wa"], 2, lambda k: qT[:, k, cc * 512:(cc + 1) * 512], [("qT", k, cc * 4 + b) for k in range(8) for b in range(4)], 8)
        grp(3, cw["wp"], 3, lambda k: pmT[:, k, cc * 512:(cc + 1) * 512], [("pmT", k, cc) for k in range(4)], 4)
        T.op("act", lambda e: e.activation(out=sg["ga"][:], in_=ps[bs][:], func=AF.Sigmoid, bias=gateb[:, j:j + 1]),
             reads=[PS(bs), ("gateb",)], writes=[(sk, "ga")])
        T.op("act", lambda e: e.activation(out=sg["gp"][:], in_=ps[bs + 1][:], func=AF.Sigmoid, bias=gateb[:, 8 + j:9 + j]),
             reads=[PS(bs + 1), ("gateb",)], writes=[(sk, "gp")])
        T.op("dve", lambda e: e.tensor_tensor(out=tq["t1"][:], in0=ps[bs + 2][:], in1=sg["ga"][:], op=ALU.mult),
             reads=[PS(bs + 2), (sk, "ga")], writes=[(sk, "t1")])
        T.op("dve", lambda e: e.tensor_tensor(out=tq["t2"][:], in0=ps[bs + 3][:], in1=sg["gp"][:], op=ALU.mult),
             reads=[PS(bs + 3), (sk, "gp")], writes=[(sk, "t2")])
        T.op("dve", lambda e: e.tensor_tensor(out=mergedT[:, j, cc * 512:(cc + 1) * 512], in0=tq["t1"][:], in1=tq["t2"][:], op=ALU.add),
             reads=[(sk, "t1"), (sk, "t2")], writes=[("mT", j, cc)])

    def c1_tile(j):
        s = j % 2
        cw = c1w[s]
        if j >= 2:
            c1_load(j)
        for cc in range(4):
            c1_unit(j, cc, j * 4 + cc, s, cw)

    for j in range(8):
        c1_tile(j)
    T.barrier(skip=("pe",))
    if "mergedT" in dump_aps:
        dump("mergedT", mergedT[:], None)
    if stop_after == "C1":
        return finish()

    tmp = [at(OW + 16384 + i * 4096, [128, D], F32, "tmp%d" % i) for i in range(3)]
    sq2 = at(OW + 28672, [128, D], BF16, "sq2")
    htok2 = [at(OW + 30720 + i * 2048, [128, D], BF16, "htk2%d" % i) for i in range(3)]
    wd = [at(OP, [128, 8, D], BF16, "wd0"), at(OW + 20480, [128, 8, D], BF16, "wd1")]
    wu = [at(OH + 32768, [128, 2, 8, 128], BF16, "wu0"), at(OW, [128, 2, 8, 128], BF16, "wu1"), at(OW + 4096, [128, 2, 8, 128], BF16, "wu2")]
    dma_in("pool", wd[0][:, 0:FPASS[0][1], :], w_down[0:FPASS[0][1] * 128, :].rearrange("(f p) n -> p f n", p=128), ("wd", 0), "wd0")

    def fold_g2(ws, nf):
        T.op("dve", lambda e: e.tensor_tensor(out=wd[ws][:, 0:nf, :], in0=wd[ws][:, 0:nf, :],
                                              in1=g2rep[:].unsqueeze(1).to_broadcast([128, nf, D]), op=ALU.mult),
             reads=[("wd", ws)] + repkeys(g2rep), writes=[("wd", ws)])

    dma_in("pool", wu[0][:, 0, :, :], wview(w_up, 0, 8, 0, 128), ("wu", 0, 0), "wu0_0")
    dma_in("pool", wu[0][:, 1, :, :], wview(w_up, 0, 8, DFF, 128), ("wu", 0, 1), "wu0_1")
    for blk in range(16):
        xk = ("xacc", blk)
        dma_in("sp", xacc[:, blk, :], xext[128 + blk * 128:256 + blk * 128, :], xk, "xin%d" % blk)

    def c2_s1(blk):
        xk = ("xacc", blk)
        tm = tmp[blk % 3]
        tk = ("tmp", blk % 3)

        def half(hh):
            pb = (blk * 2 + hh) % 4
            for k in range(8):
                T.op("pe", lambda e, k=k: e.matmul(ps[pb][:], lhsT=mergedT[:, k, blk * 128:(blk + 1) * 128], rhs=wo[:, k, hh * 512:(hh + 1) * 512],
                                                   start=(k == 0), stop=(k == 7)),
                     reads=[("wo",)] + [("mT", k2, blk // 4) for k2 in range(8)], writes=[PS(pb)], inc=(k == 7))
            T.op("dve", lambda e: e.tensor_tensor(out=xacc[:, blk, hh * 512:(hh + 1) * 512], in0=xacc[:, blk, hh * 512:(hh + 1) * 512],
                                                  in1=ps[pb][:], op=ALU.add),
                 reads=[PS(pb), xk], writes=[xk])

        half(0)
        half(1)
        norm_block(None, xacc[:, blk, :], xk, None, 20 + blk, A2rep, B2rep, None, None, None, None, None, None, sq2)

    def c2_s2(blk):
        norm_mod(xacc[:, blk, :], [("xacc", blk)], tmp[blk % 3][:], ("tmp", blk % 3), 20 + blk, A2rep, B2rep, htok2[blk % 3], ("htk2", blk % 3))

    def c2_s3(blk):
        norm_tp(h2T, blk * 128, ("h2T", blk), htok2[blk % 3], ("htk2", blk % 3), 6 + blk % 2)

    for it in range(16 + 3):
        if 0 <= it - 2 < 16:
            c2_s2(it - 2)
        if it < 16:
            c2_s1(it)
        if 0 <= it - 3 < 16:
            c2_s3(it - 3)
    T.barrier(skip=("pe",))
    if "xmid" in dump_aps:
        dump("xmid", xacc[:], None)
    if "h2T" in dump_aps:
        dump("h2T", h2T[:], None)
    if stop_after == "C2":
        return finish()

    actT = at(OM, [128, 8, TOWN], BF16, "actT")
    sil = [at(OW + 8192 + i * 2048, [128, 512], F32, "sil%d" % i) for i in range(2)]
    tmpd = [at(OW + 12288 + i * 4096, [128, D], F32, "tmpd%d" % i) for i in range(2)]
    assert OW + 20480 + 16384 <= OMOD
    dctr = dict(f=0, u=0)

    def d1_unit(fi, cc, us):
        bs = 2 * (dctr["u"] % 4)
        sl = sil[dctr["u"] % 2]
        slk = ("sil", dctr["u"] % 2)
        dctr["u"] += 1
        for ab in range(2):
            for k in range(8):
                T.op("pe", lambda e, k=k, ab=ab: e.matmul(ps[bs + ab][:], lhsT=wu[us][:, ab, k, :], rhs=h2T[:, k, cc * 512:(cc + 1) * 512],
                                                          start=(k == 0), stop=(k == 7)),
                     reads=[("wu", us, ab)] + [("h2T", cc * 4 + b2) for b2 in range(4)], writes=[PS(bs + ab)], inc=(k == 7))
        T.op("act", lambda e: e.activation(out=sl[:], in_=ps[bs][:], func=AF.Silu), reads=[PS(bs)], writes=[slk])
        T.op("dve", lambda e: e.tensor_tensor(out=actT[:, fi, cc * 512:(cc + 1) * 512], in0=ps[bs + 1][:], in1=sl[:], op=ALU.mult),
             reads=[PS(bs + 1), slk], writes=[("actT", fi, cc)])

    def d1_tile(f, fi):
        us = dctr["f"] % 3
        dctr["f"] += 1
        if f > 0:
            dma_in("pool", wu[us][:, 0, :, :], wview(w_up, 0, 8, f * 128, 128), ("wu", us, 0), "wu%d_0" % us)
            dma_in("pool", wu[us][:, 1, :, :], wview(w_up, 0, 8, DFF + f * 128, 128), ("wu", us, 1), "wu%d_1" % us)
        for cc in range(4):
            d1_unit(fi, cc, us)

    def d2_block(blk, ws, nf, last):
        xk = ("xacc", blk)
        tm = tmpd[blk % 2]
        tk = ("tmpd", blk % 2)

        def half(hh):
            pb = (blk * 2 + hh) % 8
            for fi in range(nf):
                T.op("pe", lambda e, fi=fi: e.matmul(ps[pb][:], lhsT=actT[:, fi, blk * 128:(blk + 1) * 128], rhs=wd[ws][:, fi, hh * 512:(hh + 1) * 512],
                                                     start=(fi == 0), stop=(fi == nf - 1)),
                     reads=[("wd", ws), ("actT", fi, blk // 4)], writes=[PS(pb)], inc=(fi == nf - 1))
            T.op("dve", lambda e: e.tensor_tensor(out=xacc[:, blk, hh * 512:(hh + 1) * 512], in0=xacc[:, blk, hh * 512:(hh + 1) * 512],
                                                  in1=ps[pb][:], op=ALU.add),
                 reads=[PS(pb), xk], writes=[xk])

        half(0)
        half(1)
        if last:
            T.op("sp", lambda e: e.dma_start(out=outd[blk * 128:(blk + 1) * 128, :], in_=xacc[:, blk, :]), reads=[xk], dma="out%d" % blk)
            out_sems.append("out%d" % blk)

    def d_pass(p, f0, nf):
        ws = p % 2
        if p > 0:
            dma_in("pool", wd[ws][:, 0:nf, :], w_down[f0 * 128:(f0 + nf) * 128, :].rearrange("(f p) n -> p f n", p=128), ("wd", ws), "wd%d" % ws)
        fold_g2(ws, nf)
        for fi in range(nf):
            d1_tile(f0 + fi, fi)
        for blk in range(16):
            d2_block(blk, ws, nf, p == len(FPASS) - 1)

    for p, (f0, nf) in enumerate(FPASS):
        d_pass(p, f0, nf)
    return finish()


def _head_perm():
    order = []
    for i in range(8):
        for hf in range(2):
            order.append(4 * (2 * (i // 4) + hf) + (i % 4))
    return order


def _const_mats():
    cm = np.zeros((128, 3, 128), np.float32)
    cm[:, 0, :] = np.eye(128, dtype=np.float32)
    for k in range(128):
        for m in range(128):
            if k // 64 == m // 64:
                cm[k, 1, m] = 1.0
    for m in range(128):
        d = m % 64
        a, r = d // 32, d % 32
        pd = a * 32 + (r + 16) % 32
        cm[(m - d) + pd, 2, m] = 1.0
    return cm


def _partner(d):
    a, r = d // 32, d % 32
    return a * 32 + (r + 16) % 32


def _rope_tables(half):
    tl = np.arange(TEXT)
    t = (half * TOWN - HALO + tl) % S
    row = (t // 64).astype(np.float32)
    col = (t % 64).astype(np.float32)
    inv = (1.0 / (np.float32(10000.0) ** (np.arange(0, 32, 2, dtype=np.float32) / np.float32(32)))).astype(np.float32)
    tab = np.zeros((128, 2, TEXT), np.float32)
    for d in range(64):
        a, r = d // 32, d % 32
        j = r % 16
        pos = row if a == 0 else col
        ang = (pos * inv[j]).astype(np.float32)
        tab[d, 0] = np.cos(ang)
        tab[d, 1] = (-np.sin(ang)) if r < 16 else np.sin(ang)
    tab[64:] = tab[:64]
    return tab


def _masks(half):
    jp = np.arange(128)[:, None]
    i = np.arange(128)[None, :]
    prev = (jp >= i).astype(np.float32)
    nxt = (jp <= i).astype(np.float32)
    m = np.zeros((128, 4, 128), np.float32)
    m[:, 0] = prev if half == 1 else 0.0
    m[:, 1] = prev
    m[:, 2] = nxt
    m[:, 3] = nxt if half == 0 else 0.0
    return m


def _pool_edge(half):
    pe = np.ones((128, 66), np.float32)
    pe[:, 0] = 0.0 if half == 0 else 1.0
    pe[:, 1] = 0.0 if half == 1 else 1.0
    for gi, w in enumerate((2, 4, 8, 16)):
        for i in range(8):
            if half == 0:
                t = i
                cntv = min(t + w // 2, S) - max(t - w // 2, 0)
                pe[:, 2 + gi * 8 + i] = np.float32(w) / np.float32(cntv)
            if half == 1:
                t = S - 8 + i
                cntv = min(t + w // 2, S) - max(t - w // 2, 0)
                pe[:, 34 + gi * 8 + i] = np.float32(w) / np.float32(cntv)
    return pe


def make_in_maps(x, c, ctx, c_ctx, mod_w, mod_b, norm1_g, norm2_g, w_in, gate_b, q_norm_g, k_norm_g,
                 sink, pool_w, pool_scale, w_attn_proj, w_pool_proj, w_out, w_up, w_down):
    f = lambda a: np.ascontiguousarray(np.asarray(a, dtype=np.float32))
    x, c, ctx, c_ctx = f(x), f(c), f(ctx), f(c_ctx)
    hp = _head_perm()
    qcols = np.concatenate([np.arange(h * 64, (h + 1) * 64) for h in hp])
    w_in0 = f(w_in)[0]
    w_in_p = np.ascontiguousarray(np.concatenate([w_in0[:, qcols], w_in0[:, QW:]], axis=1))
    w_attn_p = np.ascontiguousarray(f(w_attn_proj)[0][qcols, :])
    mod_w0, w_pool0, w_out0, w_up0, w_down0 = f(mod_w)[0], f(w_pool_proj)[0], f(w_out)[0], f(w_up)[0], f(w_down)[0]
    modb_rep = np.ascontiguousarray(np.broadcast_to(f(mod_b)[0][None, :], (128, 6 * D)))
    n1g_rep = np.ascontiguousarray(np.broadcast_to(f(norm1_g)[0][None, :], (128, D)))
    n2g_rep = np.ascontiguousarray(np.broadcast_to(f(norm2_g)[0][None, :], (128, D)))
    gateb_col = np.ascontiguousarray(f(gate_b)[0].reshape(16, 128).T)
    qg, kg = f(q_norm_g)[0], f(k_norm_g)[0]
    dd = np.arange(128) % 64
    pp = np.array([_partner(d) for d in dd])
    g_cols = np.ascontiguousarray(np.stack([qg[dd], qg[pp], kg[dd], kg[pp]], axis=1))
    sink_row = np.ascontiguousarray(f(sink)[0][None, :])
    sink_rep = np.ascontiguousarray(np.broadcast_to(f(sink)[0][None, :], (128, NH)))
    sinkl = np.zeros((1, 256), np.float32)
    sinkl[0, 64:128] = 1.0
    sinkl[0, 128:192] = 1.0
    pscale_col = np.ascontiguousarray(f(pool_scale)[0].reshape(4, 128).T)
    pool_w0 = f(pool_w)[0]
    cm = _const_mats()
    c_ctx_col = c_ctx.reshape(8, 128).T
    maps = []
    for core in range(8):
        b, half = core // 2, core % 2
        xe = np.zeros((TEXT, D), np.float32)
        lo = half * TOWN - HALO
        a0, a1 = max(lo, 0), min(lo + TEXT, S)
        xe[a0 - lo:a1 - lo] = x[b, a0:a1]
        cT = np.ascontiguousarray(np.concatenate([c[b].reshape(8, 128).T, c_ctx_col], axis=1))
        maps.append({
            "xext": xe, "ctx": np.ascontiguousarray(ctx[b]), "cT": cT, "mod_w": mod_w0, "modb_rep": modb_rep,
            "n1g_rep": n1g_rep, "n2g_rep": n2g_rep, "w_in_p": w_in_p, "w_attn_p": w_attn_p, "w_pool": w_pool0,
            "w_out": w_out0, "w_up": w_up0, "w_down": w_down0, "gateb_col": gateb_col, "g_cols": g_cols,
            "sink_row": sink_row, "sinkl": sinkl, "sink_rep": sink_rep, "pool_w": pool_w0, "pscale_col": pscale_col,
            "rope_cs": _rope_tables(half), "masks": _masks(half), "pool_edge": _pool_edge(half), "cmat": cm,
        })
    return maps


_NC_CACHE = {}


def kernel(**inputs):
    maps = make_in_maps(**inputs)
    if "nc" not in _NC_CACHE:
        _NC_CACHE["nc"] = build_program()
    nc = _NC_CACHE["nc"]
    res = run_bass_kernel_spmd(nc, maps, core_ids=list(range(8)))
    out = np.empty((4, S, D), np.float32)
    for core in range(8):
        b, half = core // 2, core % 2
        out[b, half * TOWN:(half + 1) * TOWN] = res.results[core]["out"]
    return out
```

```python
import numpy as np
import concourse.bass as bass
import concourse.mybir as mybir
from concourse.bass_utils import run_bass_kernel_spmd

F32 = mybir.dt.float32
BF16 = mybir.dt.bfloat16
ALU = mybir.AluOpType
AF = mybir.ActivationFunctionType

D = 1024
S = 4096
CTX = 256
TOWN = 2048
HALO = 128
TEXT = TOWN + 2 * HALO
NBLK = TEXT // 128
NH = 16
QW = 1024
KVW = 256
DFF = 2816
EPS = 1e-6
SB_BASE = 16512
SB_END = 229312
FPASS = [(0, 8), (8, 7), (15, 7)]


class Trk:
    ENG = ("pe", "act", "dve", "pool", "sp")

    def __init__(self, nc):
        self.nc = nc
        self.streams = {e: [] for e in self.ENG}
        self.cnt = {e: 0 for e in self.ENG}
        self.seen = {e: {} for e in self.ENG}
        self.state = {}
        self.dma_cnt = {}
        self.sem_handles = {}

    def sem(self, name):
        if name not in self.sem_handles:
            self.sem_handles[name] = self.nc.alloc_semaphore(name="s_" + name)
        return self.sem_handles[name]

    def _st(self, k):
        st = self.state.get(k)
        if st is None:
            st = {"w": None, "r": {}}
            self.state[k] = st
        return st

    def op(self, eng, fn, reads=(), writes=(), dma=None, inc=True):
        deps = {}

        def add(s, v):
            if s not in deps or deps[s] < v:
                deps[s] = v

        for k in reads:
            st = self._st(k)
            if st["w"] is not None:
                add(*st["w"])
            if k[0] == "ps":
                for s, v in st["r"].items():
                    add(s, v)
        for k in writes:
            st = self._st(k)
            if st["w"] is not None:
                add(*st["w"])
            for s, v in st["r"].items():
                add(s, v)
        waits = []
        for s, v in deps.items():
            if eng == "pe" and s == "pe":
                continue
            if self.seen[eng].get(s, 0) >= v:
                continue
            self.seen[eng][s] = v
            waits.append((s, v))
        if dma is not None:
            self.dma_cnt[dma] = self.dma_cnt.get(dma, 0) + 16
            ev = (dma, self.dma_cnt[dma])
            incr = (dma, 16)
        elif inc:
            self.cnt[eng] += 1
            ev = (eng, self.cnt[eng])
            incr = (eng, 1)
        else:
            ev = (eng, self.cnt[eng] + 1)
            incr = None
        self.streams[eng].append((waits, fn, incr, dma is not None))
        for k in reads:
            st = self._st(k)
            if st["r"].get(ev[0], 0) < ev[1]:
                st["r"][ev[0]] = ev[1]
        for k in writes:
            self.state[k] = {"w": ev, "r": {}}
        return ev

    def barrier(self, skip=()):
        evs = [(e, self.cnt[e]) for e in self.ENG if self.cnt[e] > 0]
        evs += list(self.dma_cnt.items())
        for e in self.ENG:
            if e in skip:
                continue
            waits = []
            for s, v in evs:
                if e == "pe" and s == "pe":
                    continue
                if self.seen[e].get(s, 0) >= v:
                    continue
                self.seen[e][s] = v
                waits.append((s, v))
            if waits:
                self.streams[e].append((waits, None, None, False))

    def final_wait(self, eng, sems):
        waits = [(s, self.dma_cnt[s]) for s in sems]
        self.streams[eng].append((waits, None, None, False))

    def replay(self, eng, e):
        for waits, fn, incr, isdma in self.streams[eng]:
            if fn is None:
                for s, v in waits:
                    e.wait_ge(self.sem(s), v)
                continue
            attach = None
            if waits and not isdma:
                attach = waits[0]
                rest = waits[1:]
            else:
                rest = waits
            for s, v in rest:
                e.wait_ge(self.sem(s), v)
            ins = fn(e)
            if attach is not None:
                ins._wait_ge(self.sem(attach[0]), attach[1])
            if incr is not None:
                ins.then_inc(self.sem(incr[0]), incr[1])


def build_program(stop_after=None, dumps=()):
    nc = bass.Bass("TRN2", target_bir_lowering=False)
    T = Trk(nc)
    cnt = [0]

    def at(off, shape, dtype, name):
        cnt[0] += 1
        esz = 4 if dtype == F32 else 2
        size = esz * int(np.prod(shape[1:]))
        assert off % 32 == 0 and off >= SB_BASE and off + size <= SB_END, (name, off, size)
        return nc.alloc_sbuf_tensor_at("%s_%d" % (name, cnt[0]), list(shape), dtype, offset=off)

    def din(name, shape):
        return nc.dram_tensor(name, list(shape), F32, kind="ExternalInput").ap()

    xext = din("xext", [TEXT, D])
    ctxd = din("ctx", [CTX, D])
    cTd = din("cT", [128, 16])
    modw = din("mod_w", [D, 6 * D])
    modbr = din("modb_rep", [128, 6 * D])
    n1g = din("n1g_rep", [128, D])
    n2g = din("n2g_rep", [128, D])
    w_in = din("w_in_p", [D, 4096])
    w_attn = din("w_attn_p", [QW, D])
    w_pool = din("w_pool", [512, D])
    w_out = din("w_out", [D, D])
    w_up = din("w_up", [D, 2 * DFF])
    w_down = din("w_down", [DFF, D])
    gatebc = din("gateb_col", [128, 16])
    gcols = din("g_cols", [128, 4])
    sinkrow = din("sink_row", [1, NH])
    sinkld = din("sinkl", [1, 256])
    sinkrepd = din("sink_rep", [128, NH])
    poolwd = din("pool_w", [4, 128, 128])
    pscol = din("pscale_col", [128, 4])
    ropecs = din("rope_cs", [128, 2, TEXT])
    masksd = din("masks", [128, 4, 128])
    pedge = din("pool_edge", [128, 66])
    cmat = din("cmat", [128, 3, 128])
    outd = nc.dram_tensor("out", [TOWN, D], F32, kind="ExternalOutput").ap()
    dump_aps = {}
    for nm, shp in dumps:
        dump_aps[nm] = nc.dram_tensor("dbg_" + nm, list(shp), F32, kind="ExternalOutput").ap()

    OC = SB_BASE
    OH = OC + 4608
    OX = OH + 36864
    OM = OX + 65536
    OP = OM + 32768
    OW = OP + 16384
    OMOD = SB_END - 16384
    assert OMOD - OW >= 34000, OMOD - OW

    c = OC
    ident = at(c, [128, 128], BF16, "ident"); c += 256
    bones = at(c, [128, 128], BF16, "bones"); c += 256
    permm = at(c, [128, 128], BF16, "permm"); c += 256
    masks = at(c, [128, 4, 128], BF16, "masks"); c += 1024
    gateb = at(c, [128, 16], F32, "gateb"); c += 64
    gcol = at(c, [128, 4], F32, "gcol"); c += 32
    pscl = at(c, [128, 4], F32, "pscl"); c += 32
    pedg = at(c, [128, 66], F32, "pedg"); c += 288
    cTs = at(c, [128, 16], F32, "cTs"); c += 64
    cTsil = at(c, [128, 16], F32, "cTsil"); c += 64
    stat = at(c, [128, 128], F32, "stat"); c += 512
    sinkl = at(c, [1, 256], BF16, "sinkl"); c += 512
    esrow = at(c, [1, NH], BF16, "esrow"); c += 32
    poolw = at(c, [128, 4, 128], BF16, "poolw"); c += 1024
    esrep = at(c, [128, NH], F32, "esrep"); c += 64
    assert c <= OH

    hT = at(OH, [128, 8, TEXT], BF16, "hT")
    h2T = at(OH, [128, 8, TOWN], BF16, "h2T")
    qT = at(OX, [128, 8, TOWN], BF16, "qT")
    kT = at(OX + 32768, [128, 2, TEXT], BF16, "kT")
    Vaug = at(OX + 41984, [128, NBLK, 2, 2, 128], BF16, "Vaug")
    kcT = at(OX + 60416, [128, 2, CTX], BF16, "kcT")
    Vcaug = at(OX + 61440, [128, 2, 2, 2, 128], BF16, "Vcaug")
    xacc = at(OX, [128, 16, D], F32, "xacc")
    g1rep = at(OMOD, [128, D], F32, "g1rep")
    A2rep = at(OMOD + 4096, [128, D], F32, "A2rep")
    B2rep = at(OMOD + 8192, [128, D], F32, "B2rep")
    g2rep = at(OMOD + 12288, [128, D], F32, "g2rep")

    psall = nc.alloc_psum_tensor("psall", [128, 4096], F32)
    ps = [psall[:, i * 512:(i + 1) * 512] for i in range(8)]

    def PS(i):
        return ("ps", i)

    def dma_in(eng, out_ap, in_ap, key, semname):
        T.op(eng, lambda e: e.dma_start(out=out_ap, in_=in_ap), writes=[key], dma=semname)

    def wview(dram, r0, nk, c0, ncol):
        return dram[r0:r0 + 128 * nk, c0:c0 + ncol].rearrange("(j p) n -> p j n", p=128)

    out_sems = []

    def dump(nm, ap, key):
        s = "dbg_" + nm
        if len(ap.shape) >= 3:
            for i in range(ap.shape[1]):
                T.op("pool", lambda e, i=i: e.dma_start(out=dump_aps[nm][:, i], in_=ap[:, i]), dma=s)
        else:
            T.op("pool", lambda e: e.dma_start(out=dump_aps[nm], in_=ap), dma=s)
        out_sems.append(s)

    def hkeys(t0, n):
        return [("hT", b) for b in range(t0 // 128, (t0 + n - 1) // 128 + 1)]

    def finish():
        T.final_wait("sp", out_sems)
        with nc.Block() as block:
            @block.tensor
            def _(e):
                T.replay("pe", e)

            @block.scalar
            def _(e):
                T.replay("act", e)

            @block.vector
            def _(e):
                T.replay("dve", e)

            @block.gpsimd
            def _(e):
                T.replay("pool", e)

            @block.sync
            def _(e):
                T.replay("sp", e)
        return nc

    cm3 = at(OW, [128, 3, 128], BF16, "cm3")
    sinkf = at(OW + 1024, [1, NH], F32, "sinkf")
    clhs = at(OW + 9216, [128, 16, 128], BF16, "clhs")
    mwr = [at(OW + 13312 + i * 8192, [128, 8, 512], BF16, "mwr%d" % i) for i in range(2)] + \
          [at(OM + i * 8192, [128, 8, 512], BF16, "mwr%d" % (2 + i)) for i in range(2)]
    mbr = [at(OW + 29696 + i * 2048, [128, 512], F32, "mbr%d" % i) for i in range(2)]
    hcT = at(OP, [128, 8, CTX], BF16, "hcT")
    ropet = at(OM, [128, 2, TEXT], F32, "ropet")
    A1rep = at(OM + 18432, [128, D], F32, "A1rep")
    B1rep = at(OM + 22528, [128, D], F32, "B1rep")
    sqscr = at(OM + 26624, [128, D], BF16, "sqscr")
    htok = [at(OX + 28672 + i * 2048, [128, D], BF16, "htok%d" % i) for i in range(4)]
    xblk = [at(OX + i * 4096, [128, D], F32, "xblk%d" % i) for i in range(3)] + [at(OX + 36864, [128, D], F32, "xblk3")]
    modtmp = [at(OX + 12288 + i * 2048, [128, 512], F32, "modtmp%d" % i) for i in range(2)]
    tmpA = [at(OX + 40960 + i * 4096, [128, D], F32, "tmpA%d" % i) for i in range(4)]
    cA1 = at(OX + 20480, [128, D], F32, "cA1")
    cB1 = at(OX + 24576, [128, D], F32, "cB1")

    T.op("dve", lambda e: e.memset(stat[:], 0.0), writes=[("statz",)])
    dma_in("pool", cm3[:], cmat, ("cm3",), "const0")
    dma_in("pool", masks[:], masksd, ("masks",), "const1")
    dma_in("pool", sinkl[:], sinkld, ("sinkl",), "const2")
    dma_in("pool", poolw[:], poolwd.rearrange("g c d -> c g d"), ("poolw",), "const3")
    for (t_, d_, k_) in [(gateb, gatebc, "gateb"), (gcol, gcols, "gcol"), (pscl, pscol, "pscl"), (pedg, pedge, "pedg"),
                         (cTs, cTd, "cTs"), (sinkf, sinkrow, "sinkf")]:
        dma_in("sp", t_[:], d_, (k_,), "c_" + k_)
    for i, (t_, k_) in enumerate([(ident, "ident"), (bones, "bones"), (permm, "permm")]):
        T.op("dve", lambda e, t_=t_, i=i: e.tensor_copy(out=t_[:], in_=cm3[:, i, :]), reads=[("cm3",)], writes=[(k_,)])
    T.op("act", lambda e: e.activation(out=esrow[:], in_=sinkf[:], func=AF.Exp), reads=[("sinkf",)], writes=[("esrow",)])
    dma_in("sp", esrep[:], sinkrepd, ("esrep",), "c_esrep")
    T.op("act", lambda e: e.activation(out=esrep[:], in_=esrep[:], func=AF.Exp), reads=[("esrep",)], writes=[("esrep",)])
    T.op("act", lambda e: e.activation(out=cTsil[:], in_=cTs[:], func=AF.Silu), reads=[("cTs",)], writes=[("cTsil",)])
    T.op("dve", lambda e: e.tensor_copy(out=clhs[:], in_=cTsil[:].unsqueeze(2).to_broadcast([128, 16, 128])),
         reads=[("cTsil",)], writes=[("clhs",)])

    def mod_dst(ch):
        kind = ch // 2
        return [(B1rep, 0), (A1rep, 1), (g1rep, 0), (B2rep, 0), (A2rep, 2), (g2rep, 0)][kind], (ch % 2) * 512

    def mod_chunk(ch, with_ctx, mwr_, mbr_, clhs_, clk, bankbase=6, preloaded=False):
        slot = ch % len(mwr_)
        wk = ("mwr", slot)
        bslot = ch % 2
        bk = ("mbr", bslot)
        if not preloaded:
            dma_in("pool", mwr_[slot][:], wview(modw, 0, 8, ch * 512, 512), wk, "mwr%d" % slot)
        dma_in("sp", mbr_[bslot][:], modbr[:, ch * 512:(ch + 1) * 512], bk, "mbr%d" % bslot)
        (dst, mode), c0 = mod_dst(ch)
        variants = []
        if with_ctx:
            variants.append((8, cB1 if ch < 2 else cA1, 1, 0 if ch < 2 else 3))
        variants.append((0, dst, 0, mode))
        for (cofs, dd, bi, md) in variants:
            bank = bankbase + bi
            for j in range(8):
                T.op("pe", lambda e, j=j, cofs=cofs, bank=bank: e.matmul(
                    ps[bank][:], lhsT=clhs_[:, cofs + j, :], rhs=mwr_[slot][:, j, :], start=(j == 0), stop=(j == 7)),
                    reads=[clk, wk], writes=[PS(bank)], inc=(j == 7))
            dk = ("rep", dd.name, c0)
            if md == 0:
                T.op("dve", lambda e, bank=bank, dd=dd: e.tensor_tensor(
                    out=dd[:, c0:c0 + 512], in0=ps[bank][:], in1=mbr_[bslot][:], op=ALU.add),
                    reads=[PS(bank), bk], writes=[dk])
            else:
                gsrc = dd if md != 3 else A1rep
                gk = dk if md != 3 else ("rep", A1rep.name, c0)
                tmpk = ("modtmp", bi)
                mt = modtmp[bi]
                T.op("dve", lambda e, bank=bank, mt=mt: e.tensor_tensor(
                    out=mt[:], in0=ps[bank][:], in1=mbr_[bslot][:], op=ALU.add),
                    reads=[PS(bank), bk], writes=[tmpk])
                T.op("dve", lambda e, dd=dd, mt=mt, gsrc=gsrc: e.scalar_tensor_tensor(
                    out=dd[:, c0:c0 + 512], in0=mt[:], scalar=1.0, in1=gsrc[:, c0:c0 + 512],
                    op0=ALU.add, op1=ALU.mult), reads=[tmpk, gk], writes=[dk])

    def repkeys(t):
        return [("rep", t.name, 0), ("rep", t.name, 512)]

    for hh_ in range(2):
        dma_in("sp", A1rep[:, hh_ * 512:(hh_ + 1) * 512], n1g[:, hh_ * 512:(hh_ + 1) * 512], ("rep", A1rep.name, hh_ * 512), "c_n1g%d" % hh_)
        dma_in("sp", A2rep[:, hh_ * 512:(hh_ + 1) * 512], n2g[:, hh_ * 512:(hh_ + 1) * 512], ("rep", A2rep.name, hh_ * 512), "c_n2g%d" % hh_)
    for ch in range(4):
        dma_in("pool", mwr[ch][:], wview(modw, 0, 8, ch * 512, 512), ("mwr", ch), "mwr%d" % ch)
    for ch in range(4):
        mod_chunk(ch, True, mwr, mbr, clhs, ("clhs",), preloaded=True)
    if stop_after == "P0":
        T.barrier(skip=("pe",))
        dump("A1", A1rep[:], None)
        return finish()

    wring = [at(OW + i * 8192, [128, 8, 512], BF16, "wring%d" % i) for i in range(2)]
    wv = at(OW + 32768, [128, 8, 256], BF16, "wv")
    T.op("pool", lambda e: e.dma_start(out=wring[0][:, :, 0:256], in_=wview(w_in, 0, 8, QW, 256)),
         writes=[("wring", 0), ("cm3",), ("sinkf",)], dma="wring0")
    T.op("pool", lambda e: e.dma_start(out=wv[:], in_=wview(w_in, 0, 8, QW + KVW, 256)),
         writes=[("wv",), ("mbr", 1)], dma="wv")
    T.op("pool", lambda e: e.dma_start(out=wring[1][:], in_=wview(w_in, 0, 8, 0, 512)),
         writes=[("wring", 1), ("clhs",), ("mwr", 0)], dma="wring1")

    def norm_block(src_ap, xb, xk, load, si, Arep, Brep, dstT, dst_c0, dkey, ht, hk, tpbank, sq):
        if load is not None:
            dma_in("sp", xb, src_ap, xk, load)
        T.op("act", lambda e: e.activation(out=sq[:], in_=xb, func=AF.Square, accum_out=stat[:, si:si + 1]),
             reads=[xk, ("statz",)], writes=[("sqscr",), ("stat", si)])
        T.op("act", lambda e: e.activation(out=stat[:, 64 + si:65 + si], in_=stat[:, si:si + 1], func=AF.Sqrt,
                                           scale=1.0 / D, bias=EPS), reads=[("stat", si)], writes=[("stat2", si)])
        return ht, hk

    def norm_finish(xin_ap, xin_keys, tmp_ap, tmpk, si, Arep, Brep, dstT, dst_c0, dkey, ht, hk, tpbank):
        norm_mod(xin_ap, xin_keys, tmp_ap, tmpk, si, Arep, Brep, ht, hk)
        norm_tp(dstT, dst_c0, dkey, ht, hk, tpbank)

    def norm_mod(xin_ap, xin_keys, tmp_ap, tmpk, si, Arep, Brep, ht, hk, split=False):
        T.op("dve", lambda e: e.reciprocal(out=stat[:, 64 + si:65 + si], in_=stat[:, 64 + si:65 + si]),
             reads=[("stat2", si)], writes=[("stat2", si)])
        T.op("dve", lambda e: e.scalar_tensor_tensor(out=tmp_ap, in0=xin_ap, scalar=stat[:, 64 + si:65 + si],
                                                     in1=Arep[:], op0=ALU.mult, op1=ALU.mult),
             reads=xin_keys + [("stat2", si)] + repkeys(Arep), writes=[tmpk])
        if split:
            T.op("pool", lambda e: e.tensor_tensor(out=ht[:, 0:512], in0=tmp_ap[:, 0:512], in1=Brep[:, 0:512], op=ALU.add),
                 reads=[tmpk] + repkeys(Brep), writes=[(hk, 0)])
            T.op("dve", lambda e: e.tensor_tensor(out=ht[:, 512:1024], in0=tmp_ap[:, 512:1024], in1=Brep[:, 512:1024], op=ALU.add),
                 reads=[tmpk] + repkeys(Brep), writes=[(hk, 1)])
        else:
            T.op("pool", lambda e: e.tensor_tensor(out=ht[:], in0=tmp_ap, in1=Brep[:], op=ALU.add),
                 reads=[tmpk] + repkeys(Brep), writes=[(hk, 0), (hk, 1)])

    def norm_tp(dstT, dst_c0, dkey, ht, hk, tpbank):
        tpv = ps[tpbank][:].bitcast(BF16)
        for j in range(8):
            T.op("pe", lambda e, j=j: e.transpose(tpv[:, j * 128:(j + 1) * 128], ht[:, j * 128:(j + 1) * 128], ident[:]),
                 reads=[(hk, 0), (hk, 1), ("ident",)], writes=[PS(tpbank)], inc=(j == 7))
        T.op("act", lambda e: e.activation(out=dstT[:, :, dst_c0:dst_c0 + 128],
                                           in_=tpv.rearrange("p (j t) -> p j t", j=8), func=AF.Identity),
             reads=[PS(tpbank)], writes=[dkey])

    a0 = []
    for which in range(2 + NBLK):
        if which < 2:
            a0.append((ctxd[which * 128:(which + 1) * 128, :], cA1, cB1, hcT, which * 128, ("hcT", which)))
        else:
            b = which - 2
            a0.append((xext[b * 128:(b + 1) * 128, :], A1rep, B1rep, hT, b * 128, ("hT", b)))

    def a0_s1(bi):
        src, Ar, Br, dstT, c0, dk = a0[bi]
        xs = bi % 4
        norm_block(src, xblk[xs][:], ("xblk", xs), "xblk%d" % xs, bi, Ar, Br, dstT, c0, dk, None, None, None, sqscr)

    def a0_s2(bi):
        src, Ar, Br, dstT, c0, dk = a0[bi]
        xs = bi % 4
        norm_mod(xblk[xs][:], [("xblk", xs)], tmpA[bi % 4][:], ("tmpA", bi % 4), bi, Ar, Br, htok[bi % 4], ("htok", bi % 4), split=True)

    def a0_s3(bi):
        src, Ar, Br, dstT, c0, dk = a0[bi]
        norm_tp(dstT, c0, dk, htok[bi % 4], ("htok", bi % 4), 4 + bi % 2)

    for it in range(len(a0) + 3):
        if it == 8:
            T.op("sp", lambda e: e.dma_start(out=ropet[:], in_=ropecs), writes=[("ropet",), ("mwr", 2), ("mwr", 3)], dma="c_ropet")
        if it < len(a0):
            a0_s1(it)
        if 0 <= it - 1 < len(a0):
            a0_s2(it - 1)
        if 0 <= it - 3 < len(a0):
            a0_s3(it - 3)
    T.barrier(skip=("pe",))
    if "hT" in dump_aps:
        dump("hT", hT[:], None)
    if "hcT" in dump_aps:
        dump("hcT", hcT[:], None)
    if "A1" in dump_aps:
        dump("A1", A1rep[:], None)
    if stop_after == "A0":
        return finish()

    scr = []
    for i in range(2):
        o = OW + 16384 + i * 8192
        scr.append(dict(sq=at(o, [128, 512], BF16, "sq%d" % i), rawb=at(o + 1024, [128, 512], BF16, "rawb%d" % i),
                        sroot=at(o + 2048, [128, 512], F32, "sroot%d" % i), t1=at(o + 4096, [128, 512], F32, "t1%d" % i),
                        t2=at(o + 6144, [128, 512], F32, "t2%d" % i)))
    cosT = ropet[:, 0, :]
    snT = ropet[:, 1, :]
    pctr = [0]
    cctr = [0]

    def qk_tile(wt, wtk, wcol0, srcT, skeyf, chunks, gci, rope, dst_fn, dkey_fn):
        pend = []

        def tail(st):
            pb, n, t0, sc, sck, ab, dst, dkeys = st
            T.op("pe", lambda e: e.matmul(ps[ab][:, :n], lhsT=bones[:], rhs=sc["sq"][:, :n], start=True, stop=True),
                 reads=[("bones",), (sck, "sq")], writes=[PS(ab)])
            if rope:
                T.op("pe", lambda e: e.matmul(ps[ab + 1][:, :n], lhsT=permm[:], rhs=sc["rawb"][:, :n], start=True, stop=True),
                     reads=[("permm",), (sck, "rawb")], writes=[PS(ab + 1)])
            T.op("act", lambda e: e.activation(out=sc["sroot"][:, :n], in_=ps[ab][:, :n], func=AF.Ln, scale=1.0 / 64, bias=EPS),
                 reads=[PS(ab)], writes=[(sck, "sroot")])
            T.op("act", lambda e: e.activation(out=sc["sroot"][:, :n], in_=sc["sroot"][:, :n], func=AF.Exp, scale=-0.5),
                 reads=[(sck, "sroot")], writes=[(sck, "sroot")])
            if rope:
                T.op("dve", lambda e: e.scalar_tensor_tensor(out=sc["t1"][:, :n], in0=ps[pb][:, :n], scalar=gcol[:, gci:gci + 1],
                                                             in1=cosT[:, t0:t0 + n], op0=ALU.mult, op1=ALU.mult),
                     reads=[PS(pb), ("gcol",), ("ropet",)], writes=[(sck, "t1")])
                T.op("dve", lambda e: e.scalar_tensor_tensor(out=sc["t2"][:, :n], in0=ps[ab + 1][:, :n], scalar=gcol[:, gci + 1:gci + 2],
                                                             in1=snT[:, t0:t0 + n], op0=ALU.mult, op1=ALU.mult),
                     reads=[PS(ab + 1), ("gcol",), ("ropet",)], writes=[(sck, "t2")])
                T.op("dve", lambda e: e.tensor_tensor(out=sc["t1"][:, :n], in0=sc["t1"][:, :n], in1=sc["t2"][:, :n], op=ALU.add),
                     reads=[(sck, "t1"), (sck, "t2")], writes=[(sck, "t1")])
                T.op("pool", lambda e: e.tensor_tensor(out=dst, in0=sc["t1"][:, :n], in1=sc["sroot"][:, :n], op=ALU.mult),
                     reads=[(sck, "t1"), (sck, "sroot")], writes=dkeys)
            else:
                T.op("dve", lambda e: e.scalar_tensor_tensor(out=dst, in0=ps[pb][:, :n], scalar=gcol[:, gci:gci + 1],
                                                             in1=sc["sroot"][:, :n], op0=ALU.mult, op1=ALU.mult),
                     reads=[PS(pb), ("gcol",), (sck, "sroot")], writes=dkeys)

        for (t0, n) in chunks:
            pb = pctr[0] % 3
            pctr[0] += 1
            si = cctr[0] % 2
            cctr[0] += 1
            sc = scr[si]
            sck = "scr%d" % si
            ab = 3 + 2 * si
            for j in range(8):
                T.op("pe", lambda e, j=j, pb=pb, t0=t0, n=n: e.matmul(
                    ps[pb][:, :n], lhsT=wt[:, j, wcol0:wcol0 + 128], rhs=srcT[:, j, t0:t0 + n], start=(j == 0), stop=(j == 7)),
                    reads=[wtk] + skeyf(t0, n), writes=[PS(pb)], inc=(j == 7))
            T.op("act", lambda e, pb=pb, n=n, sc=sc: e.activation(out=sc["sq"][:, :n], in_=ps[pb][:, :n], func=AF.Square),
                 reads=[PS(pb)], writes=[(sck, "sq")])
            if rope:
                T.op("act", lambda e, pb=pb, n=n, sc=sc: e.activation(out=sc["rawb"][:, :n], in_=ps[pb][:, :n], func=AF.Identity),
                     reads=[PS(pb)], writes=[(sck, "rawb")])
            if pend:
                tail(pend.pop())
            pend.append((pb, n, t0, sc, sck, ab, dst_fn(t0, n), dkey_fn(t0, n)))
        tail(pend.pop())

    XCH = [(0, 512), (512, 512), (1024, 512), (1536, 512), (2048, 256)]
    QCH = [(128 + i * 512, 512) for i in range(4)]

    T.op("dve", lambda e: e.memset(Vaug[:, :, :, 0, 64:128], 1.0), writes=[("Vones",)])
    T.op("dve", lambda e: e.memset(Vaug[:, :, :, 1, 0:64], 1.0), writes=[("Vones",)])
    T.op("dve", lambda e: e.memset(Vcaug[:, :, :, 0, 64:128], 1.0), writes=[("Vones",)])
    T.op("dve", lambda e: e.memset(Vcaug[:, :, :, 1, 0:64], 1.0), writes=[("Vones",)])
    for jj in range(2):
        qk_tile(wring[0], ("wring", 0), jj * 128, hcT, lambda t0, n: [("hcT", 0), ("hcT", 1)], [(0, 256)], 2, False,
                lambda t0, n, jj=jj: kcT[:, jj, t0:t0 + n], lambda t0, n, jj=jj: [("kcT", jj)])
    for jj in range(2):
        qk_tile(wring[0], ("wring", 0), jj * 128, hT, hkeys, XCH, 2, True,
                lambda t0, n, jj=jj: kT[:, jj, t0:t0 + n],
                lambda t0, n, jj=jj: [("kT", jj, b) for b in range(t0 // 128, (t0 + n) // 128)])

    vctr = [0]

    def v_pair(srcT, skey, b0, dstV, dkeyname):
        vb = 6 + vctr[0] % 2
        vctr[0] += 1
        for hh in range(2):
            b = b0 + hh
            for j in range(8):
                T.op("pe", lambda e, j=j, b=b, hh=hh: e.matmul(
                    ps[vb][:, hh * 256:(hh + 1) * 256], lhsT=srcT[:, j, b * 128:(b + 1) * 128], rhs=wv[:, j, :],
                    start=(j == 0), stop=(j == 7)), reads=[("wv",), skey(b)], writes=[PS(vb)], inc=(j == 7))
        pv = ps[vb][:].rearrange("p (b j h d) -> p b j h d", b=2, j=2, h=2)
        T.op("act", lambda e: e.activation(out=dstV[:, b0:b0 + 2, :, 0, 0:64], in_=pv[:, :, :, 0, :], func=AF.Copy),
             reads=[PS(vb), ("Vones",)], writes=[(dkeyname, b0, 0)])
        T.op("act", lambda e: e.activation(out=dstV[:, b0:b0 + 2, :, 1, 64:128], in_=pv[:, :, :, 1, :], func=AF.Copy),
             reads=[PS(vb), ("Vones",)], writes=[(dkeyname, b0, 1)])

    def v_blocks(srcT, skey, nblocks, dstV, dkeyname):
        for b0 in range(0, nblocks, 2):
            v_pair(srcT, skey, b0, dstV, dkeyname)

    v_blocks(hcT, lambda b: ("hcT", b), 2, Vcaug, "Vc")
    v_blocks(hT, lambda b: ("hT", b), NBLK, Vaug, "V")
    dma_in("pool", wring[0][:], wview(w_in, 0, 8, 512, 512), ("wring", 0), "wring0")
    for grp in range(2):
        slot = 1 - grp
        for tt in range(4):
            tile_i = grp * 4 + tt
            qk_tile(wring[slot], ("wring", slot), tt * 128, hT, hkeys, QCH, 0, True,
                    lambda t0, n, tile_i=tile_i: qT[:, tile_i, t0 - 128:t0 - 128 + n],
                    lambda t0, n, tile_i=tile_i: [("qT", tile_i, b) for b in range((t0 - 128) // 128, (t0 - 128 + n) // 128)])
    T.barrier(skip=("pe",))
    for nm, t_ in (("qT", qT), ("kT", kT), ("kcT", kcT), ("Vaug", Vaug), ("Vcaug", Vcaug)):
        if nm in dump_aps:
            dump(nm, t_[:], None)
    if stop_after == "A1":
        return finish()

    wpl = at(OW + 27648, [128, 8, 512], BF16, "wpl")
    dma_in("pool", wpl[:], wview(w_in, 0, 8, QW + 2 * KVW, 512), ("wpl",), "wpl")
    NPP = 10
    pT = [at(OM + i * 2048, [128, 2, 512], BF16, "pT%d" % i) for i in range(NPP)]
    Otok = [at(OM + 20480 + i * 1024, [128, 4, 2, 64], BF16, "Otok%d" % i) for i in range(2)]
    rdt = [at(OM + 22528 + i * 32, [128, 4], F32, "rdt%d" % i) for i in range(4)]
    bctr = dict(s=0, p=0, u=0, d=0)

    def b_scores(n, jj):
        kbs = [("w", n), ("w", n + 1), ("w", n + 2), ("c", 0), ("c", 1)]
        slots = []
        qk = [[("qT", 4 * jj + r, n) for r in range(4)] for hf in range(2)]

        def s_pair(ki, kind, kb):
            sp = bctr["s"] % 2
            bctr["s"] += 1
            pslot = bctr["p"] % NPP
            bctr["p"] += 1
            slots.append(pslot)
            mi = None
            if ki == 0:
                mi = 0 if n == 0 else 1
            elif ki == 2:
                mi = 3 if n == 15 else 2
            for hf in range(2):
                rows = slice(hf * 64, hf * 64 + 64)
                bank = 2 * sp + hf
                if kind == "w":
                    lhs = kT[rows, jj, kb * 128:(kb + 1) * 128]
                    lk = ("kT", jj, kb)
                else:
                    lhs = kcT[rows, jj, kb * 128:(kb + 1) * 128]
                    lk = ("kcT", jj)
                qrhs = qT[rows, 4 * jj:4 * jj + 4, n * 128:(n + 1) * 128]
                T.op("pe", lambda e, bank=bank, lhs=lhs, qrhs=qrhs: e.matmul(ps[bank][:], lhsT=lhs, rhs=qrhs, start=True, stop=True),
                     reads=[lk] + qk[hf], writes=[PS(bank)], inc=(hf == 1))
            T.op("act", lambda e: e.activation(out=pT[pslot][:].rearrange("p a b -> p (a b)"), in_=psall[:, sp * 1024:(sp + 1) * 1024],
                                               func=AF.Exp, scale=0.125),
                 reads=[PS(2 * sp), PS(2 * sp + 1)], writes=[("pT", pslot)])
            if mi is not None:
                pv8 = pT[pslot][:].rearrange("p a (r q) -> p (a r) q", r=4)
                T.op("dve", lambda e: e.tensor_tensor(out=pv8, in0=pv8, in1=masks[:, mi:mi + 1, :].to_broadcast([128, 8, 128]), op=ALU.mult),
                     reads=[("pT", pslot), ("masks",)], writes=[("pT", pslot)])

        fns = [(lambda ki=ki, kind=kind, kb=kb: s_pair(ki, kind, kb)) for ki, (kind, kb) in enumerate(kbs)]
        return (n, jj, kbs, slots), fns

    def b_pv(st):
        n, jj, kbs, slots = st
        osl = bctr["d"] % 2
        bctr["d"] += 1
        chunks = []
        qkeys = [("qT", 4 * jj + r, n) for r in range(4)]
        ot = Otok[osl]
        otk = ("Otok", osl)
        ob0 = 4 + 2 * osl
        for hf in range(2):
            g = 2 * jj + hf
            ob = 4 + 2 * osl + hf
            rd = rdt[2 * osl + hf]
            rk = ("rdt", 2 * osl + hf)
            c0 = 0 if hf == 0 else 63
            dcol = 64 if hf == 0 else 0
            o0 = 0 if hf == 0 else 1

            def head(r, ob=ob, hf=hf, c0=c0):
                for ki, (kind, kb) in enumerate(kbs):
                    if kind == "w":
                        rhs = Vaug[:, kb, jj, hf, c0:c0 + 65]
                        lk = [("V", (kb // 2) * 2, hf), ("Vones",)]
                    else:
                        rhs = Vcaug[:, kb, jj, hf, c0:c0 + 65]
                        lk = [("Vc", 0, hf), ("Vones",)]
                    sl = slots[ki]
                    T.op("pe", lambda e, rhs=rhs, sl=sl, ki=ki: e.matmul(ps[ob][:, r * 65:(r + 1) * 65], lhsT=pT[sl][:, hf, r * 128:(r + 1) * 128],
                                                                       rhs=rhs, start=(ki == 0), stop=(ki == 4)),
                         reads=lk + [("pT", sl)], writes=[PS(ob)], inc=(ki == 4))

            def epi(ob=ob, hf=hf, g=g, rd=rd, rk=rk, dcol=dcol, o0=o0):
                ov = ps[ob][:, 0:260].rearrange("p (r c) -> p r c", r=4)
                T.op("dve", lambda e: e.tensor_tensor(out=rd[:], in0=ov[:, :, dcol], in1=esrep[:, 4 * g:4 * g + 4], op=ALU.add),
                     reads=[PS(ob), ("esrep",)], writes=[rk])
                T.op("dve", lambda e: e.reciprocal(out=rd[:], in_=rd[:]), reads=[rk], writes=[rk])
                T.op("dve", lambda e: e.tensor_tensor(out=ot[:, :, hf, :], in0=ov[:, :, o0:o0 + 64],
                                                      in1=rd[:].unsqueeze(2).to_broadcast([128, 4, 64]), op=ALU.mult),
                     reads=[PS(ob), rk], writes=[(otk, hf)])

            def c_a(head=head):
                head(0); head(1)

            def c_b(head=head, epi=epi, hf=hf):
                head(2); head(3); epi()

            chunks += [c_a, c_b]

        def tail():
            tpv = ps[ob0][:].bitcast(BF16)
            o2 = ot[:].rearrange("p r a d -> p (r a d)")
            for r in range(4):
                T.op("pe", lambda e, r=r: e.transpose(tpv[:, r * 128:(r + 1) * 128], o2[:, r * 128:(r + 1) * 128], ident[:]),
                     reads=[(otk, 0), (otk, 1), ("ident",)], writes=[PS(ob0)], inc=(r == 3))
            T.op("dve", lambda e: e.tensor_copy(out=qT[:, 4 * jj:4 * jj + 4, n * 128:(n + 1) * 128],
                                                in_=tpv[:, 0:512].rearrange("p (r q) -> p r q", r=4)),
                 reads=[PS(ob0)], writes=qkeys)

        return chunks + [tail]

    mwr2 = [at(OW + 10240 + i * 8192, [128, 8, 512], BF16, "mwr2%d" % i) for i in range(2)]
    mbr2 = [at(OW + i * 2048, [128, 512], F32, "mbr2%d" % i) for i in range(2)]
    clhs2 = at(OW + 4096, [128, 16, 128], BF16, "clhs2")
    modtmp.clear()
    modtmp.append(at(OW + 8192, [128, 512], F32, "modtmpb"))
    T.op("dve", lambda e: e.tensor_copy(out=clhs2[:], in_=cTsil[:].unsqueeze(2).to_broadcast([128, 16, 128])),
         reads=[("cTsil",)], writes=[("clhs2",)])
    units = [(n, jj) for n in range(16) for jj in range(2)]
    noop = lambda: None
    prev_st = None
    prev_tail = noop
    for ui in range(len(units) + 1):
        if ui < len(units):
            st, sfn = b_scores(*units[ui])
        else:
            st, sfn = None, [noop] * 5
        pvc = b_pv(prev_st) if prev_st is not None else [noop] * 5
        sfn[0](); pvc[0](); sfn[1](); prev_tail(); pvc[1](); sfn[2](); pvc[2](); sfn[3](); pvc[3](); sfn[4]()
        prev_tail = pvc[4]
        prev_st = st
        if ui % 4 == 1 and ui // 4 < 8:
            mod_chunk(4 + ui // 4, False, mwr2, mbr2, clhs2, ("clhs2",), bankbase=7)
    prev_tail()
    T.barrier(skip=("pe",))
    if "attnT" in dump_aps:
        dump("attnT", qT[:], None)
    if stop_after == "B":
        return finish()

    pmT = at(OP, [128, 4, TOWN], BF16, "pmT")
    c1w = []
    for s in range(2):
        o = OX + 32768 + s * 7168
        c1w.append(dict(ga=at(o, [128, 8, 128], BF16, "wga%d" % s), gp=at(o + 2048, [128, 8, 128], BF16, "wgp%d" % s),
                        wa=at(o + 4096, [128, 8, 128], BF16, "wa%d" % s), wp=at(o + 6144, [128, 4, 128], BF16, "wp%d" % s)))

    def c1_load(j):
        s_ = j % 2
        cw = c1w[s_]
        dma_in("pool", cw["ga"][:], wview(w_in, 0, 8, 2048 + j * 128, 128), ("c1w", s_, 0), "c1w%d_0" % s_)
        dma_in("pool", cw["gp"][:], wview(w_in, 0, 8, 3072 + j * 128, 128), ("c1w", s_, 1), "c1w%d_1" % s_)
        dma_in("pool", cw["wa"][:], wview(w_attn, 0, 8, j * 128, 128), ("c1w", s_, 2), "c1w%d_2" % s_)
        dma_in("pool", cw["wp"][:], wview(w_pool, 0, 4, j * 128, 128), ("c1w", s_, 3), "c1w%d_3" % s_)

    c1_load(0)
    c1_load(1)
    ubuf = at(OW, [128, TEXT], F32, "ubuf")
    sa = at(OW + 9216, [128, TEXT], F32, "sa")
    sbb = at(OW + 18432, [128, TEXT], F32, "sbb")
    diff = [at(OM + i * 4096, [128, TOWN], BF16, "diff%d" % i) for i in range(2)]
    ubufs = [ubuf, at(OM + 8192, [128, TEXT], F32, "ubuf2")]

    def ap_proj(gi, ci, t0, n):
        ubuf = ubufs[gi % 2]
        pb = (gi * 5 + ci) % 4
        for j in range(8):
            T.op("pe", lambda e, j=j: e.matmul(ps[pb][:, :n], lhsT=wpl[:, j, gi * 128:(gi + 1) * 128], rhs=hT[:, j, t0:t0 + n],
                                               start=(j == 0), stop=(j == 7)),
                 reads=[("wpl",)] + hkeys(t0, n), writes=[PS(pb)], inc=(j == 7))
        T.op("act", lambda e: e.activation(out=ubuf[:, t0:t0 + n], in_=ps[pb][:, :n], func=AF.Copy),
             reads=[PS(pb)], writes=[("u", gi % 2, ci)])

    def ap_mm(gi, cc, df, dfk):
        pb = 4 + cc % 2
        T.op("pe", lambda e: e.matmul(ps[pb][:], lhsT=poolw[:, gi, :], rhs=df[:, cc * 512:(cc + 1) * 512], start=True, stop=True),
             reads=[("poolw",), dfk], writes=[PS(pb)])
        T.op("act", lambda e: e.activation(out=pmT[:, gi, cc * 512:(cc + 1) * 512], in_=ps[pb][:], func=AF.Copy, scale=pscl[:, gi:gi + 1]),
             reads=[PS(pb), ("pscl",)], writes=[("pmT", gi, cc)])

    def ap_projs(gi):
        for ci, (t0, n) in enumerate(XCH):
            ap_proj(gi, ci, t0, n)

    def ap_group(gi):
        w = 2 << gi
        ubuf = ubufs[gi % 2]
        uk = [("u", gi % 2, ci) for ci in range(5)]
        T.op("dve", lambda e: e.tensor_scalar(out=ubuf[:, 120:128], in0=ubuf[:, 120:128], scalar1=pedg[:, 0:1], scalar2=None, op0=ALU.mult),
             reads=[("u", gi % 2, 0), ("pedg",)], writes=[("u", gi % 2, 0)])
        T.op("dve", lambda e: e.tensor_scalar(out=ubuf[:, 2176:2184], in0=ubuf[:, 2176:2184], scalar1=pedg[:, 1:2], scalar2=None, op0=ALU.mult),
             reads=[("u", gi % 2, 4), ("pedg",)], writes=[("u", gi % 2, 4)])
        T.op("dve", lambda e: e.tensor_tensor(out=sa[:, 1:TEXT], in0=ubuf[:, 1:TEXT], in1=ubuf[:, 0:TEXT - 1], op=ALU.add),
             reads=uk, writes=[("sa",)])
        W_, wk_ = sa, ("sa",)
        if gi >= 1:
            T.op("dve", lambda e: e.tensor_tensor(out=sbb[:, 3:TEXT], in0=sa[:, 3:TEXT], in1=sa[:, 1:TEXT - 2], op=ALU.add),
                 reads=[("sa",)], writes=[("sbb",)])
            W_, wk_ = sbb, ("sbb",)
        if gi >= 2:
            T.op("dve", lambda e: e.tensor_tensor(out=sa[:, 7:TEXT], in0=sbb[:, 7:TEXT], in1=sbb[:, 3:TEXT - 4], op=ALU.add),
                 reads=[("sbb",)], writes=[("sa",)])
            W_, wk_ = sa, ("sa",)
        if gi >= 3:
            T.op("dve", lambda e: e.tensor_tensor(out=sbb[:, 15:TEXT], in0=sa[:, 15:TEXT], in1=sa[:, 7:TEXT - 8], op=ALU.add),
                 reads=[("sa",)], writes=[("sbb",)])
            W_, wk_ = sbb, ("sbb",)
        woff = 128 + (w // 2 - 1)
        T.op("dve", lambda e: e.tensor_tensor(out=W_[:, woff:woff + 8], in0=W_[:, woff:woff + 8],
                                              in1=pedg[:, 2 + gi * 8:10 + gi * 8], op=ALU.mult),
             reads=[wk_, ("pedg",)], writes=[wk_])
        T.op("dve", lambda e: e.tensor_tensor(out=W_[:, woff + 2040:woff + 2048], in0=W_[:, woff + 2040:woff + 2048],
                                              in1=pedg[:, 34 + gi * 8:42 + gi * 8], op=ALU.mult),
             reads=[wk_, ("pedg",)], writes=[wk_])
        df = diff[gi % 2]
        dfk = ("diff", gi % 2)
        T.op("dve", lambda e: e.scalar_tensor_tensor(out=df[:], in0=W_[:, woff:woff + TOWN], scalar=1.0 / w, in1=ubuf[:, 128:128 + TOWN],
                                                     op0=ALU.mult, op1=ALU.subtract),
             reads=[wk_] + uk, writes=[dfk])
        for cc in range(4):
            ap_mm(gi, cc, df, dfk)

    gorder = [3, 2, 1, 0]
    ap_projs(gorder[0])
    for gx in range(4):
        if gx + 1 < 4:
            ap_projs(gorder[gx + 1])
        ap_group(gorder[gx])
    T.barrier(skip=("pe",))
    if "pmT" in dump_aps:
        dump("pmT", pmT[:], None)
    if stop_after == "AP":
        return finish()

    mergedT = at(OM, [128, 8, TOWN], BF16, "mergedT")
    sig = [dict(ga=at(OW + 16384 + s * 4096, [128, 512], F32, "sga%d" % s), gp=at(OW + 18432 + s * 4096, [128, 512], F32, "sgp%d" % s))
           for s in range(2)]
    tt_ = [dict(t1=at(OW + 24576 + s * 4096, [128, 512], F32, "ct1%d" % s), t2=at(OW + 26624 + s * 4096, [128, 512], F32, "ct2%d" % s))
           for s in range(2)]
    wo = at(OW, [128, 8, D], BF16, "wo")
    dma_in("pool", wo[:], wview(w_out, 0, 8, 0, D), ("wo",), "wo")
    T.op("dve", lambda e: e.tensor_tensor(out=wo[:], in0=wo[:], in1=g1rep[:].unsqueeze(1).to_broadcast([128, 8, D]), op=ALU.mult),
         reads=[("wo",)] + repkeys(g1rep), writes=[("wo",)])
    def c1_unit(j, cc, un, s, cw):
        bs = 4 * (un % 2)
        sg = sig[un % 2]
        tq = tt_[un % 2]
        sk = "c1s%d" % (un % 2)
        t0 = 128 + cc * 512

        def grp(bo, wt, wi, srcf, skeys, nk):
            for k in range(nk):
                T.op("pe", lambda e, k=k: e.matmul(ps[bs + bo][:], lhsT=wt[:, k, :], rhs=srcf(k), start=(k == 0), stop=(k == nk - 1)),
                     reads=[("c1w", s, wi)] + skeys, writes=[PS(bs + bo)], inc=(k == nk - 1))

        grp(0, cw["ga"], 0, lambda k: hT[:, k, t0:t0 + 512], hkeys(t0, 512), 8)
        grp(1, cw["gp"], 1, lambda k: hT[:, k, t0:t0 + 512], hkeys(t0, 512), 8)
        grp(2, cw["wa"], 2, lambda k: qT[:, k, cc * 512:(cc + 1) * 512], [("qT", k, cc * 4 + b) for k in range(8) for b in range(4)], 8)
        grp(3, cw["wp"], 3, lambda k: pmT[:, k, cc * 512:(cc + 1) * 512], [("pmT", k, cc) for k in range(4)], 4)
        T.op("act", lambda e: e.activation(out=sg["ga"][:], in_=ps[bs][:], func=AF.Sigmoid, bias=gateb[:, j:j + 1]),
             reads=[PS(bs), ("gateb",)], writes=[(sk, "ga")])
        T.op("act", lambda e: e.activation(out=sg["gp"][:], in_=ps[bs + 1][:], func=AF.Sigmoid, bias=gateb[:, 8 + j:9 + j]),
             reads=[PS(bs + 1), ("gateb",)], writes=[(sk, "gp")])
        T.op("dve", lambda e: e.tensor_tensor(out=tq["t1"][:], in0=ps[bs + 2][:], in1=sg["ga"][:], op=ALU.mult),
             reads=[PS(bs + 2), (sk, "ga")], writes=[(sk, "t1")])
        T.op("dve", lambda e: e.tensor_tensor(out=tq["t2"][:], in0=ps[bs + 3][:], in1=sg["gp"][:], op=ALU.mult),
             reads=[PS(bs + 3), (sk, "gp")], writes=[(sk, "t2")])
        T.op("dve", lambda e: e.tensor_tensor(out=mergedT[:, j, cc * 512:(cc + 1) * 512], in0=tq["t1"][:], in1=tq["t2"][:], op=ALU.add),
             reads=[(sk, "t1"), (sk, "t2")], writes=[("mT", j, cc)])

    def c1_tile(j):
        s = j % 2
        cw = c1w[s]
        if j >= 2:
            c1_load(j)
        for cc in range(4):
            c1_unit(j, cc, j * 4 + cc, s, cw)

    for j in range(8):
        c1_tile(j)
    T.barrier(skip=("pe",))
    if "mergedT" in dump_aps:
        dump("mergedT", mergedT[:], None)
    if stop_after == "C1":
        return finish()

    tmp = [at(OW + 16384 + i * 4096, [128, D], F32, "tmp%d" % i) for i in range(3)]
    sq2 = at(OW + 28672, [128, D], BF16, "sq2")
    htok2 = [at(OW + 30720 + i * 2048, [128, D], BF16, "htk2%d" % i) for i in range(3)]
    wd = [at(OP, [128, 8, D], BF16, "wd0"), at(OW + 20480, [128, 8, D], BF16, "wd1")]
    wu = [at(OH + 32768, [128, 2, 8, 128], BF16, "wu0"), at(OW, [128, 2, 8, 128], BF16, "wu1"), at(OW + 4096, [128, 2, 8, 128], BF16, "wu2")]
    dma_in("pool", wd[0][:, 0:FPASS[0][1], :], w_down[0:FPASS[0][1] * 128, :].rearrange("(f p) n -> p f n", p=128), ("wd", 0), "wd0")

    def fold_g2(ws, nf):
        T.op("dve", lambda e: e.tensor_tensor(out=wd[ws][:, 0:nf, :], in0=wd[ws][:, 0:nf, :],
                                              in1=g2rep[:].unsqueeze(1).to_broadcast([128, nf, D]), op=ALU.mult),
             reads=[("wd", ws)] + repkeys(g2rep), writes=[("wd", ws)])

    dma_in("pool", wu[0][:, 0, :, :], wview(w_up, 0, 8, 0, 128), ("wu", 0, 0), "wu0_0")
    dma_in("pool", wu[0][:, 1, :, :], wview(w_up, 0, 8, DFF, 128), ("wu", 0, 1), "wu0_1")
    for blk in range(16):
        xk = ("xacc", blk)
        dma_in("sp", xacc[:, blk, :], xext[128 + blk * 128:256 + blk * 128, :], xk, "xin%d" % blk)

    def c2_s1(blk):
        xk = ("xacc", blk)
        tm = tmp[blk % 3]
        tk = ("tmp", blk % 3)

        def half(hh):
            pb = (blk * 2 + hh) % 4
            for k in range(8):
                T.op("pe", lambda e, k=k: e.matmul(ps[pb][:], lhsT=mergedT[:, k, blk * 128:(blk + 1) * 128], rhs=wo[:, k, hh * 512:(hh + 1) * 512],
                                                   start=(k == 0), stop=(k == 7)),
                     reads=[("wo",)] + [("mT", k2, blk // 4) for k2 in range(8)], writes=[PS(pb)], inc=(k == 7))
            T.op("dve", lambda e: e.tensor_tensor(out=xacc[:, blk, hh * 512:(hh + 1) * 512], in0=xacc[:, blk, hh * 512:(hh + 1) * 512],
                                                  in1=ps[pb][:], op=ALU.add),
                 reads=[PS(pb), xk], writes=[xk])

        half(0)
        half(1)
        norm_block(None, xacc[:, blk, :], xk, None, 20 + blk, A2rep, B2rep, None, None, None, None, None, None, sq2)

    def c2_s2(blk):
        norm_mod(xacc[:, blk, :], [("xacc", blk)], tmp[blk % 3][:], ("tmp", blk % 3), 20 + blk, A2rep, B2rep, htok2[blk % 3], ("htk2", blk % 3))

    def c2_s3(blk):
        norm_tp(h2T, blk * 128, ("h2T", blk), htok2[blk % 3], ("htk2", blk % 3), 6 + blk % 2)

    for it in range(16 + 3):
        if 0 <= it - 2 < 16:
            c2_s2(it - 2)
        if it < 16:
            c2_s1(it)
        if 0 <= it - 3 < 16:
            c2_s3(it - 3)
    T.barrier(skip=("pe",))
    if "xmid" in dump_aps:
        dump("xmid", xacc[:], None)
    if "h2T" in dump_aps:
        dump("h2T", h2T[:], None)
    if stop_after == "C2":
        return finish()

    actT = at(OM, [128, 8, TOWN], BF16, "actT")
    sil = [at(OW + 8192 + i * 2048, [128, 512], F32, "sil%d" % i) for i in range(2)]
    tmpd = [at(OW + 12288 + i * 4096, [128, D], F32, "tmpd%d" % i) for i in range(2)]
    assert OW + 20480 + 16384 <= OMOD
    dctr = dict(f=0, u=0)

    def d1_unit(fi, cc, us):
        bs = 2 * (dctr["u"] % 4)
        sl = sil[dctr["u"] % 2]
        slk = ("sil", dctr["u"] % 2)
        dctr["u"] += 1
        for ab in range(2):
            for k in range(8):
                T.op("pe", lambda e, k=k, ab=ab: e.matmul(ps[bs + ab][:], lhsT=wu[us][:, ab, k, :], rhs=h2T[:, k, cc * 512:(cc + 1) * 512],
                                                          start=(k == 0), stop=(k == 7)),
                     reads=[("wu", us, ab)] + [("h2T", cc * 4 + b2) for b2 in range(4)], writes=[PS(bs + ab)], inc=(k == 7))
        T.op("act", lambda e: e.activation(out=sl[:], in_=ps[bs][:], func=AF.Silu), reads=[PS(bs)], writes=[slk])
        T.op("dve", lambda e: e.tensor_tensor(out=actT[:, fi, cc * 512:(cc + 1) * 512], in0=ps[bs + 1][:], in1=sl[:], op=ALU.mult),
             reads=[PS(bs + 1), slk], writes=[("actT", fi, cc)])

    def d1_tile(f, fi):
        us = dctr["f"] % 3
        dctr["f"] += 1
        if f > 0:
            dma_in("pool", wu[us][:, 0, :, :], wview(w_up, 0, 8, f * 128, 128), ("wu", us, 0), "wu%d_0" % us)
            dma_in("pool", wu[us][:, 1, :, :], wview(w_up, 0, 8, DFF + f * 128, 128), ("wu", us, 1), "wu%d_1" % us)
        for cc in range(4):
            d1_unit(fi, cc, us)

    def d2_block(blk, ws, nf, last):
        xk = ("xacc", blk)
        tm = tmpd[blk % 2]
        tk = ("tmpd", blk % 2)

        def half(hh):
            pb = (blk * 2 + hh) % 8
            for fi in range(nf):
                T.op("pe", lambda e, fi=fi: e.matmul(ps[pb][:], lhsT=actT[:, fi, blk * 128:(blk + 1) * 128], rhs=wd[ws][:, fi, hh * 512:(hh + 1) * 512],
                                                     start=(fi == 0), stop=(fi == nf - 1)),
                     reads=[("wd", ws), ("actT", fi, blk // 4)], writes=[PS(pb)], inc=(fi == nf - 1))
            T.op("dve", lambda e: e.tensor_tensor(out=xacc[:, blk, hh * 512:(hh + 1) * 512], in0=xacc[:, blk, hh * 512:(hh + 1) * 512],
                                                  in1=ps[pb][:], op=ALU.add),
                 reads=[PS(pb), xk], writes=[xk])

        half(0)
        half(1)
        if last:
            T.op("sp", lambda e: e.dma_start(out=outd[blk * 128:(blk + 1) * 128, :], in_=xacc[:, blk, :]), reads=[xk], dma="out%d" % blk)
            out_sems.append("out%d" % blk)

    def d_pass(p, f0, nf):
        ws = p % 2
        if p > 0:
            dma_in("pool", wd[ws][:, 0:nf, :], w_down[f0 * 128:(f0 + nf) * 128, :].rearrange("(f p) n -> p f n", p=128), ("wd", ws), "wd%d" % ws)
        fold_g2(ws, nf)
        for fi in range(nf):
            d1_tile(f0 + fi, fi)
        for blk in range(16):
            d2_block(blk, ws, nf, p == len(FPASS) - 1)

    for p, (f0, nf) in enumerate(FPASS):
        d_pass(p, f0, nf)
    return finish()


def _head_perm():
    order = []
    for i in range(8):
        for hf in range(2):
            order.append(4 * (2 * (i // 4) + hf) + (i % 4))
    return order


def _const_mats():
    cm = np.zeros((128, 3, 128), np.float32)
    cm[:, 0, :] = np.eye(128, dtype=np.float32)
    for k in range(128):
        for m in range(128):
            if k // 64 == m // 64:
                cm[k, 1, m] = 1.0
    for m in range(128):
        d = m % 64
        a, r = d // 32, d % 32
        pd = a * 32 + (r + 16) % 32
        cm[(m - d) + pd, 2, m] = 1.0
    return cm


def _partner(d):
    a, r = d // 32, d % 32
    return a * 32 + (r + 16) % 32


def _rope_tables(half):
    tl = np.arange(TEXT)
    t = (half * TOWN - HALO + tl) % S
    row = (t // 64).astype(np.float32)
    col = (t % 64).astype(np.float32)
    inv = (1.0 / (np.float32(10000.0) ** (np.arange(0, 32, 2, dtype=np.float32) / np.float32(32)))).astype(np.float32)
    tab = np.zeros((128, 2, TEXT), np.float32)
    for d in range(64):
        a, r = d // 32, d % 32
        j = r % 16
        pos = row if a == 0 else col
        ang = (pos * inv[j]).astype(np.float32)
        tab[d, 0] = np.cos(ang)
        tab[d, 1] = (-np.sin(ang)) if r < 16 else np.sin(ang)
    tab[64:] = tab[:64]
    return tab


def _masks(half):
    jp = np.arange(128)[:, None]
    i = np.arange(128)[None, :]
    prev = (jp >= i).astype(np.float32)
    nxt = (jp <= i).astype(np.float32)
    m = np.zeros((128, 4, 128), np.float32)
    m[:, 0] = prev if half == 1 else 0.0
    m[:, 1] = prev
    m[:, 2] = nxt
    m[:, 3] = nxt if half == 0 else 0.0
    return m


def _pool_edge(half):
    pe = np.ones((128, 66), np.float32)
    pe[:, 0] = 0.0 if half == 0 else 1.0
    pe[:, 1] = 0.0 if half == 1 else 1.0
    for gi, w in enumerate((2, 4, 8, 16)):
        for i in range(8):
            if half == 0:
                t = i
                cntv = min(t + w // 2, S) - max(t - w // 2, 0)
                pe[:, 2 + gi * 8 + i] = np.float32(w) / np.float32(cntv)
            if half == 1:
                t = S - 8 + i
                cntv = min(t + w // 2, S) - max(t - w // 2, 0)
                pe[:, 34 + gi * 8 + i] = np.float32(w) / np.float32(cntv)
    return pe


def make_in_maps(x, c, ctx, c_ctx, mod_w, mod_b, norm1_g, norm2_g, w_in, gate_b, q_norm_g, k_norm_g,
                 sink, pool_w, pool_scale, w_attn_proj, w_pool_proj, w_out, w_up, w_down):
    f = lambda a: np.ascontiguousarray(np.asarray(a, dtype=np.float32))
    x, c, ctx, c_ctx = f(x), f(c), f(ctx), f(c_ctx)
    hp = _head_perm()
    qcols = np.concatenate([np.arange(h * 64, (h + 1) * 64) for h in hp])
    w_in0 = f(w_in)[0]
    w_in_p = np.ascontiguousarray(np.concatenate([w_in0[:, qcols], w_in0[:, QW:]], axis=1))
    w_attn_p = np.ascontiguousarray(f(w_attn_proj)[0][qcols, :])
    mod_w0, w_pool0, w_out0, w_up0, w_down0 = f(mod_w)[0], f(w_pool_proj)[0], f(w_out)[0], f(w_up)[0], f(w_down)[0]
    modb_rep = np.ascontiguousarray(np.broadcast_to(f(mod_b)[0][None, :], (128, 6 * D)))
    n1g_rep = np.ascontiguousarray(np.broadcast_to(f(norm1_g)[0][None, :], (128, D)))
    n2g_rep = np.ascontiguousarray(np.broadcast_to(f(norm2_g)[0][None, :], (128, D)))
    gateb_col = np.ascontiguousarray(f(gate_b)[0].reshape(16, 128).T)
    qg, kg = f(q_norm_g)[0], f(k_norm_g)[0]
    dd = np.arange(128) % 64
    pp = np.array([_partner(d) for d in dd])
    g_cols = np.ascontiguousarray(np.stack([qg[dd], qg[pp], kg[dd], kg[pp]], axis=1))
    sink_row = np.ascontiguousarray(f(sink)[0][None, :])
    sink_rep = np.ascontiguousarray(np.broadcast_to(f(sink)[0][None, :], (128, NH)))
    sinkl = np.zeros((1, 256), np.float32)
    sinkl[0, 64:128] = 1.0
    sinkl[0, 128:192] = 1.0
    pscale_col = np.ascontiguousarray(f(pool_scale)[0].reshape(4, 128).T)
    pool_w0 = f(pool_w)[0]
    cm = _const_mats()
    c_ctx_col = c_ctx.reshape(8, 128).T
    maps = []
    for core in range(8):
        b, half = core // 2, core % 2
        xe = np.zeros((TEXT, D), np.float32)
        lo = half * TOWN - HALO
        a0, a1 = max(lo, 0), min(lo + TEXT, S)
        xe[a0 - lo:a1 - lo] = x[b, a0:a1]
        cT = np.ascontiguousarray(np.concatenate([c[b].reshape(8, 128).T, c_ctx_col], axis=1))
        maps.append({
            "xext": xe, "ctx": np.ascontiguousarray(ctx[b]), "cT": cT, "mod_w": mod_w0, "modb_rep": modb_rep,
            "n1g_rep": n1g_rep, "n2g_rep": n2g_rep, "w_in_p": w_in_p, "w_attn_p": w_attn_p, "w_pool": w_pool0,
            "w_out": w_out0, "w_up": w_up0, "w_down": w_down0, "gateb_col": gateb_col, "g_cols": g_cols,
            "sink_row": sink_row, "sinkl": sinkl, "sink_rep": sink_rep, "pool_w": pool_w0, "pscale_col": pscale_col,
            "rope_cs": _rope_tables(half), "masks": _masks(half), "pool_edge": _pool_edge(half), "cmat": cm,
        })
    return maps


_NC_CACHE = {}


def kernel(**inputs):
    maps = make_in_maps(**inputs)
    if "nc" not in _NC_CACHE:
        _NC_CACHE["nc"] = build_program()
    nc = _NC_CACHE["nc"]
    res = run_bass_kernel_spmd(nc, maps, core_ids=list(range(8)))
    out = np.empty((4, S, D), np.float32)
    for core in range(8):
        b, half = core // 2, core % 2
        out[b, half * TOWN:(half + 1) * TOWN] = res.results[core]["out"]
    return out
```

```python
import numpy as np
import concourse.bass as bass
import concourse.mybir as mybir
from concourse.bass_utils import run_bass_kernel_spmd

F32 = mybir.dt.float32
BF16 = mybir.dt.bfloat16
ALU = mybir.AluOpType
AF = mybir.ActivationFunctionType

D = 1024
S = 4096
CTX = 256
TOWN = 2048
HALO = 128
TEXT = TOWN + 2 * HALO
NBLK = TEXT // 128
NH = 16
QW = 1024
KVW = 256
DFF = 2816
EPS = 1e-6
SB_BASE = 16512
SB_END = 229312
FPASS = [(0, 8), (8, 7), (15, 7)]


class Trk:
    ENG = ("pe", "act", "dve", "pool", "sp")

    def __init__(self, nc):
        self.nc = nc
        self.streams = {e: [] for e in self.ENG}
        self.cnt = {e: 0 for e in self.ENG}
        self.seen = {e: {} for e in self.ENG}
        self.state = {}
        self.dma_cnt = {}
        self.sem_handles = {}

    def sem(self, name):
        if name not in self.sem_handles:
            self.sem_handles[name] = self.nc.alloc_semaphore(name="s_" + name)
        return self.sem_handles[name]

    def _st(self, k):
        st = self.state.get(k)
        if st is None:
            st = {"w": None, "r": {}}
            self.state[k] = st
        return st

    def op(self, eng, fn, reads=(), writes=(), dma=None, inc=True):
        deps = {}

        def add(s, v):
            if s not in deps or deps[s] < v:
                deps[s] = v

        for k in reads:
            st = self._st(k)
            if st["w"] is not None:
                add(*st["w"])
            if k[0] == "ps":
                for s, v in st["r"].items():
                    add(s, v)
        for k in writes:
            st = self._st(k)
            if st["w"] is not None:
                add(*st["w"])
            for s, v in st["r"].items():
                add(s, v)
        waits = []
        for s, v in deps.items():
            if eng == "pe" and s == "pe":
                continue
            if self.seen[eng].get(s, 0) >= v:
                continue
            self.seen[eng][s] = v
            waits.append((s, v))
        if dma is not None:
            self.dma_cnt[dma] = self.dma_cnt.get(dma, 0) + 16
            ev = (dma, self.dma_cnt[dma])
            incr = (dma, 16)
        elif inc:
            self.cnt[eng] += 1
            ev = (eng, self.cnt[eng])
            incr = (eng, 1)
        else:
            ev = (eng, self.cnt[eng] + 1)
            incr = None
        self.streams[eng].append((waits, fn, incr, dma is not None))
        for k in reads:
            st = self._st(k)
            if st["r"].get(ev[0], 0) < ev[1]:
                st["r"][ev[0]] = ev[1]
        for k in writes:
            self.state[k] = {"w": ev, "r": {}}
        return ev

    def barrier(self, skip=()):
        evs = [(e, self.cnt[e]) for e in self.ENG if self.cnt[e] > 0]
        evs += list(self.dma_cnt.items())
        for e in self.ENG:
            if e in skip:
                continue
            waits = []
            for s, v in evs:
                if e == "pe" and s == "pe":
                    continue
                if self.seen[e].get(s, 0) >= v:
                    continue
                self.seen[e][s] = v
                waits.append((s, v))
            if waits:
                self.streams[e].append((waits, None, None, False))

    def final_wait(self, eng, sems):
        waits = [(s, self.dma_cnt[s]) for s in sems]
        self.streams[eng].append((waits, None, None, False))

    def replay(self, eng, e):
        for waits, fn, incr, isdma in self.streams[eng]:
            if fn is None:
                for s, v in waits:
                    e.wait_ge(self.sem(s), v)
                continue
            attach = None
            if waits and not isdma:
                attach = waits[0]
                rest = waits[1:]
            else:
                rest = waits
            for s, v in rest:
                e.wait_ge(self.sem(s), v)
            ins = fn(e)
            if attach is not None:
                ins._wait_ge(self.sem(attach[0]), attach[1])
            if incr is not None:
                ins.then_inc(self.sem(incr[0]), incr[1])


def build_program(stop_after=None, dumps=()):
    nc = bass.Bass("TRN2", target_bir_lowering=False)
    T = Trk(nc)
    cnt = [0]

    def at(off, shape, dtype, name):
        cnt[0] += 1
        esz = 4 if dtype == F32 else 2
        size = esz * int(np.prod(shape[1:]))
        assert off % 32 == 0 and off >= SB_BASE and off + size <= SB_END, (name, off, size)
        return nc.alloc_sbuf_tensor_at("%s_%d" % (name, cnt[0]), list(shape), dtype, offset=off)

    def din(name, shape):
        return nc.dram_tensor(name, list(shape), F32, kind="ExternalInput").ap()

    xext = din("xext", [TEXT, D])
    ctxd = din("ctx", [CTX, D])
    cTd = din("cT", [128, 16])
    modw = din("mod_w", [D, 6 * D])
    modbr = din("modb_rep", [128, 6 * D])
    n1g = din("n1g_rep", [128, D])
    n2g = din("n2g_rep", [128, D])
    w_in = din("w_in_p", [D, 4096])
    w_attn = din("w_attn_p", [QW, D])
    w_pool = din("w_pool", [512, D])
    w_out = din("w_out", [D, D])
    w_up = din("w_up", [D, 2 * DFF])
    w_down = din("w_down", [DFF, D])
    gatebc = din("gateb_col", [128, 16])
    gcols = din("g_cols", [128, 4])
    sinkrow = din("sink_row", [1, NH])
    sinkld = din("sinkl", [1, 256])
    sinkrepd = din("sink_rep", [128, NH])
    poolwd = din("pool_w", [4, 128, 128])
    pscol = din("pscale_col", [128, 4])
    ropecs = din("rope_cs", [128, 2, TEXT])
    masksd = din("masks", [128, 4, 128])
    pedge = din("pool_edge", [128, 66])
    cmat = din("cmat", [128, 3, 128])
    outd = nc.dram_tensor("out", [TOWN, D], F32, kind="ExternalOutput").ap()
    dump_aps = {}
    for nm, shp in dumps:
        dump_aps[nm] = nc.dram_tensor("dbg_" + nm, list(shp), F32, kind="ExternalOutput").ap()

    OC = SB_BASE
    OH = OC + 4608
    OX = OH + 36864
    OM = OX + 65536
    OP = OM + 32768
    OW = OP + 16384
    OMOD = SB_END - 16384
    assert OMOD - OW >= 34000, OMOD - OW

    c = OC
    ident = at(c, [128, 128], BF16, "ident"); c += 256
    bones = at(c, [128, 128], BF16, "bones"); c += 256
    permm = at(c, [128, 128], BF16, "permm"); c += 256
    masks = at(c, [128, 4, 128], BF16, "masks"); c += 1024
    gateb = at(c, [128, 16], F32, "gateb"); c += 64
    gcol = at(c, [128, 4], F32, "gcol"); c += 32
    pscl = at(c, [128, 4], F32, "pscl"); c += 32
    pedg = at(c, [128, 66], F32, "pedg"); c += 288
    cTs = at(c, [128, 16], F32, "cTs"); c += 64
    cTsil = at(c, [128, 16], F32, "cTsil"); c += 64
    stat = at(c, [128, 128], F32, "stat"); c += 512
    sinkl = at(c, [1, 256], BF16, "sinkl"); c += 512
    esrow = at(c, [1, NH], BF16, "esrow"); c += 32
    poolw = at(c, [128, 4, 128], BF16, "poolw"); c += 1024
    esrep = at(c, [128, NH], F32, "esrep"); c += 64
    assert c <= OH

    hT = at(OH, [128, 8, TEXT], BF16, "hT")
    h2T = at(OH, [128, 8, TOWN], BF16, "h2T")
    qT = at(OX, [128, 8, TOWN], BF16, "qT")
    kT = at(OX + 32768, [128, 2, TEXT], BF16, "kT")
    Vaug = at(OX + 41984, [128, NBLK, 2, 2, 128], BF16, "Vaug")
    kcT = at(OX + 60416, [128, 2, CTX], BF16, "kcT")
    Vcaug = at(OX + 61440, [128, 2, 2, 2, 128], BF16, "Vcaug")
    xacc = at(OX, [128, 16, D], F32, "xacc")
    g1rep = at(OMOD, [128, D], F32, "g1rep")
    A2rep = at(OMOD + 4096, [128, D], F32, "A2rep")
    B2rep = at(OMOD + 8192, [128, D], F32, "B2rep")
    g2rep = at(OMOD + 12288, [128, D], F32, "g2rep")

    psall = nc.alloc_psum_tensor("psall", [128, 4096], F32)
    ps = [psall[:, i * 512:(i + 1) * 512] for i in range(8)]

    def PS(i):
        return ("ps", i)

    def dma_in(eng, out_ap, in_ap, key, semname):
        T.op(eng, lambda e: e.dma_start(out=out_ap, in_=in_ap), writes=[key], dma=semname)

    def wview(dram, r0, nk, c0, ncol):
        return dram[r0:r0 + 128 * nk, c0:c0 + ncol].rearrange("(j p) n -> p j n", p=128)

    out_sems = []

    def dump(nm, ap, key):
        s = "dbg_" + nm
        if len(ap.shape) >= 3:
            for i in range(ap.shape[1]):
                T.op("pool", lambda e, i=i: e.dma_start(out=dump_aps[nm][:, i], in_=ap[:, i]), dma=s)
        else:
            T.op("pool", lambda e: e.dma_start(out=dump_aps[nm], in_=ap), dma=s)
        out_sems.append(s)

    def hkeys(t0, n):
        return [("hT", b) for b in range(t0 // 128, (t0 + n - 1) // 128 + 1)]

    def finish():
        T.final_wait("sp", out_sems)
        with nc.Block() as block:
            @block.tensor
            def _(e):
                T.replay("pe", e)

            @block.scalar
            def _(e):
                T.replay("act", e)

            @block.vector
            def _(e):
                T.replay("dve", e)

            @block.gpsimd
            def _(e):
                T.replay("pool", e)

            @block.sync
            def _(e):
                T.replay("sp", e)
        return nc

    cm3 = at(OW, [128, 3, 128], BF16, "cm3")
    sinkf = at(OW + 1024, [1, NH], F32, "sinkf")
    clhs = at(OW + 9216, [128, 16, 128], BF16, "clhs")
    mwr = [at(OW + 13312 + i * 8192, [128, 8, 512], BF16, "mwr%d" % i) for i in range(2)] + \
          [at(OM + i * 8192, [128, 8, 512], BF16, "mwr%d" % (2 + i)) for i in range(2)]
    mbr = [at(OW + 29696 + i * 2048, [128, 512], F32, "mbr%d" % i) for i in range(2)]
    hcT = at(OP, [128, 8, CTX], BF16, "hcT")
    ropet = at(OM, [128, 2, TEXT], F32, "ropet")
    A1rep = at(OM + 18432, [128, D], F32, "A1rep")
    B1rep = at(OM + 22528, [128, D], F32, "B1rep")
    sqscr = at(OM + 26624, [128, D], BF16, "sqscr")
    htok = [at(OX + 28672 + i * 2048, [128, D], BF16, "htok%d" % i) for i in range(4)]
    xblk = [at(OX + i * 4096, [128, D], F32, "xblk%d" % i) for i in range(3)] + [at(OX + 36864, [128, D], F32, "xblk3")]
    modtmp = [at(OX + 12288 + i * 2048, [128, 512], F32, "modtmp%d" % i) for i in range(2)]
    tmpA = [at(OX + 40960 + i * 4096, [128, D], F32, "tmpA%d" % i) for i in range(4)]
    cA1 = at(OX + 20480, [128, D], F32, "cA1")
    cB1 = at(OX + 24576, [128, D], F32, "cB1")

    T.op("dve", lambda e: e.memset(stat[:], 0.0), writes=[("statz",)])
    dma_in("pool", cm3[:], cmat, ("cm3",), "const0")
    dma_in("pool", masks[:], masksd, ("masks",), "const1")
    dma_in("pool", sinkl[:], sinkld, ("sinkl",), "const2")
    dma_in("pool", poolw[:], poolwd.rearrange("g c d -> c g d"), ("poolw",), "const3")
    for (t_, d_, k_) in [(gateb, gatebc, "gateb"), (gcol, gcols, "gcol"), (pscl, pscol, "pscl"), (pedg, pedge, "pedg"),
                         (cTs, cTd, "cTs"), (sinkf, sinkrow, "sinkf")]:
        dma_in("sp", t_[:], d_, (k_,), "c_" + k_)
    for i, (t_, k_) in enumerate([(ident, "ident"), (bones, "bones"), (permm, "permm")]):
        T.op("dve", lambda e, t_=t_, i=i: e.tensor_copy(out=t_[:], in_=cm3[:, i, :]), reads=[("cm3",)], writes=[(k_,)])
    T.op("act", lambda e: e.activation(out=esrow[:], in_=sinkf[:], func=AF.Exp), reads=[("sinkf",)], writes=[("esrow",)])
    dma_in("sp", esrep[:], sinkrepd, ("esrep",), "c_esrep")
    T.op("act", lambda e: e.activation(out=esrep[:], in_=esrep[:], func=AF.Exp), reads=[("esrep",)], writes=[("esrep",)])
    T.op("act", lambda e: e.activation(out=cTsil[:], in_=cTs[:], func=AF.Silu), reads=[("cTs",)], writes=[("cTsil",)])
    T.op("dve", lambda e: e.tensor_copy(out=clhs[:], in_=cTsil[:].unsqueeze(2).to_broadcast([128, 16, 128])),
         reads=[("cTsil",)], writes=[("clhs",)])

    def mod_dst(ch):
        kind = ch // 2
        return [(B1rep, 0), (A1rep, 1), (g1rep, 0), (B2rep, 0), (A2rep, 2), (g2rep, 0)][kind], (ch % 2) * 512

    def mod_chunk(ch, with_ctx, mwr_, mbr_, clhs_, clk, bankbase=6, preloaded=False):
        slot = ch % len(mwr_)
        wk = ("mwr", slot)
        bslot = ch % 2
        bk = ("mbr", bslot)
        if not preloaded:
            dma_in("pool", mwr_[slot][:], wview(modw, 0, 8, ch * 512, 512), wk, "mwr%d" % slot)
        dma_in("sp", mbr_[bslot][:], modbr[:, ch * 512:(ch + 1) * 512], bk, "mbr%d" % bslot)
        (dst, mode), c0 = mod_dst(ch)
        variants = []
        if with_ctx:
            variants.append((8, cB1 if ch < 2 else cA1, 1, 0 if ch < 2 else 3))
        variants.append((0, dst, 0, mode))
        for (cofs, dd, bi, md) in variants:
            bank = bankbase + bi
            for j in range(8):
                T.op("pe", lambda e, j=j, cofs=cofs, bank=bank: e.matmul(
                    ps[bank][:], lhsT=clhs_[:, cofs + j, :], rhs=mwr_[slot][:, j, :], start=(j == 0), stop=(j == 7)),
                    reads=[clk, wk], writes=[PS(bank)], inc=(j == 7))
            dk = ("rep", dd.name, c0)
            if md == 0:
                T.op("dve", lambda e, bank=bank, dd=dd: e.tensor_tensor(
                    out=dd[:, c0:c0 + 512], in0=ps[bank][:], in1=mbr_[bslot][:], op=ALU.add),
                    reads=[PS(bank), bk], writes=[dk])
            else:
                gsrc = dd if md != 3 else A1rep
                gk = dk if md != 3 else ("rep", A1rep.name, c0)
                tmpk = ("modtmp", bi)
                mt = modtmp[bi]
                T.op("dve", lambda e, bank=bank, mt=mt: e.tensor_tensor(
                    out=mt[:], in0=ps[bank][:], in1=mbr_[bslot][:], op=ALU.add),
                    reads=[PS(bank), bk], writes=[tmpk])
                T.op("dve", lambda e, dd=dd, mt=mt, gsrc=gsrc: e.scalar_tensor_tensor(
                    out=dd[:, c0:c0 + 512], in0=mt[:], scalar=1.0, in1=gsrc[:, c0:c0 + 512],
                    op0=ALU.add, op1=ALU.mult), reads=[tmpk, gk], writes=[dk])

    def repkeys(t):
        return [("rep", t.name, 0), ("rep", t.name, 512)]

    for hh_ in range(2):
        dma_in("sp", A1rep[:, hh_ * 512:(hh_ + 1) * 512], n1g[:, hh_ * 512:(hh_ + 1) * 512], ("rep", A1rep.name, hh_ * 512), "c_n1g%d" % hh_)
        dma_in("sp", A2rep[:, hh_ * 512:(hh_ + 1) * 512], n2g[:, hh_ * 512:(hh_ + 1) * 512], ("rep", A2rep.name, hh_ * 512), "c_n2g%d" % hh_)
    for ch in range(4):
        dma_in("pool", mwr[ch][:], wview(modw, 0, 8, ch * 512, 512), ("mwr", ch), "mwr%d" % ch)
    for ch in range(4):
        mod_chunk(ch, True, mwr, mbr, clhs, ("clhs",), preloaded=True)
    if stop_after == "P0":
        T.barrier(skip=("pe",))
        dump("A1", A1rep[:], None)
        return finish()

    wring = [at(OW + i * 8192, [128, 8, 512], BF16, "wring%d" % i) for i in range(2)]
    wv = at(OW + 32768, [128, 8, 256], BF16, "wv")
    T.op("pool", lambda e: e.dma_start(out=wring[0][:, :, 0:256], in_=wview(w_in, 0, 8, QW, 256)),
         writes=[("wring", 0), ("cm3",), ("sinkf",)], dma="wring0")
    T.op("pool", lambda e: e.dma_start(out=wv[:], in_=wview(w_in, 0, 8, QW + KVW, 256)),
         writes=[("wv",), ("mbr", 1)], dma="wv")
    T.op("pool", lambda e: e.dma_start(out=wring[1][:], in_=wview(w_in, 0, 8, 0, 512)),
         writes=[("wring", 1), ("clhs",), ("mwr", 0)], dma="wring1")

    def norm_block(src_ap, xb, xk, load, si, Arep, Brep, dstT, dst_c0, dkey, ht, hk, tpbank, sq):
        if load is not None:
            dma_in("sp", xb, src_ap, xk, load)
        T.op("act", lambda e: e.activation(out=sq[:], in_=xb, func=AF.Square, accum_out=stat[:, si:si + 1]),
             reads=[xk, ("statz",)], writes=[("sqscr",), ("stat", si)])
        T.op("act", lambda e: e.activation(out=stat[:, 64 + si:65 + si], in_=stat[:, si:si + 1], func=AF.Sqrt,
                                           scale=1.0 / D, bias=EPS), reads=[("stat", si)], writes=[("stat2", si)])
        return ht, hk

    def norm_finish(xin_ap, xin_keys, tmp_ap, tmpk, si, Arep, Brep, dstT, dst_c0, dkey, ht, hk, tpbank):
        norm_mod(xin_ap, xin_keys, tmp_ap, tmpk, si, Arep, Brep, ht, hk)
        norm_tp(dstT, dst_c0, dkey, ht, hk, tpbank)

    def norm_mod(xin_ap, xin_keys, tmp_ap, tmpk, si, Arep, Brep, ht, hk, split=False):
        T.op("dve", lambda e: e.reciprocal(out=stat[:, 64 + si:65 + si], in_=stat[:, 64 + si:65 + si]),
             reads=[("stat2", si)], writes=[("stat2", si)])
        T.op("dve", lambda e: e.scalar_tensor_tensor(out=tmp_ap, in0=xin_ap, scalar=stat[:, 64 + si:65 + si],
                                                     in1=Arep[:], op0=ALU.mult, op1=ALU.mult),
             reads=xin_keys + [("stat2", si)] + repkeys(Arep), writes=[tmpk])
        if split:
            T.op("pool", lambda e: e.tensor_tensor(out=ht[:, 0:768], in0=tmp_ap[:, 0:768], in1=Brep[:, 0:768], op=ALU.add),
                 reads=[tmpk] + repkeys(Brep), writes=[(hk, 0)])
            T.op("dve", lambda e: e.tensor_tensor(out=ht[:, 768:1024], in0=tmp_ap[:, 768:1024], in1=Brep[:, 768:1024], op=ALU.add),
                 reads=[tmpk] + repkeys(Brep), writes=[(hk, 1)])
        else:
            T.op("pool", lambda e: e.tensor_tensor(out=ht[:], in0=tmp_ap, in1=Brep[:], op=ALU.add),
                 reads=[tmpk] + repkeys(Brep), writes=[(hk, 0), (hk, 1)])

    def norm_tp(dstT, dst_c0, dkey, ht, hk, tpbank):
        tpv = ps[tpbank][:].bitcast(BF16)
        for j in range(8):
            T.op("pe", lambda e, j=j: e.transpose(tpv[:, j * 128:(j + 1) * 128], ht[:, j * 128:(j + 1) * 128], ident[:]),
                 reads=[(hk, 0), (hk, 1), ("ident",)], writes=[PS(tpbank)], inc=(j == 7))
        T.op("act", lambda e: e.activation(out=dstT[:, :, dst_c0:dst_c0 + 128],
                                           in_=tpv.rearrange("p (j t) -> p j t", j=8), func=AF.Identity),
             reads=[PS(tpbank)], writes=[dkey])

    a0 = []
    for which in range(2 + NBLK):
        if which < 2:
            a0.append((ctxd[which * 128:(which + 1) * 128, :], cA1, cB1, hcT, which * 128, ("hcT", which)))
        else:
            b = which - 2
            a0.append((xext[b * 128:(b + 1) * 128, :], A1rep, B1rep, hT, b * 128, ("hT", b)))

    def a0_s1(bi):
        src, Ar, Br, dstT, c0, dk = a0[bi]
        xs = bi % 4
        norm_block(src, xblk[xs][:], ("xblk", xs), "xblk%d" % xs, bi, Ar, Br, dstT, c0, dk, None, None, None, sqscr)

    def a0_s2(bi):
        src, Ar, Br, dstT, c0, dk = a0[bi]
        xs = bi % 4
        norm_mod(xblk[xs][:], [("xblk", xs)], tmpA[bi % 4][:], ("tmpA", bi % 4), bi, Ar, Br, htok[bi % 4], ("htok", bi % 4), split=True)

    def a0_s3(bi):
        src, Ar, Br, dstT, c0, dk = a0[bi]
        norm_tp(dstT, c0, dk, htok[bi % 4], ("htok", bi % 4), 4 + bi % 2)

    for it in range(len(a0) + 3):
        if it == 8:
            T.op("sp", lambda e: e.dma_start(out=ropet[:], in_=ropecs), writes=[("ropet",), ("mwr", 2), ("mwr", 3)], dma="c_ropet")
        if it < len(a0):
            a0_s1(it)
        if 0 <= it - 1 < len(a0):
            a0_s2(it - 1)
        if 0 <= it - 3 < len(a0):
            a0_s3(it - 3)
    T.barrier(skip=("pe",))
    if "hT" in dump_aps:
        dump("hT", hT[:], None)
    if "hcT" in dump_aps:
        dump("hcT", hcT[:], None)
    if "A1" in dump_aps:
        dump("A1", A1rep[:], None)
    if stop_after == "A0":
        return finish()

    scr = []
    for i in range(2):
        o = OW + 16384 + i * 8192
        scr.append(dict(sq=at(o, [128, 512], BF16, "sq%d" % i), rawb=at(o + 1024, [128, 512], BF16, "rawb%d" % i),
                        sroot=at(o + 2048, [128, 512], F32, "sroot%d" % i), t1=at(o + 4096, [128, 512], F32, "t1%d" % i),
                        t2=at(o + 6144, [128, 512], F32, "t2%d" % i)))
    cosT = ropet[:, 0, :]
    snT = ropet[:, 1, :]
    pctr = [0]
    cctr = [0]

    def qk_tile(wt, wtk, wcol0, srcT, skeyf, chunks, gci, rope, dst_fn, dkey_fn):
        pend = []

        def tail(st):
            pb, n, t0, sc, sck, ab, dst, dkeys = st
            T.op("pe", lambda e: e.matmul(ps[ab][:, :n], lhsT=bones[:], rhs=sc["sq"][:, :n], start=True, stop=True),
                 reads=[("bones",), (sck, "sq")], writes=[PS(ab)])
            if rope:
                T.op("pe", lambda e: e.matmul(ps[ab + 1][:, :n], lhsT=permm[:], rhs=sc["rawb"][:, :n], start=True, stop=True),
                     reads=[("permm",), (sck, "rawb")], writes=[PS(ab + 1)])
            T.op("act", lambda e: e.activation(out=sc["sroot"][:, :n], in_=ps[ab][:, :n], func=AF.Ln, scale=1.0 / 64, bias=EPS),
                 reads=[PS(ab)], writes=[(sck, "sroot")])
            T.op("act", lambda e: e.activation(out=sc["sroot"][:, :n], in_=sc["sroot"][:, :n], func=AF.Exp, scale=-0.5),
                 reads=[(sck, "sroot")], writes=[(sck, "sroot")])
            if rope:
                T.op("dve", lambda e: e.scalar_tensor_tensor(out=sc["t1"][:, :n], in0=ps[pb][:, :n], scalar=gcol[:, gci:gci + 1],
                                                             in1=cosT[:, t0:t0 + n], op0=ALU.mult, op1=ALU.mult),
                     reads=[PS(pb), ("gcol",), ("ropet",)], writes=[(sck, "t1")])
                T.op("dve", lambda e: e.scalar_tensor_tensor(out=sc["t2"][:, :n], in0=ps[ab + 1][:, :n], scalar=gcol[:, gci + 1:gci + 2],
                                                             in1=snT[:, t0:t0 + n], op0=ALU.mult, op1=ALU.mult),
                     reads=[PS(ab + 1), ("gcol",), ("ropet",)], writes=[(sck, "t2")])
                T.op("dve", lambda e: e.tensor_tensor(out=sc["t1"][:, :n], in0=sc["t1"][:, :n], in1=sc["t2"][:, :n], op=ALU.add),
                     reads=[(sck, "t1"), (sck, "t2")], writes=[(sck, "t1")])
                T.op("pool", lambda e: e.tensor_tensor(out=dst, in0=sc["t1"][:, :n], in1=sc["sroot"][:, :n], op=ALU.mult),
                     reads=[(sck, "t1"), (sck, "sroot")], writes=dkeys)
            else:
                T.op("dve", lambda e: e.scalar_tensor_tensor(out=dst, in0=ps[pb][:, :n], scalar=gcol[:, gci:gci + 1],
                                                             in1=sc["sroot"][:, :n], op0=ALU.mult, op1=ALU.mult),
                     reads=[PS(pb), ("gcol",), (sck, "sroot")], writes=dkeys)

        for (t0, n) in chunks:
            pb = pctr[0] % 3
            pctr[0] += 1
            si = cctr[0] % 2
            cctr[0] += 1
            sc = scr[si]
            sck = "scr%d" % si
            ab = 3 + 2 * si
            for j in range(8):
                T.op("pe", lambda e, j=j, pb=pb, t0=t0, n=n: e.matmul(
                    ps[pb][:, :n], lhsT=wt[:, j, wcol0:wcol0 + 128], rhs=srcT[:, j, t0:t0 + n], start=(j == 0), stop=(j == 7)),
                    reads=[wtk] + skeyf(t0, n), writes=[PS(pb)], inc=(j == 7))
            T.op("act", lambda e, pb=pb, n=n, sc=sc: e.activation(out=sc["sq"][:, :n], in_=ps[pb][:, :n], func=AF.Square),
                 reads=[PS(pb)], writes=[(sck, "sq")])
            if rope:
                T.op("act", lambda e, pb=pb, n=n, sc=sc: e.activation(out=sc["rawb"][:, :n], in_=ps[pb][:, :n], func=AF.Identity),
                     reads=[PS(pb)], writes=[(sck, "rawb")])
            if pend:
                tail(pend.pop())
            pend.append((pb, n, t0, sc, sck, ab, dst_fn(t0, n), dkey_fn(t0, n)))
        tail(pend.pop())

    XCH = [(0, 512), (512, 512), (1024, 512), (1536, 512), (2048, 256)]
    QCH = [(128 + i * 512, 512) for i in range(4)]

    T.op("dve", lambda e: e.memset(Vaug[:, :, :, 0, 64:128], 1.0), writes=[("Vones",)])
    T.op("dve", lambda e: e.memset(Vaug[:, :, :, 1, 0:64], 1.0), writes=[("Vones",)])
    T.op("dve", lambda e: e.memset(Vcaug[:, :, :, 0, 64:128], 1.0), writes=[("Vones",)])
    T.op("dve", lambda e: e.memset(Vcaug[:, :, :, 1, 0:64], 1.0), writes=[("Vones",)])
    for jj in range(2):
        qk_tile(wring[0], ("wring", 0), jj * 128, hcT, lambda t0, n: [("hcT", 0), ("hcT", 1)], [(0, 256)], 2, False,
                lambda t0, n, jj=jj: kcT[:, jj, t0:t0 + n], lambda t0, n, jj=jj: [("kcT", jj)])
    for jj in range(2):
        qk_tile(wring[0], ("wring", 0), jj * 128, hT, hkeys, XCH, 2, True,
                lambda t0, n, jj=jj: kT[:, jj, t0:t0 + n],
                lambda t0, n, jj=jj: [("kT", jj, b) for b in range(t0 // 128, (t0 + n) // 128)])

    vctr = [0]

    def v_pair(srcT, skey, b0, dstV, dkeyname):
        vb = 6 + vctr[0] % 2
        vctr[0] += 1
        for hh in range(2):
            b = b0 + hh
            for j in range(8):
                T.op("pe", lambda e, j=j, b=b, hh=hh: e.matmul(
                    ps[vb][:, hh * 256:(hh + 1) * 256], lhsT=srcT[:, j, b * 128:(b + 1) * 128], rhs=wv[:, j, :],
                    start=(j == 0), stop=(j == 7)), reads=[("wv",), skey(b)], writes=[PS(vb)], inc=(j == 7))
        pv = ps[vb][:].rearrange("p (b j h d) -> p b j h d", b=2, j=2, h=2)
        T.op("act", lambda e: e.activation(out=dstV[:, b0:b0 + 2, :, 0, 0:64], in_=pv[:, :, :, 0, :], func=AF.Copy),
             reads=[PS(vb), ("Vones",)], writes=[(dkeyname, b0, 0)])
        T.op("act", lambda e: e.activation(out=dstV[:, b0:b0 + 2, :, 1, 64:128], in_=pv[:, :, :, 1, :], func=AF.Copy),
             reads=[PS(vb), ("Vones",)], writes=[(dkeyname, b0, 1)])

    def v_blocks(srcT, skey, nblocks, dstV, dkeyname):
        for b0 in range(0, nblocks, 2):
            v_pair(srcT, skey, b0, dstV, dkeyname)

    v_blocks(hcT, lambda b: ("hcT", b), 2, Vcaug, "Vc")
    v_blocks(hT, lambda b: ("hT", b), NBLK, Vaug, "V")
    dma_in("pool", wring[0][:], wview(w_in, 0, 8, 512, 512), ("wring", 0), "wring0")
    for grp in range(2):
        slot = 1 - grp
        for tt in range(4):
            tile_i = grp * 4 + tt
            qk_tile(wring[slot], ("wring", slot), tt * 128, hT, hkeys, QCH, 0, True,
                    lambda t0, n, tile_i=tile_i: qT[:, tile_i, t0 - 128:t0 - 128 + n],
                    lambda t0, n, tile_i=tile_i: [("qT", tile_i, b) for b in range((t0 - 128) // 128, (t0 - 128 + n) // 128)])
    T.barrier(skip=("pe",))
    for nm, t_ in (("qT", qT), ("kT", kT), ("kcT", kcT), ("Vaug", Vaug), ("Vcaug", Vcaug)):
        if nm in dump_aps:
            dump(nm, t_[:], None)
    if stop_after == "A1":
        return finish()

    wpl = at(OW + 27648, [128, 8, 512], BF16, "wpl")
    dma_in("pool", wpl[:], wview(w_in, 0, 8, QW + 2 * KVW, 512), ("wpl",), "wpl")
    NPP = 10
    pT = [at(OM + i * 2048, [128, 2, 512], BF16, "pT%d" % i) for i in range(NPP)]
    Otok = [at(OM + 20480 + i * 1024, [128, 4, 2, 64], BF16, "Otok%d" % i) for i in range(2)]
    rdt = [at(OM + 22528 + i * 32, [128, 4], F32, "rdt%d" % i) for i in range(4)]
    bctr = dict(s=0, p=0, u=0, d=0)

    def b_scores(n, jj):
        kbs = [("w", n), ("w", n + 1), ("w", n + 2), ("c", 0), ("c", 1)]
        slots = []
        qk = [[("qT", 4 * jj + r, n) for r in range(4)] for hf in range(2)]

        def s_pair(ki, kind, kb):
            sp = bctr["s"] % 2
            bctr["s"] += 1
            pslot = bctr["p"] % NPP
            bctr["p"] += 1
            slots.append(pslot)
            mi = None
            if ki == 0:
                mi = 0 if n == 0 else 1
            elif ki == 2:
                mi = 3 if n == 15 else 2
            for hf in range(2):
                rows = slice(hf * 64, hf * 64 + 64)
                bank = 2 * sp + hf
                if kind == "w":
                    lhs = kT[rows, jj, kb * 128:(kb + 1) * 128]
                    lk = ("kT", jj, kb)
                else:
                    lhs = kcT[rows, jj, kb * 128:(kb + 1) * 128]
                    lk = ("kcT", jj)
                qrhs = qT[rows, 4 * jj:4 * jj + 4, n * 128:(n + 1) * 128]
                T.op("pe", lambda e, bank=bank, lhs=lhs, qrhs=qrhs: e.matmul(ps[bank][:], lhsT=lhs, rhs=qrhs, start=True, stop=True),
                     reads=[lk] + qk[hf], writes=[PS(bank)], inc=(hf == 1))
            T.op("act", lambda e: e.activation(out=pT[pslot][:].rearrange("p a b -> p (a b)"), in_=psall[:, sp * 1024:(sp + 1) * 1024],
                                               func=AF.Exp, scale=0.125),
                 reads=[PS(2 * sp), PS(2 * sp + 1)], writes=[("pT", pslot)])
            if mi is not None:
                pv8 = pT[pslot][:].rearrange("p a (r q) -> p (a r) q", r=4)
                T.op("dve", lambda e: e.tensor_tensor(out=pv8, in0=pv8, in1=masks[:, mi:mi + 1, :].to_broadcast([128, 8, 128]), op=ALU.mult),
                     reads=[("pT", pslot), ("masks",)], writes=[("pT", pslot)])

        fns = [(lambda ki=ki, kind=kind, kb=kb: s_pair(ki, kind, kb)) for ki, (kind, kb) in enumerate(kbs)]
        return (n, jj, kbs, slots), fns

    def b_pv(st):
        n, jj, kbs, slots = st
        osl = bctr["d"] % 2
        bctr["d"] += 1
        chunks = []
        qkeys = [("qT", 4 * jj + r, n) for r in range(4)]
        ot = Otok[osl]
        otk = ("Otok", osl)
        ob0 = 4 + 2 * osl
        for hf in range(2):
            g = 2 * jj + hf
            ob = 4 + 2 * osl + hf
            rd = rdt[2 * osl + hf]
            rk = ("rdt", 2 * osl + hf)
            c0 = 0 if hf == 0 else 63
            dcol = 64 if hf == 0 else 0
            o0 = 0 if hf == 0 else 1

            def head(r, ob=ob, hf=hf, c0=c0):
                for ki, (kind, kb) in enumerate(kbs):
                    if kind == "w":
                        rhs = Vaug[:, kb, jj, hf, c0:c0 + 65]
                        lk = [("V", (kb // 2) * 2, hf), ("Vones",)]
                    else:
                        rhs = Vcaug[:, kb, jj, hf, c0:c0 + 65]
                        lk = [("Vc", 0, hf), ("Vones",)]
                    sl = slots[ki]
                    T.op("pe", lambda e, rhs=rhs, sl=sl, ki=ki: e.matmul(ps[ob][:, r * 65:(r + 1) * 65], lhsT=pT[sl][:, hf, r * 128:(r + 1) * 128],
                                                                       rhs=rhs, start=(ki == 0), stop=(ki == 4)),
                         reads=lk + [("pT", sl)], writes=[PS(ob)], inc=(ki == 4))

            def epi(ob=ob, hf=hf, g=g, rd=rd, rk=rk, dcol=dcol, o0=o0):
                ov = ps[ob][:, 0:260].rearrange("p (r c) -> p r c", r=4)
                T.op("dve", lambda e: e.tensor_tensor(out=rd[:], in0=ov[:, :, dcol], in1=esrep[:, 4 * g:4 * g + 4], op=ALU.add),
                     reads=[PS(ob), ("esrep",)], writes=[rk])
                T.op("dve", lambda e: e.reciprocal(out=rd[:], in_=rd[:]), reads=[rk], writes=[rk])
                T.op("dve", lambda e: e.tensor_tensor(out=ot[:, :, hf, :], in0=ov[:, :, o0:o0 + 64],
                                                      in1=rd[:].unsqueeze(2).to_broadcast([128, 4, 64]), op=ALU.mult),
                     reads=[PS(ob), rk], writes=[(otk, hf)])

            def c_a(head=head):
                head(0); head(1)

            def c_b(head=head, epi=epi, hf=hf):
                head(2); head(3); epi()

            chunks += [c_a, c_b]

        def tail():
            tpv = ps[ob0][:].bitcast(BF16)
            o2 = ot[:].rearrange("p r a d -> p (r a d)")
            for r in range(4):
                T.op("pe", lambda e, r=r: e.transpose(tpv[:, r * 128:(r + 1) * 128], o2[:, r * 128:(r + 1) * 128], ident[:]),
                     reads=[(otk, 0), (otk, 1), ("ident",)], writes=[PS(ob0)], inc=(r == 3))
            T.op("dve", lambda e: e.tensor_copy(out=qT[:, 4 * jj:4 * jj + 4, n * 128:(n + 1) * 128],
                                                in_=tpv[:, 0:512].rearrange("p (r q) -> p r q", r=4)),
                 reads=[PS(ob0)], writes=qkeys)

        return chunks + [tail]

    mwr2 = [at(OW + 10240 + i * 8192, [128, 8, 512], BF16, "mwr2%d" % i) for i in range(2)]
    mbr2 = [at(OW + i * 2048, [128, 512], F32, "mbr2%d" % i) for i in range(2)]
    clhs2 = at(OW + 4096, [128, 16, 128], BF16, "clhs2")
    modtmp.clear()
    modtmp.append(at(OW + 8192, [128, 512], F32, "modtmpb"))
    T.op("dve", lambda e: e.tensor_copy(out=clhs2[:], in_=cTsil[:].unsqueeze(2).to_broadcast([128, 16, 128])),
         reads=[("cTsil",)], writes=[("clhs2",)])
    units = [(n, jj) for n in range(16) for jj in range(2)]
    noop = lambda: None
    prev_st = None
    prev_tail = noop
    for ui in range(len(units) + 1):
        if ui < len(units):
            st, sfn = b_scores(*units[ui])
        else:
            st, sfn = None, [noop] * 5
        pvc = b_pv(prev_st) if prev_st is not None else [noop] * 5
        sfn[0](); pvc[0](); sfn[1](); prev_tail(); pvc[1](); sfn[2](); pvc[2](); sfn[3](); pvc[3](); sfn[4]()
        prev_tail = pvc[4]
        prev_st = st
        if ui % 4 == 1 and ui // 4 < 8:
            mod_chunk(4 + ui // 4, False, mwr2, mbr2, clhs2, ("clhs2",), bankbase=7)
    prev_tail()
    T.barrier(skip=("pe",))
    if "attnT" in dump_aps:
        dump("attnT", qT[:], None)
    if stop_after == "B":
        return finish()

    pmT = at(OP, [128, 4, TOWN], BF16, "pmT")
    c1w = []
    for s in range(2):
        o = OX + 32768 + s * 7168
        c1w.append(dict(ga=at(o, [128, 8, 128], BF16, "wga%d" % s), gp=at(o + 2048, [128, 8, 128], BF16, "wgp%d" % s),
                        wa=at(o + 4096, [128, 8, 128], BF16, "wa%d" % s), wp=at(o + 6144, [128, 4, 128], BF16, "wp%d" % s)))

    def c1_load(j):
        s_ = j % 2
        cw = c1w[s_]
        dma_in("pool", cw["ga"][:], wview(w_in, 0, 8, 2048 + j * 128, 128), ("c1w", s_, 0), "c1w%d_0" % s_)
        dma_in("pool", cw["gp"][:], wview(w_in, 0, 8, 3072 + j * 128, 128), ("c1w", s_, 1), "c1w%d_1" % s_)
        dma_in("pool", cw["wa"][:], wview(w_attn, 0, 8, j * 128, 128), ("c1w", s_, 2), "c1w%d_2" % s_)
        dma_in("pool", cw["wp"][:], wview(w_pool, 0, 4, j * 128, 128), ("c1w", s_, 3), "c1w%d_3" % s_)

    c1_load(0)
    c1_load(1)
    ubuf = at(OW, [128, TEXT], F32, "ubuf")
    sa = at(OW + 9216, [128, TEXT], F32, "sa")
    sbb = at(OW + 18432, [128, TEXT], F32, "sbb")
    diff = [at(OM + i * 4096, [128, TOWN], BF16, "diff%d" % i) for i in range(2)]
    ubufs = [ubuf, at(OM + 8192, [128, TEXT], F32, "ubuf2")]

    def ap_proj(gi, ci, t0, n):
        ubuf = ubufs[gi % 2]
        pb = (gi * 5 + ci) % 4
        for j in range(8):
            T.op("pe", lambda e, j=j: e.matmul(ps[pb][:, :n], lhsT=wpl[:, j, gi * 128:(gi + 1) * 128], rhs=hT[:, j, t0:t0 + n],
                                               start=(j == 0), stop=(j == 7)),
                 reads=[("wpl",)] + hkeys(t0, n), writes=[PS(pb)], inc=(j == 7))
        T.op("act", lambda e: e.activation(out=ubuf[:, t0:t0 + n], in_=ps[pb][:, :n], func=AF.Copy),
             reads=[PS(pb)], writes=[("u", gi % 2, ci)])

    def ap_mm(gi, cc, df, dfk):
        pb = 4 + cc % 2
        T.op("pe", lambda e: e.matmul(ps[pb][:], lhsT=poolw[:, gi, :], rhs=df[:, cc * 512:(cc + 1) * 512], start=True, stop=True),
             reads=[("poolw",), dfk], writes=[PS(pb)])
        T.op("act", lambda e: e.activation(out=pmT[:, gi, cc * 512:(cc + 1) * 512], in_=ps[pb][:], func=AF.Copy, scale=pscl[:, gi:gi + 1]),
             reads=[PS(pb), ("pscl",)], writes=[("pmT", gi, cc)])

    def ap_projs(gi):
        for ci, (t0, n) in enumerate(XCH):
            ap_proj(gi, ci, t0, n)

    def ap_group(gi):
        w = 2 << gi
        ubuf = ubufs[gi % 2]
        uk = [("u", gi % 2, ci) for ci in range(5)]
        T.op("dve", lambda e: e.tensor_scalar(out=ubuf[:, 120:128], in0=ubuf[:, 120:128], scalar1=pedg[:, 0:1], scalar2=None, op0=ALU.mult),
             reads=[("u", gi % 2, 0), ("pedg",)], writes=[("u", gi % 2, 0)])
        T.op("dve", lambda e: e.tensor_scalar(out=ubuf[:, 2176:2184], in0=ubuf[:, 2176:2184], scalar1=pedg[:, 1:2], scalar2=None, op0=ALU.mult),
             reads=[("u", gi % 2, 4), ("pedg",)], writes=[("u", gi % 2, 4)])
        T.op("dve", lambda e: e.tensor_tensor(out=sa[:, 1:TEXT], in0=ubuf[:, 1:TEXT], in1=ubuf[:, 0:TEXT - 1], op=ALU.add),
             reads=uk, writes=[("sa",)])
        W_, wk_ = sa, ("sa",)
        if gi >= 1:
            T.op("dve", lambda e: e.tensor_tensor(out=sbb[:, 3:TEXT], in0=sa[:, 3:TEXT], in1=sa[:, 1:TEXT - 2], op=ALU.add),
                 reads=[("sa",)], writes=[("sbb",)])
            W_, wk_ = sbb, ("sbb",)
        if gi >= 2:
            T.op("dve", lambda e: e.tensor_tensor(out=sa[:, 7:TEXT], in0=sbb[:, 7:TEXT], in1=sbb[:, 3:TEXT - 4], op=ALU.add),
                 reads=[("sbb",)], writes=[("sa",)])
            W_, wk_ = sa, ("sa",)
        if gi >= 3:
            T.op("dve", lambda e: e.tensor_tensor(out=sbb[:, 15:TEXT], in0=sa[:, 15:TEXT], in1=sa[:, 7:TEXT - 8], op=ALU.add),
                 reads=[("sa",)], writes=[("sbb",)])
            W_, wk_ = sbb, ("sbb",)
        woff = 128 + (w // 2 - 1)
        T.op("dve", lambda e: e.tensor_tensor(out=W_[:, woff:woff + 8], in0=W_[:, woff:woff + 8],
                                              in1=pedg[:, 2 + gi * 8:10 + gi * 8], op=ALU.mult),
             reads=[wk_, ("pedg",)], writes=[wk_])
        T.op("dve", lambda e: e.tensor_tensor(out=W_[:, woff + 2040:woff + 2048], in0=W_[:, woff + 2040:woff + 2048],
                                              in1=pedg[:, 34 + gi * 8:42 + gi * 8], op=ALU.mult),
             reads=[wk_, ("pedg",)], writes=[wk_])
        df = diff[gi % 2]
        dfk = ("diff", gi % 2)
        T.op("dve", lambda e: e.scalar_tensor_tensor(out=df[:], in0=W_[:, woff:woff + TOWN], scalar=1.0 / w, in1=ubuf[:, 128:128 + TOWN],
                                                     op0=ALU.mult, op1=ALU.subtract),
             reads=[wk_] + uk, writes=[dfk])
        for cc in range(4):
            ap_mm(gi, cc, df, dfk)

    ap_projs(0)
    for gi in range(4):
        if gi + 1 < 4:
            ap_projs(gi + 1)
        ap_group(gi)
    T.barrier(skip=("pe",))
    if "pmT" in dump_aps:
        dump("pmT", pmT[:], None)
    if stop_after == "AP":
        return finish()

    mergedT = at(OM, [128, 8, TOWN], BF16, "mergedT")
    sig = [dict(ga=at(OW + 16384 + s * 4096, [128, 512], F32, "sga%d" % s), gp=at(OW + 18432 + s * 4096, [128, 512], F32, "sgp%d" % s))
           for s in range(2)]
    tt_ = [dict(t1=at(OW + 24576 + s * 4096, [128, 512], F32, "ct1%d" % s), t2=at(OW + 26624 + s * 4096, [128, 512], F32, "ct2%d" % s))
           for s in range(2)]
    wo = at(OW, [128, 8, D], BF16, "wo")
    dma_in("pool", wo[:], wview(w_out, 0, 8, 0, D), ("wo",), "wo")
    T.op("dve", lambda e: e.tensor_tensor(out=wo[:], in0=wo[:], in1=g1rep[:].unsqueeze(1).to_broadcast([128, 8, D]), op=ALU.mult),
         reads=[("wo",)] + repkeys(g1rep), writes=[("wo",)])
    def c1_unit(j, cc, un, s, cw):
        bs = 4 * (un % 2)
        sg = sig[un % 2]
        tq = tt_[un % 2]
        sk = "c1s%d" % (un % 2)
        t0 = 128 + cc * 512

        def grp(bo, wt, wi, srcf, skeys, nk):
            for k in range(nk):
                T.op("pe", lambda e, k=k: e.matmul(ps[bs + bo][:], lhsT=wt[:, k, :], rhs=srcf(k), start=(k == 0), stop=(k == nk - 1)),
                     reads=[("c1w", s, wi)] + skeys, writes=[PS(bs + bo)], inc=(k == nk - 1))

        grp(0, cw["ga"], 0, lambda k: hT[:, k, t0:t0 + 512], hkeys(t0, 512), 8)
        grp(1, cw["gp"], 1, lambda k: hT[:, k, t0:t0 + 512], hkeys(t0, 512), 8)
        grp(2, cw["wa"], 2, lambda k: qT[:, k, cc * 512:(cc + 1) * 512], [("qT", k, cc * 4 + b) for k in range(8) for b in range(4)], 8)
        grp(3, cw["wp"], 3, lambda k: pmT[:, k, cc * 512:(cc + 1) * 512], [("pmT", k, cc) for k in range(4)], 4)
        T.op("act", lambda e: e.activation(out=sg["ga"][:], in_=ps[bs][:], func=AF.Sigmoid, bias=gateb[:, j:j + 1]),
             reads=[PS(bs), ("gateb",)], writes=[(sk, "ga")])
        T.op("act", lambda e: e.activation(out=sg["gp"][:], in_=ps[bs + 1][:], func=AF.Sigmoid, bias=gateb[:, 8 + j:9 + j]),
             reads=[PS(bs + 1), ("gateb",)], writes=[(sk, "gp")])
        T.op("dve", lambda e: e.tensor_tensor(out=tq["t1"][:], in0=ps[bs + 2][:], in1=sg["ga"][:], op=ALU.mult),
             reads=[PS(bs + 2), (sk, "ga")], writes=[(sk, "t1")])
        T.op("dve", lambda e: e.tensor_tensor(out=tq["t2"][:], in0=ps[bs + 3][:], in1=sg["gp"][:], op=ALU.mult),
             reads=[PS(bs + 3), (sk, "gp")], writes=[(sk, "t2")])
        T.op("dve", lambda e: e.tensor_tensor(out=mergedT[:, j, cc * 512:(cc + 1) * 512], in0=tq["t1"][:], in1=tq["t2"][:], op=ALU.add),
             reads=[(sk, "t1"), (sk, "t2")], writes=[("mT", j, cc)])

    def c1_tile(j):
        s = j % 2
        cw = c1w[s]
        if j >= 2:
            c1_load(j)
        for cc in range(4):
            c1_unit(j, cc, j * 4 + cc, s, cw)

    for j in range(8):
        c1_tile(j)
    T.barrier(skip=("pe",))
    if "mergedT" in dump_aps:
        dump("mergedT", mergedT[:], None)
    if stop_after == "C1":
        return finish()

    tmp = [at(OW + 16384 + i * 4096, [128, D], F32, "tmp%d" % i) for i in range(3)]
    sq2 = at(OW + 28672, [128, D], BF16, "sq2")
    htok2 = [at(OW + 30720 + i * 2048, [128, D], BF16, "htk2%d" % i) for i in range(3)]
    wd = [at(OP, [128, 8, D], BF16, "wd0"), at(OW + 20480, [128, 8, D], BF16, "wd1")]
    wu = [at(OH + 32768, [128, 2, 8, 128], BF16, "wu0"), at(OW, [128, 2, 8, 128], BF16, "wu1"), at(OW + 4096, [128, 2, 8, 128], BF16, "wu2")]
    dma_in("pool", wd[0][:, 0:FPASS[0][1], :], w_down[0:FPASS[0][1] * 128, :].rearrange("(f p) n -> p f n", p=128), ("wd", 0), "wd0")

    def fold_g2(ws, nf):
        T.op("dve", lambda e: e.tensor_tensor(out=wd[ws][:, 0:nf, :], in0=wd[ws][:, 0:nf, :],
                                              in1=g2rep[:].unsqueeze(1).to_broadcast([128, nf, D]), op=ALU.mult),
             reads=[("wd", ws)] + repkeys(g2rep), writes=[("wd", ws)])

    dma_in("pool", wu[0][:, 0, :, :], wview(w_up, 0, 8, 0, 128), ("wu", 0, 0), "wu0_0")
    dma_in("pool", wu[0][:, 1, :, :], wview(w_up, 0, 8, DFF, 128), ("wu", 0, 1), "wu0_1")
    for blk in range(16):
        xk = ("xacc", blk)
        dma_in("sp", xacc[:, blk, :], xext[128 + blk * 128:256 + blk * 128, :], xk, "xin%d" % blk)

    def c2_s1(blk):
        xk = ("xacc", blk)
        tm = tmp[blk % 3]
        tk = ("tmp", blk % 3)

        def half(hh):
            pb = (blk * 2 + hh) % 4
            for k in range(8):
                T.op("pe", lambda e, k=k: e.matmul(ps[pb][:], lhsT=mergedT[:, k, blk * 128:(blk + 1) * 128], rhs=wo[:, k, hh * 512:(hh + 1) * 512],
                                                   start=(k == 0), stop=(k == 7)),
                     reads=[("wo",)] + [("mT", k2, blk // 4) for k2 in range(8)], writes=[PS(pb)], inc=(k == 7))
            T.op("dve", lambda e: e.tensor_tensor(out=xacc[:, blk, hh * 512:(hh + 1) * 512], in0=xacc[:, blk, hh * 512:(hh + 1) * 512],
                                                  in1=ps[pb][:], op=ALU.add),
                 reads=[PS(pb), xk], writes=[xk])

        half(0)
        half(1)
        norm_block(None, xacc[:, blk, :], xk, None, 20 + blk, A2rep, B2rep, None, None, None, None, None, None, sq2)

    def c2_s2(blk):
        norm_mod(xacc[:, blk, :], [("xacc", blk)], tmp[blk % 3][:], ("tmp", blk % 3), 20 + blk, A2rep, B2rep, htok2[blk % 3], ("htk2", blk % 3))

    def c2_s3(blk):
        norm_tp(h2T, blk * 128, ("h2T", blk), htok2[blk % 3], ("htk2", blk % 3), 6 + blk % 2)

    for it in range(16 + 3):
        if 0 <= it - 2 < 16:
            c2_s2(it - 2)
        if it < 16:
            c2_s1(it)
        if 0 <= it - 3 < 16:
            c2_s3(it - 3)
    T.barrier(skip=("pe",))
    if "xmid" in dump_aps:
        dump("xmid", xacc[:], None)
    if "h2T" in dump_aps:
        dump("h2T", h2T[:], None)
    if stop_after == "C2":
        return finish()

    actT = at(OM, [128, 8, TOWN], BF16, "actT")
    sil = [at(OW + 8192 + i * 2048, [128, 512], F32, "sil%d" % i) for i in range(2)]
    tmpd = [at(OW + 12288 + i * 4096, [128, D], F32, "tmpd%d" % i) for i in range(2)]
    assert OW + 20480 + 16384 <= OMOD
    dctr = dict(f=0, u=0)

    def d1_unit(fi, cc, us):
        bs = 2 * (dctr["u"] % 4)
        sl = sil[dctr["u"] % 2]
        slk = ("sil", dctr["u"] % 2)
        dctr["u"] += 1
        for ab in range(2):
            for k in range(8):
                T.op("pe", lambda e, k=k, ab=ab: e.matmul(ps[bs + ab][:], lhsT=wu[us][:, ab, k, :], rhs=h2T[:, k, cc * 512:(cc + 1) * 512],
                                                          start=(k == 0), stop=(k == 7)),
                     reads=[("wu", us, ab)] + [("h2T", cc * 4 + b2) for b2 in range(4)], writes=[PS(bs + ab)], inc=(k == 7))
        T.op("act", lambda e: e.activation(out=sl[:], in_=ps[bs][:], func=AF.Silu), reads=[PS(bs)], writes=[slk])
        T.op("dve", lambda e: e.tensor_tensor(out=actT[:, fi, cc * 512:(cc + 1) * 512], in0=ps[bs + 1][:], in1=sl[:], op=ALU.mult),
             reads=[PS(bs + 1), slk], writes=[("actT", fi, cc)])

    def d1_tile(f, fi):
        us = dctr["f"] % 3
        dctr["f"] += 1
        if f > 0:
            dma_in("pool", wu[us][:, 0, :, :], wview(w_up, 0, 8, f * 128, 128), ("wu", us, 0), "wu%d_0" % us)
            dma_in("pool", wu[us][:, 1, :, :], wview(w_up, 0, 8, DFF + f * 128, 128), ("wu", us, 1), "wu%d_1" % us)
        for cc in range(4):
            d1_unit(fi, cc, us)

    def d2_block(blk, ws, nf, last):
        xk = ("xacc", blk)
        tm = tmpd[blk % 2]
        tk = ("tmpd", blk % 2)

        def half(hh):
            pb = (blk * 2 + hh) % 8
            for fi in range(nf):
                T.op("pe", lambda e, fi=fi: e.matmul(ps[pb][:], lhsT=actT[:, fi, blk * 128:(blk + 1) * 128], rhs=wd[ws][:, fi, hh * 512:(hh + 1) * 512],
                                                     start=(fi == 0), stop=(fi == nf - 1)),
                     reads=[("wd", ws), ("actT", fi, blk // 4)], writes=[PS(pb)], inc=(fi == nf - 1))
            T.op("dve", lambda e: e.tensor_tensor(out=xacc[:, blk, hh * 512:(hh + 1) * 512], in0=xacc[:, blk, hh * 512:(hh + 1) * 512],
                                                  in1=ps[pb][:], op=ALU.add),
                 reads=[PS(pb), xk], writes=[xk])

        half(0)
        half(1)
        if last:
            T.op("sp", lambda e: e.dma_start(out=outd[blk * 128:(blk + 1) * 128, :], in_=xacc[:, blk, :]), reads=[xk], dma="out%d" % blk)
            out_sems.append("out%d" % blk)

    def d_pass(p, f0, nf):
        ws = p % 2
        if p > 0:
            dma_in("pool", wd[ws][:, 0:nf, :], w_down[f0 * 128:(f0 + nf) * 128, :].rearrange("(f p) n -> p f n", p=128), ("wd", ws), "wd%d" % ws)
        fold_g2(ws, nf)
        for fi in range(nf):
            d1_tile(f0 + fi, fi)
        for blk in range(16):
            d2_block(blk, ws, nf, p == len(FPASS) - 1)

    for p, (f0, nf) in enumerate(FPASS):
        d_pass(p, f0, nf)
    return finish()


def _head_perm():
    order = []
    for i in range(8):
        for hf in range(2):
            order.append(4 * (2 * (i // 4) + hf) + (i % 4))
    return order


def _const_mats():
    cm = np.zeros((128, 3, 128), np.float32)
    cm[:, 0, :] = np.eye(128, dtype=np.float32)
    for k in range(128):
        for m in range(128):
            if k // 64 == m // 64:
                cm[k, 1, m] = 1.0
    for m in range(128):
        d = m % 64
        a, r = d // 32, d % 32
        pd = a * 32 + (r + 16) % 32
        cm[(m - d) + pd, 2, m] = 1.0
    return cm


def _partner(d):
    a, r = d // 32, d % 32
    return a * 32 + (r + 16) % 32


def _rope_tables(half):
    tl = np.arange(TEXT)
    t = (half * TOWN - HALO + tl) % S
    row = (t // 64).astype(np.float32)
    col = (t % 64).astype(np.float32)
    inv = (1.0 / (np.float32(10000.0) ** (np.arange(0, 32, 2, dtype=np.float32) / np.float32(32)))).astype(np.float32)
    tab = np.zeros((128, 2, TEXT), np.float32)
    for d in range(64):
        a, r = d // 32, d % 32
        j = r % 16
        pos = row if a == 0 else col
        ang = (pos * inv[j]).astype(np.float32)
        tab[d, 0] = np.cos(ang)
        tab[d, 1] = (-np.sin(ang)) if r < 16 else np.sin(ang)
    tab[64:] = tab[:64]
    return tab


def _masks(half):
    jp = np.arange(128)[:, None]
    i = np.arange(128)[None, :]
    prev = (jp >= i).astype(np.float32)
    nxt = (jp <= i).astype(np.float32)
    m = np.zeros((128, 4, 128), np.float32)
    m[:, 0] = prev if half == 1 else 0.0
    m[:, 1] = prev
    m[:, 2] = nxt
    m[:, 3] = nxt if half == 0 else 0.0
    return m


def _pool_edge(half):
    pe = np.ones((128, 66), np.float32)
    pe[:, 0] = 0.0 if half == 0 else 1.0
    pe[:, 1] = 0.0 if half == 1 else 1.0
    for gi, w in enumerate((2, 4, 8, 16)):
        for i in range(8):
            if half == 0:
                t = i
                cntv = min(t + w // 2, S) - max(t - w // 2, 0)
                pe[:, 2 + gi * 8 + i] = np.float32(w) / np.float32(cntv)
            if half == 1:
                t = S - 8 + i
                cntv = min(t + w // 2, S) - max(t - w // 2, 0)
                pe[:, 34 + gi * 8 + i] = np.float32(w) / np.float32(cntv)
    return pe


def make_in_maps(x, c, ctx, c_ctx, mod_w, mod_b, norm1_g, norm2_g, w_in, gate_b, q_norm_g, k_norm_g,
                 sink, pool_w, pool_scale, w_attn_proj, w_pool_proj, w_out, w_up, w_down):
    f = lambda a: np.ascontiguousarray(np.asarray(a, dtype=np.float32))
    x, c, ctx, c_ctx = f(x), f(c), f(ctx), f(c_ctx)
    hp = _head_perm()
    qcols = np.concatenate([np.arange(h * 64, (h + 1) * 64) for h in hp])
    w_in0 = f(w_in)[0]
    w_in_p = np.ascontiguousarray(np.concatenate([w_in0[:, qcols], w_in0[:, QW:]], axis=1))
    w_attn_p = np.ascontiguousarray(f(w_attn_proj)[0][qcols, :])
    mod_w0, w_pool0, w_out0, w_up0, w_down0 = f(mod_w)[0], f(w_pool_proj)[0], f(w_out)[0], f(w_up)[0], f(w_down)[0]
    modb_rep = np.ascontiguousarray(np.broadcast_to(f(mod_b)[0][None, :], (128, 6 * D)))
    n1g_rep = np.ascontiguousarray(np.broadcast_to(f(norm1_g)[0][None, :], (128, D)))
    n2g_rep = np.ascontiguousarray(np.broadcast_to(f(norm2_g)[0][None, :], (128, D)))
    gateb_col = np.ascontiguousarray(f(gate_b)[0].reshape(16, 128).T)
    qg, kg = f(q_norm_g)[0], f(k_norm_g)[0]
    dd = np.arange(128) % 64
    pp = np.array([_partner(d) for d in dd])
    g_cols = np.ascontiguousarray(np.stack([qg[dd], qg[pp], kg[dd], kg[pp]], axis=1))
    sink_row = np.ascontiguousarray(f(sink)[0][None, :])
    sink_rep = np.ascontiguousarray(np.broadcast_to(f(sink)[0][None, :], (128, NH)))
    sinkl = np.zeros((1, 256), np.float32)
    sinkl[0, 64:128] = 1.0
    sinkl[0, 128:192] = 1.0
    pscale_col = np.ascontiguousarray(f(pool_scale)[0].reshape(4, 128).T)
    pool_w0 = f(pool_w)[0]
    cm = _const_mats()
    c_ctx_col = c_ctx.reshape(8, 128).T
    maps = []
    for core in range(8):
        b, half = core // 2, core % 2
        xe = np.zeros((TEXT, D), np.float32)
        lo = half * TOWN - HALO
        a0, a1 = max(lo, 0), min(lo + TEXT, S)
        xe[a0 - lo:a1 - lo] = x[b, a0:a1]
        cT = np.ascontiguousarray(np.concatenate([c[b].reshape(8, 128).T, c_ctx_col], axis=1))
        maps.append({
            "xext": xe, "ctx": np.ascontiguousarray(ctx[b]), "cT": cT, "mod_w": mod_w0, "modb_rep": modb_rep,
            "n1g_rep": n1g_rep, "n2g_rep": n2g_rep, "w_in_p": w_in_p, "w_attn_p": w_attn_p, "w_pool": w_pool0,
            "w_out": w_out0, "w_up": w_up0, "w_down": w_down0, "gateb_col": gateb_col, "g_cols": g_cols,
            "sink_row": sink_row, "sinkl": sinkl, "sink_rep": sink_rep, "pool_w": pool_w0, "pscale_col": pscale_col,
            "rope_cs": _rope_tables(half), "masks": _masks(half), "pool_edge": _pool_edge(half), "cmat": cm,
        })
    return maps


_NC_CACHE = {}


def kernel(**inputs):
    maps = make_in_maps(**inputs)
    if "nc" not in _NC_CACHE:
        _NC_CACHE["nc"] = build_program()
    nc = _NC_CACHE["nc"]
    res = run_bass_kernel_spmd(nc, maps, core_ids=list(range(8)))
    out = np.empty((4, S, D), np.float32)
    for core in range(8):
        b, half = core // 2, core % 2
        out[b, half * TOWN:(half + 1) * TOWN] = res.results[core]["out"]
    return out
```
